# Optimizing a Trainium2 kernel written in Bass

```python
import jax, jax.numpy as jnp
from jax import lax
import numpy as np

D_MODEL = 1024
BATCH = 4
SEQ = 4096
DEPTH = 2

GRID_W = 64
CTX_LEN = 256
Q_BLOCK = 128
ROPE_BASE = 10000.0
EPS = 1e-6

MLA_HEADS = 8
MLA_NOPE = 64
MLA_ROPE = 32
MLA_QK = MLA_NOPE + MLA_ROPE
MLA_V = 64
MLA_Q_RANK = 384
MLA_KV_RANK = 256
GQA_HEADS = 8
GQA_KV_HEADS = 2
GQA_GROUP = GQA_HEADS // GQA_KV_HEADS
GQA_DIM = 64
FOUR_GROUPS = 4
FOUR_GROUP_DIM = 128
FOUR_WIDTH = FOUR_GROUPS * FOUR_GROUP_DIM
D_FF = 2816
CONV_W = 3
N_BRANCH = 3

KV_COLS = MLA_KV_RANK + MLA_ROPE + 2 * GQA_KV_HEADS * GQA_DIM
Q_COLS = MLA_Q_RANK + GQA_HEADS * GQA_DIM
FOUR_START = KV_COLS + Q_COLS
GATE_START = FOUR_START + FOUR_WIDTH
IN_COLS = GATE_START + N_BRANCH * D_MODEL

kernel_name = "hybrid_mla_gqa_fnet_convffn_prefix_block"


def rms_norm(x, g):
    xf = x.astype(jnp.float32)
    y = xf * lax.rsqrt(jnp.mean(xf * xf, axis=-1, keepdims=True) + EPS)
    return (y * g.astype(jnp.float32)).astype(x.dtype)


def modulate(x, g, shift, scale):
    return rms_norm(x, g) * (1 + scale) + shift


def axial_rope_tables(rows, rot_dim, dtype):
    row = jnp.repeat(jnp.arange(rows, dtype=jnp.float32), GRID_W)
    col = jnp.tile(jnp.arange(GRID_W, dtype=jnp.float32), rows)
    n_f = rot_dim // 4
    inv = ROPE_BASE ** (-jnp.arange(n_f, dtype=jnp.float32) / n_f)
    ang = jnp.concatenate([row[:, None] * inv, col[:, None] * inv], axis=-1)
    return (jnp.cos(ang)[:, None, :].astype(dtype), jnp.sin(ang)[:, None, :].astype(dtype))


def apply_rope(x, cs):
    cos, sin = cs
    x1, x2 = jnp.split(x, 2, axis=-1)
    return jnp.concatenate([x1 * cos - x2 * sin, x2 * cos + x1 * sin], axis=-1)


def attend(q, k, v, scale):
    s = jnp.einsum("bqkgd,blkd->bkgql", q, k).astype(jnp.float32) * scale
    p = jax.nn.softmax(s, axis=-1).astype(v.dtype)
    return jnp.einsum("bkgql,blkd->bqkgd", p, v)


def blocked_attend(q, k, v, scale):
    b, lq, hk, g, dq = q.shape
    nb = lq // Q_BLOCK
    qb = jnp.moveaxis(q.reshape(b, nb, Q_BLOCK, hk, g, dq), 1, 0)
    ob = lax.map(lambda qi: attend(qi, k, v, scale), qb)
    return jnp.moveaxis(ob, 0, 1).reshape(b, lq, hk, g, v.shape[-1])


def mixer_kv(kvc, lp, rope_mla, rope_gqa):
    b, l, _ = kvc.shape
    c_kv, k_rope, k_g, v_g = jnp.split(
        kvc, [MLA_KV_RANK, MLA_KV_RANK + MLA_ROPE, MLA_KV_RANK + MLA_ROPE + GQA_KV_HEADS * GQA_DIM], axis=-1)
    kv = (rms_norm(c_kv, lp["g_ckv"]) @ lp["w_ukv"]).reshape(b, l, MLA_HEADS, MLA_NOPE + MLA_V)
    k_nope, v_mla = jnp.split(kv, [MLA_NOPE], axis=-1)
    k_rope = jnp.broadcast_to(k_rope[:, :, None, :], (b, l, MLA_HEADS, MLA_ROPE))
    k_mla = rms_norm(jnp.concatenate([k_nope, k_rope], axis=-1), lp["g_kn_mla"])
    k_gqa = rms_norm(k_g.reshape(b, l, GQA_KV_HEADS, GQA_DIM), lp["g_kn_gqa"])
    v_gqa = v_g.reshape(b, l, GQA_KV_HEADS, GQA_DIM)
    if rope_mla is not None:
        k_mla = jnp.concatenate([k_mla[..., :MLA_NOPE], apply_rope(k_mla[..., MLA_NOPE:], rope_mla)], axis=-1)
        k_gqa = apply_rope(k_gqa, rope_gqa)
    return k_mla, v_mla, k_gqa, v_gqa


def mixer_q(qc, lp, rope_mla, rope_gqa):
    b, l, _ = qc.shape
    c_q, q_g = jnp.split(qc, [MLA_Q_RANK], axis=-1)
    q_mla = (rms_norm(c_q, lp["g_cq"]) @ lp["w_uq"]).reshape(b, l, MLA_HEADS, MLA_QK)
    q_mla = rms_norm(q_mla, lp["g_qn_mla"])
    q_gqa = rms_norm(q_g.reshape(b, l, GQA_HEADS, GQA_DIM), lp["g_qn_gqa"])
    if rope_mla is not None:
        q_mla = jnp.concatenate([q_mla[..., :MLA_NOPE], apply_rope(q_mla[..., MLA_NOPE:], rope_mla)], axis=-1)
        q_gqa = apply_rope(q_gqa, rope_gqa)
    return q_mla[:, :, :, None, :], q_gqa.reshape(b, l, GQA_KV_HEADS, GQA_GROUP, GQA_DIM)


def fourier_mix(f):
    b, l, _ = f.shape
    z = jnp.fft.fftn(f.astype(jnp.float32).reshape(b, l, FOUR_GROUPS, FOUR_GROUP_DIM), axes=(1, 3), norm="ortho")
    return jnp.real(z).astype(f.dtype).reshape(b, l, FOUR_WIDTH)


def merge_branches(o_mla, o_gqa, o_four, gate_cols, lp):
    b, l = o_four.shape[:2]
    g_mla, g_gqa, g_four = jnp.split(jax.nn.sigmoid(gate_cols), N_BRANCH, axis=-1)
    y = (g_mla * (o_mla.reshape(b, l, MLA_HEADS * MLA_V) @ lp["w_br_mla"])
         + g_gqa * (o_gqa.reshape(b, l, GQA_HEADS * GQA_DIM) @ lp["w_br_gqa"])
         + g_four * (o_four @ lp["w_four"]))
    return y @ lp["w_o"]


def conv_ffn(h, lp):
    u = h @ lp["w_up"]
    ch = u.shape[-1]
    u = lax.conv_general_dilated(
        u, lp["conv_w"][:, None, :].astype(u.dtype), window_strides=(1,),
        padding=((CONV_W // 2, CONV_W // 2),), dimension_numbers=("NWC", "WIO", "NWC"),
        feature_group_count=ch) + lp["conv_b"]
    a, v = jnp.split(u, 2, axis=-1)
    return (jax.nn.silu(a) * v) @ lp["w_down"]


def layer(x, xc, silu_c, silu_cc, lp, rope_mla, rope_gqa, update_ctx):
    d = D_MODEL
    mod = (silu_c @ lp["w_mod"] + lp["b_mod"])[:, None, :]
    sh1, sc1, gt1, sh2, sc2, gt2 = jnp.split(mod, 6, axis=-1)
    n_mod_c = 6 * d if update_ctx else 2 * d
    modc = (silu_cc @ lp["w_mod"][:, :n_mod_c] + lp["b_mod"][:n_mod_c])[None, None, :]

    h = modulate(x, lp["g_norm1"], sh1, sc1)
    hc = modulate(xc, lp["g_norm1"], modc[..., :d], modc[..., d:2 * d])
    proj = h @ lp["w_in"]
    n_in_c = IN_COLS if update_ctx else KV_COLS
    projc = hc @ lp["w_in"][:, :n_in_c]

    k_mla_l, v_mla_l, k_gqa_l, v_gqa_l = mixer_kv(proj[..., :KV_COLS], lp, rope_mla, rope_gqa)
    k_mla_c, v_mla_c, k_gqa_c, v_gqa_c = mixer_kv(projc[..., :KV_COLS], lp, None, None)
    q_mla, q_gqa = mixer_q(proj[..., KV_COLS:FOUR_START], lp, rope_mla, rope_gqa)

    o_mla = blocked_attend(q_mla, jnp.concatenate([k_mla_c, k_mla_l], axis=1),
                           jnp.concatenate([v_mla_c, v_mla_l], axis=1), MLA_QK ** -0.5)
    o_gqa = blocked_attend(q_gqa, jnp.concatenate([k_gqa_c, k_gqa_l], axis=1),
                           jnp.concatenate([v_gqa_c, v_gqa_l], axis=1), GQA_DIM ** -0.5)
    o_four = fourier_mix(proj[..., FOUR_START:GATE_START])
    x = x + gt1 * merge_branches(o_mla, o_gqa, o_four, proj[..., GATE_START:], lp)
    x = x + gt2 * conv_ffn(modulate(x, lp["g_norm2"], sh2, sc2), lp)

    if update_ctx:
        c_gt1, c_sh2, c_sc2, c_gt2 = jnp.split(modc[..., 2 * d:], 4, axis=-1)
        qc_mla, qc_gqa = mixer_q(projc[..., KV_COLS:FOUR_START], lp, None, None)
        oc_mla = attend(qc_mla, k_mla_c, v_mla_c, MLA_QK ** -0.5)
        oc_gqa = attend(qc_gqa, k_gqa_c, v_gqa_c, GQA_DIM ** -0.5)
        oc_four = fourier_mix(projc[..., FOUR_START:GATE_START])
        xc = xc + c_gt1 * merge_branches(oc_mla, oc_gqa, oc_four, projc[..., GATE_START:], lp)
        xc = xc + c_gt2 * conv_ffn(modulate(xc, lp["g_norm2"], c_sh2, c_sc2), lp)
    return x, xc


def setup_inputs(seed: int = 0) -> dict:
    key = jax.random.key(seed)
    ks = jax.random.split(key, 32)
    f32 = jnp.float32

    def w(k, shape, fan_in, gain=1.0):
        return (jax.random.normal(k, shape, f32) * (gain * fan_in ** -0.5)).astype(f32)

    def gain(k, shape):
        return 1.0 + 0.02 * jax.random.normal(k, shape, f32)

    d = D_MODEL
    return {
        "x": jax.random.normal(ks[0], (BATCH, SEQ, d), f32),
        "c": jax.random.normal(ks[1], (BATCH, d), f32),
        "ctx": jax.random.normal(ks[2], (BATCH, CTX_LEN, d), f32),
        "c_ctx": jax.random.normal(ks[3], (d,), f32),
        "w_mod": w(ks[4], (DEPTH, d, 6 * d), d, 0.5),
        "b_mod": 0.01 * jax.random.normal(ks[5], (DEPTH, 6 * d), f32),
        "g_norm1": gain(ks[6], (DEPTH, d)),
        "g_norm2": gain(ks[7], (DEPTH, d)),
        "w_in": w(ks[8], (DEPTH, d, IN_COLS), d),
        "g_cq": gain(ks[9], (DEPTH, MLA_Q_RANK)),
        "g_ckv": gain(ks[10], (DEPTH, MLA_KV_RANK)),
        "w_uq": w(ks[11], (DEPTH, MLA_Q_RANK, MLA_HEADS * MLA_QK), MLA_Q_RANK),
        "w_ukv": w(ks[12], (DEPTH, MLA_KV_RANK, MLA_HEADS * (MLA_NOPE + MLA_V)), MLA_KV_RANK),
        "g_qn_mla": gain(ks[13], (DEPTH, MLA_QK)),
        "g_kn_mla": gain(ks[14], (DEPTH, MLA_QK)),
        "g_qn_gqa": gain(ks[15], (DEPTH, GQA_DIM)),
        "g_kn_gqa": gain(ks[16], (DEPTH, GQA_DIM)),
        "w_br_mla": w(ks[17], (DEPTH, MLA_HEADS * MLA_V, d), MLA_HEADS * MLA_V),
        "w_br_gqa": w(ks[18], (DEPTH, GQA_HEADS * GQA_DIM, d), GQA_HEADS * GQA_DIM),
        "w_four": w(ks[19], (DEPTH, FOUR_WIDTH, d), FOUR_WIDTH),
        "w_o": w(ks[20], (DEPTH, d, d), d),
        "w_up": w(ks[21], (DEPTH, d, 2 * D_FF), d),
        "conv_w": w(ks[22], (DEPTH, CONV_W, 2 * D_FF), CONV_W),
        "conv_b": 0.01 * jax.random.normal(ks[23], (DEPTH, 2 * D_FF), f32),
        "w_down": w(ks[24], (DEPTH, D_FF, d), D_FF),
    }


def reference(x, c, ctx, c_ctx, w_mod, b_mod, g_norm1, g_norm2, w_in, g_cq, g_ckv, w_uq, w_ukv,
              g_qn_mla, g_kn_mla, g_qn_gqa, g_kn_gqa, w_br_mla, w_br_gqa, w_four, w_o,
              w_up, conv_w, conv_b, w_down):
    n_tok = x.shape[1]
    rows = n_tok // GRID_W
    rope_mla = axial_rope_tables(rows, MLA_ROPE, x.dtype)
    rope_gqa = axial_rope_tables(rows, GQA_DIM, x.dtype)
    silu_c = jax.nn.silu(c)
    silu_cc = jax.nn.silu(c_ctx)
    xc = ctx
    for l in range(DEPTH):
        lp = {
            "w_mod": w_mod[l], "b_mod": b_mod[l], "g_norm1": g_norm1[l], "g_norm2": g_norm2[l],
            "w_in": w_in[l], "g_cq": g_cq[l], "g_ckv": g_ckv[l], "w_uq": w_uq[l], "w_ukv": w_ukv[l],
            "g_qn_mla": g_qn_mla[l], "g_kn_mla": g_kn_mla[l], "g_qn_gqa": g_qn_gqa[l], "g_kn_gqa": g_kn_gqa[l],
            "w_br_mla": w_br_mla[l], "w_br_gqa": w_br_gqa[l], "w_four": w_four[l], "w_o": w_o[l],
            "w_up": w_up[l], "conv_w": conv_w[l], "conv_b": conv_b[l], "w_down": w_down[l],
        }
        x, xc = layer(x, xc, silu_c, silu_cc, lp, rope_mla, rope_gqa, l < DEPTH - 1)
    return x
```

```python
import numpy as np
import ml_dtypes
from contextlib import ExitStack
import concourse.bass as bass
import concourse.mybir as mybir
from concourse.bass_utils import run_bass_kernel_spmd

F32 = mybir.dt.float32
BF16 = mybir.dt.bfloat16
ALU = mybir.AluOpType
AF = mybir.ActivationFunctionType
AX = mybir.AxisListType

D = 1024
NTOK = 4352
CTX = 256
SEQ = 4096
NT = NTOK // 128
DEPTH = 2
DFF = 2816
EPS = 1e-6
WCOLS = 1440 + 1024 + 3072


class Buf:
    __slots__ = ("wi", "ri", "name")

    def __init__(self, name=""):
        self.wi = set()
        self.ri = set()
        self.name = name


class K:
    NDMA = 24
    EPOCH = 60000
    WINDOW = 64
    LAT = 250.0

    def __init__(self, nc):
        self.nc = nc
        self.es = ExitStack()
        self.engs = {"pe": nc.tensor, "act": nc.scalar, "dve": nc.vector, "pool": nc.gpsimd, "sp": nc.sync}
        self.sems = {}
        self.val = {}
        self.waited = {e: {} for e in self.engs}
        self.epoch = {e: 0 for e in ("pe", "act", "dve", "pool")}
        self.cur = {}
        for e in ("pe", "act", "dve", "pool"):
            self._new_epoch(e)
        self.dma_names = []
        for i in range(self.NDMA):
            n = f"dma{i}"
            self._newsem(n)
            self.dma_names.append(n)
        self.dma_rr = 0
        self.ninst = 0
        self.recs = []
        self.touched = set()
        self.sim_log = []

    def _newsem(self, name):
        self.sems[name] = self.es.enter_context(self.nc.semaphore(name))
        self.val[name] = 0

    def _new_epoch(self, e):
        n = f"{e}_{self.epoch[e]}"
        self.epoch[e] += 1
        self._newsem(n)
        self.cur[e] = n

    def wait(self, eng, name, v):
        if v <= 0:
            return
        if self.waited[eng].get(name, 0) >= v:
            return
        self.engs[eng].wait_ge(self.sems[name], v)
        self.waited[eng][name] = v

    def _record(self, kind, eng, payload, r, w, n):
        i = len(self.recs)
        raw = set()
        order = set()
        for b in r:
            raw |= b.wi
        for b in w:
            order |= b.wi
            order |= b.ri
        order -= raw
        order.discard(i)
        self.recs.append([kind, eng, payload, raw, order, float(n), None, None])
        for b in r:
            b.ri.add(i)
            self.touched.add(b)
        for b in w:
            b.wi = {i}
            b.ri = set()
            self.touched.add(b)

    def op(self, eng, fn, r=(), w=(), n=512):
        self._record("op", eng, fn, r, w, n)

    def dma(self, q, out, in_, r=(), w=(), slow=False):
        nb = max(out.nbytes(), in_.nbytes())
        self._record("dma", q, (out, in_, slow), r, w, nb)

    @staticmethod
    def _dur(kind, eng, n):
        if kind == "dma":
            return 1000.0 if eng == "pool" else 60.0
        if eng == "pe":
            return max(n, 64.0) / 2.4 + 8.0
        if eng == "act":
            return 224.0 + 0.833 * n
        if eng == "dve":
            return 70.0 + 1.05 * n
        return 120.0 + 2.1 * n

    def flush(self):
        recs = self.recs
        N = len(recs)
        if N == 0:
            return
        self._busy = {}
        queues = {e: [] for e in self.engs}
        for i, rc_ in enumerate(recs):
            queues[rc_[1]].append(i)
        head = {e: 0 for e in queues}
        issued = [False] * N
        done = [None] * N
        eng_free = {e: 0.0 for e in queues}
        dma_pipe = 0.0
        remaining = N
        W = self.WINDOW
        LAT = self.LAT
        while remaining:
            best = None
            for e, q in queues.items():
                h = head[e]
                L = len(q)
                while h < L and issued[q[h]]:
                    h += 1
                head[e] = h
                cnt = 0
                j = h
                ef = eng_free[e]
                while j < L and cnt < W:
                    idx = q[j]
                    j += 1
                    if issued[idx]:
                        continue
                    cnt += 1
                    rc_ = recs[idx]
                    ready = 0.0
                    ok = True
                    for d in rc_[3]:
                        t = done[d]
                        if t is None:
                            ok = False
                            break
                        if t + LAT > ready:
                            ready = t + LAT
                    if not ok:
                        continue
                    for d in rc_[4]:
                        t = done[d]
                        if t is None:
                            ok = False
                            break
                        if recs[d][1] != e or recs[d][0] == "dma":
                            if t + LAT > ready:
                                ready = t + LAT
                    if not ok:
                        continue
                    start = ready if ready > ef else ef
                    if best is None or start < best[0] or (start == best[0] and idx < best[2]):
                        best = (start, e, idx)
                    if start <= ef:
                        break
            assert best is not None, "scheduler deadlock"
            start, e, idx = best
            rc_ = recs[idx]
            kind, _, payload, raw, order, n = rc_[0], rc_[1], rc_[2], rc_[3], rc_[4], rc_[5]
            dur = self._dur(kind, e, n)
            self._busy[e] = self._busy.get(e, 0.0) + dur
            eng_free[e] = start + dur
            if kind == "dma":
                xs = max(start + dur, dma_pipe)
                dma_pipe = xs + n / 180.0
                done[idx] = xs + 2000.0 + n / 180.0
            else:
                done[idx] = start + dur
            issued[idx] = True
            remaining -= 1
            self._emit(idx, rc_)
        self.sim_log.append((N, max(t for t in done if t is not None), dict(self._busy)))
        self.recs = []
        for b in self.touched:
            b.wi = set()
            b.ri = set()
        self.touched = set()

    def _emit(self, idx, rc_):
        kind, eng, payload, raw, order = rc_[0], rc_[1], rc_[2], rc_[3], rc_[4]
        recs = self.recs
        for d in raw:
            name, v = recs[d][6]
            if recs[d][7] == eng and eng == "pe":
                continue
            self.wait(eng, name, v)
        for d in order:
            if recs[d][7] == eng:
                continue
            name, v = recs[d][6]
            self.wait(eng, name, v)
        if kind == "op":
            if self.val[self.cur[eng]] >= self.EPOCH:
                self._new_epoch(eng)
            name = self.cur[eng]
            ins = payload()
            self.val[name] += 1
            ins.then_inc(self.sems[name], 1)
            rc_[6] = (name, self.val[name])
            rc_[7] = eng
        else:
            name = self.dma_names[self.dma_rr]
            self.dma_rr = (self.dma_rr + 1) % self.NDMA
            self.wait(eng, name, self.val[name])
            out, in_, slow = payload
            if slow:
                ins = self.engs[eng].dma_start(out=out, in_=in_, allow_slow_non_contiguous=True)
            else:
                ins = self.engs[eng].dma_start(out=out, in_=in_)
            self.val[name] += 16
            ins.then_inc(self.sems[name], 16)
            rc_[6] = (name, self.val[name])
            rc_[7] = "dma"
        rc_[2] = None
        self.ninst += 1

    def barrier(self):
        self.flush()
        for e in self.engs:
            for name, v in self.val.items():
                self.wait(e, name, v)


def build(depth=DEPTH, debug=(), stop_after=None):
    nc = bass.Bass("TRN2", target_bir_lowering=False)
    k = K(nc)
    dbg = set(debug)
    _uid = [0]

    def sbt(name, shape, dt):
        _uid[0] += 1
        return nc.sbuf_tensor(f"{name}_u{_uid[0]}", shape, dt)

    def din(name, shape, dt=F32):
        return nc.dram_tensor(name, list(shape), dt, kind="ExternalInput").ap()

    def dscr(name, shape, dt=F32):
        kind = "ExternalOutput" if name in dbg else "Internal"
        return nc.dram_tensor(name, list(shape), dt, kind=kind).ap()

    xin = din("xin", [NTOK, D])
    cT_in = din("cT", [128, 8, 2])
    w_mod = din("w_mod", [depth, D, 6 * D])
    b_mod = din("b_mod", [depth, 6 * D])
    g_norm1 = din("g_norm1", [depth, D])
    g_norm2 = din("g_norm2", [depth, D])
    w_in = din("w_in", [depth, D, 5024])
    gvec = din("gvec", [depth, 960])
    w_uq = din("w_uq", [depth, 384, 768])
    w_ukv = din("w_ukv", [depth, 256, 1024])
    w_br = [din("w_br_mla", [depth, 512, D]), din("w_br_gqa", [depth, 512, D]), din("w_four", [depth, 512, D])]
    w_o = din("w_o", [depth, D, D])
    w_up = din("w_up", [depth, D, 2 * DFF])
    convw = din("convw", [depth, 128, 44, 3])
    convb = din("convb", [depth, 128, 44])
    w_down = din("w_down", [depth, DFF, D])
    rope = din("rope", [NTOK, 96])
    ccsc = din("ccsc", [128, 256])
    dftc = din("dftc", [SEQ, SEQ], BF16)
    dfts = din("dfts", [SEQ, SEQ], BF16)
    dftc_c = din("dftc_c", [CTX, CTX], BF16)
    dfts_c = din("dfts_c", [CTX, CTX], BF16)
    y_out = nc.dram_tensor("y", [SEQ // 2, D], F32, kind="ExternalOutput").ap()
    cmask = din("cmask", [128, 2])

    modv = dscr("modv", [depth, 2, 6 * D])
    x1 = dscr("x1", [NTOK, D])
    xmid = dscr("xmid", [NTOK, D])
    KTm = dscr("KTm", [8, 96, NTOK], BF16)
    QTm = dscr("QTm", [8, 96, NTOK], BF16)
    Vm = dscr("Vm", [NTOK, 8, 64], BF16)
    KTg = dscr("KTg", [2, 64, NTOK], BF16)
    QTg = dscr("QTg", [8, 64, NTOK], BF16)
    Vg = dscr("Vg", [NTOK, 2, 64], BF16)
    ABd = dscr("ABd", [NTOK, 1024], BF16)
    Gd = dscr("Gd", [NTOK, 3072], BF16)
    OTm = dscr("OTm", [4, 128, NTOK], BF16)
    OTg = dscr("OTg", [4, 128, NTOK], BF16)
    OTf = dscr("OTf", [4, 128, NTOK], BF16)
    H2T = dscr("H2T", [8, 128, NTOK], BF16)

    with k.es:
        gE = k.es.enter_context
        PSALL = gE(nc.psum_tensor("psall", [128, 4096], F32))

        class Bank:
            def __init__(self, i):
                self.i = i

            def __getitem__(self, key):
                b0 = self.i * 512
                if isinstance(key, slice):
                    return PSALL[:, b0:b0 + 512]
                rows, cols = key
                c0 = b0 + (cols.start or 0)
                c1 = b0 + (512 if cols.stop is None else cols.stop)
                return PSALL[rows, c0:c1]

        PS = [Bank(i) for i in range(8)]
        PSB = [Buf(f"ps{i}") for i in range(8)]
        identb = gE(sbt("identb", [128, 128], BF16))
        ident32 = gE(sbt("ident32", [128, 128], F32))
        idB = Buf("ident")
        k.op("pool", lambda: nc.gpsimd.memset(identb[:], 0.0), w=[idB])
        k.op("pool", lambda: nc.gpsimd.affine_select(out=identb[:], in_=identb[:], pattern=[[-1, 128]],
                                                     compare_op=ALU.not_equal, fill=1.0, base=0,
                                                     channel_multiplier=1), r=[idB], w=[idB])
        k.op("pool", lambda: nc.gpsimd.memset(ident32[:], 0.0), w=[idB])
        k.op("pool", lambda: nc.gpsimd.affine_select(out=ident32[:], in_=ident32[:], pattern=[[-1, 128]],
                                                     compare_op=ALU.not_equal, fill=1.0, base=0,
                                                     channel_multiplier=1), r=[idB], w=[idB])
        k.barrier()

        def psbf(i):
            return PS[i][:].bitcast(BF16)

        done = [False]

        def check_stop(tag):
            if stop_after == tag:
                done[0] = True
            return done[0]

        def phase0(l, E_outer=None, banks=(0, 1), cbw=512):
            with ExitStack() as es:
                E = es.enter_context if E_outer is None else E_outer
                cT = E(sbt("p0_cT", [128, 8, 2], F32))
                sc = E(sbt("p0_sc", [128, 8, 2], F32))
                ob = [E(sbt(f"p0_o{i}", [2, cbw], F32)) for i in range(2)]
                bm = [E(sbt(f"p0_bm{i}", [2, cbw], F32)) for i in range(2)]
                gg = [E(sbt(f"p0_g{i}", [2, cbw], F32)) for i in range(2)]
                wb = [E(sbt(f"p0_w{i}", [128, 8, cbw], F32)) for i in range(2)]
                bcT, bsc = Buf(), Buf()
                bob, bbm, bgg, bwb = [Buf(), Buf()], [Buf(), Buf()], [Buf(), Buf()], [Buf(), Buf()]
                k.dma("sp", cT[:], cT_in, w=[bcT])
                k.op("act", lambda: nc.scalar.activation(out=sc[:], in_=cT[:], func=AF.Silu), r=[bcT], w=[bsc],
                     n=16)
                wm = w_mod[l].rearrange("(k p) n -> p k n", p=128)

                def colblock(cb):
                    j = cb % 2
                    c0 = cb * cbw
                    k.dma("sp", wb[j][:], wm[:, :, c0:c0 + cbw], w=[bwb[j]])
                    k.dma("sp", bm[j][:], b_mod[l, c0:c0 + cbw].partition_broadcast(2), w=[bbm[j]])
                    gsrc = None
                    if D <= c0 < 2 * D:
                        gsrc = g_norm1[l, c0 - D:c0 - D + cbw]
                    elif 4 * D <= c0 < 5 * D:
                        gsrc = g_norm2[l, c0 - 4 * D:c0 - 4 * D + cbw]
                    if gsrc is not None:
                        k.dma("sp", gg[j][:], gsrc.partition_broadcast(2), w=[bgg[j]])
                    for kk in range(8):
                        k.op("pe", lambda kk=kk: nc.tensor.matmul(PS[banks[j]][0:2, 0:cbw], lhsT=sc[:, kk, :],
                                                                  rhs=wb[j][:, kk, :], start=(kk == 0),
                                                                  stop=(kk == 7)),
                             r=[bsc, bwb[j]], w=[PSB[banks[j]]], n=4 * cbw)
                    k.op("dve", lambda: nc.vector.tensor_tensor(out=ob[j][:], in0=PS[banks[j]][0:2, 0:cbw],
                                                                in1=bm[j][:], op=ALU.add),
                         r=[PSB[banks[j]], bbm[j]], w=[bob[j]], n=cbw)
                    if gsrc is not None:
                        k.op("dve", lambda: nc.vector.scalar_tensor_tensor(out=ob[j][:], in0=ob[j][:], scalar=1.0,
                                                                           in1=gg[j][:], op0=ALU.add,
                                                                           op1=ALU.mult),
                             r=[bob[j], bgg[j]], w=[bob[j]], n=cbw)
                    k.dma("sp", modv[l, :, c0:c0 + cbw], ob[j][:], r=[bob[j]])

                for cb in range(6 * D // cbw):
                    colblock(cb)
                if E_outer is None:
                    k.barrier()

        def phase1(l, xcur):
            with ExitStack() as es:
                E = es.enter_context

                def sb(name, shape, dt=F32):
                    return E(sbt("p1_" + name, list(shape), dt))

                win = sb("win", [128, 8, WCOLS], BF16)
                wukv = sb("wukv", [128, 2, 1024], BF16)
                wuq = sb("wuq", [128, 3, 768], BF16)
                gv = sb("gv", [128, 960])
                gm = [sb("gm0", [128, D]), sb("gm1", [128, D])]
                sh = [sb("sh0", [128, D]), sb("sh1", [128, D])]
                bwin = [Buf() for _ in range(8)]
                bwin2 = [Buf() for _ in range(8)]
                bwk, bwq, bgv = Buf(), Buf(), Buf()
                bgm = [Buf(), Buf()]
                bsh = [Buf(), Buf()]
                wi = w_in[l].rearrange("(k p) n -> p k n", p=128)
                for kk in range(8):
                    k.dma("pool", win[:, kk, 0:1440], wi[:, kk, 0:1440], w=[bwin[kk]])
                k.dma("pool", wukv[:], w_ukv[l].rearrange("(k p) n -> p k n", p=128), w=[bwk])
                k.dma("pool", wuq[:], w_uq[l].rearrange("(k p) n -> p k n", p=128), w=[bwq])
                k.dma("sp", gv[:], gvec[l].partition_broadcast(128), w=[bgv])
                for j in range(2):
                    k.dma("sp", sh[j][:], modv[l, j, 0:D].partition_broadcast(128), w=[bsh[j]])
                    k.dma("sp", gm[j][:], modv[l, j, D:2 * D].partition_broadcast(128), w=[bgm[j]])
                bwab = Buf()
                with ExitStack() as es2:
                    E2 = es2.enter_context
                    wf = E2(sbt("p1_wf", [128, 8, 512], F32))
                    cc = E2(sbt("p1_cc", [128, 256], F32))
                    wft = [E2(sbt(f"p1_wft{i}", [128, 128], F32)) for i in range(2)]
                    bwf, bcc = Buf(), Buf()
                    bwft = [Buf(), Buf()]
                    k.dma("sp", wf[:], wi[:, :, 1440:1952], w=[bwf])
                    k.dma("sp", cc[:], ccsc, w=[bcc])
                    n = 0
                    for g in range(4):
                        for kk in range(8):
                            j = n % 2
                            n += 1
                            k.op("pe", lambda g=g, kk=kk, j=j: nc.tensor.transpose(
                                out=PS[j][:, 0:128], in_=wf[:, kk, g * 128:(g + 1) * 128], identity=ident32[:]),
                                 r=[bwf, idB], w=[PSB[j]], n=512)
                            k.op("act", lambda j=j: nc.scalar.copy(out=wft[j][:], in_=PS[j][:, 0:128]),
                                 r=[PSB[j]], w=[bwft[j]], n=128)
                            k.op("pe", lambda j=j: nc.tensor.matmul(PS[2 + j][:, 0:256], lhsT=wft[j][:], rhs=cc[:],
                                                                    start=True, stop=True),
                                 r=[bwft[j], bcc], w=[PSB[2 + j]], n=1024)
                            k.op("dve", lambda g=g, kk=kk, j=j: nc.vector.tensor_copy(
                                out=win[:, kk, 1440 + g * 128:1440 + (g + 1) * 128], in_=PS[2 + j][:, 0:128]),
                                 r=[PSB[2 + j]], w=[bwab], n=128)
                            k.op("dve", lambda g=g, kk=kk, j=j: nc.vector.tensor_copy(
                                out=win[:, kk, 1952 + g * 128:1952 + (g + 1) * 128], in_=PS[2 + j][:, 128:256]),
                                 r=[PSB[2 + j]], w=[bwab], n=128)
                    k.barrier()
                for kk in range(8):
                    k.dma("pool", win[:, kk, 2464:WCOLS], wi[:, kk, 1952:5024], w=[bwin2[kk]])
                WALL = bwin + bwin2 + [bwab]

                xt = [sb(f"xt{i}", [128, D]) for i in range(2)]
                rt = [sb(f"rt{i}", [128, 96]) for i in range(3)]
                hb = [sb(f"hb{i}", [128, D], BF16) for i in range(2)]
                hT = [sb(f"hT{i}", [128, D], BF16) for i in range(2)]
                proj = [sb(f"proj{i}", [128, 1440]) for i in range(3)]
                abb_ = sb("abb", [128, 1024], BF16)
                gts_ = sb("gts", [128, 3072], BF16)
                abb = [abb_, abb_]
                gts = [gts_, gts_]
                tmp = sb("tmp", [128, D])
                stx = sb("stx", [128, 2])
                rsx = sb("rsx", [128, 2])
                sq = sb("sq", [128, 1440], BF16)
                st_ = [sb(f"st{i}", [128, 16]) for i in range(2)]
                rs_ = [sb(f"rs{i}", [128, 16]) for i in range(2)]
                st2 = sb("st2", [128, 16])
                rs2 = sb("rs2", [128, 16])
                cn = sb("cn", [128, 640], BF16)
                cT5 = sb("cT5", [128, 640], BF16)
                kvsb_ = [sb(f"kvsb{i}", [128, 1024]) for i in range(2)]
                qm_ = [sb(f"qm{i}", [128, 768]) for i in range(2)]
                kcat = sb("kcat", [128, 768])
                tA = [sb(f"tA{i}", [128, 768]) for i in range(2)]
                t1 = [sb(f"t1_{i}", [128, 256]) for i in range(2)]
                t2 = [sb(f"t2_{i}", [128, 256]) for i in range(2)]
                kb = sb("kb", [128, 768], BF16)
                qb = sb("qb", [128, 768], BF16)
                vb = sb("vb", [128, 512], BF16)
                kgb = sb("kgb", [128, 128], BF16)
                vgb = sb("vgb", [128, 128], BF16)
                qgb = sb("qgb", [128, 512], BF16)
                ktb = sb("ktb", [128, 1024], BF16)
                qtb = sb("qtb", [128, 1024], BF16)
                kgt = sb("kgt", [128, 128], BF16)
                qgt = sb("qgt", [128, 512], BF16)
                names2 = ["xt", "rt", "hb", "hT", "proj", "abb", "gts", "tA", "t1", "t2"]
                B = {n_: [Buf(n_ + "0"), Buf(n_ + "1"), Buf(n_ + "2")] for n_ in names2 + ["st", "rs", "kvsb", "qm"]}
                B["abb"][1] = B["abb"][0]
                B["gts"][1] = B["gts"][0]
                for n_ in ["tmp", "stx", "rsx", "sq", "st2", "rs2", "cn", "cT5", "kcat",
                           "kb", "qb", "vb", "kgb", "vgb", "qgb", "ktb", "qtb", "kgt", "qgt"]:
                    B[n_] = Buf(n_)

                def rstd_from(stt, rst, lo, hi, n, bs, br):
                    k.op("act", lambda: nc.scalar.activation(out=rst[:, lo:hi], in_=stt[:, lo:hi], func=AF.Ln,
                                                             scale=1.0 / n, bias=EPS), r=[bs], w=[br], n=16)

                def norm_rope(src3, H, dh, rs_ap, g_ap, nrot, cos_ap, sin_ap, out3, rsrc, bout, ci, brt):
                    tA3 = tA[ci][:, 0:H * dh].rearrange("p (h d) -> p h d", h=H)
                    bA, b1, b2 = B["tA"][ci], B["t1"][ci], B["t2"][ci]
                    k.op("dve", lambda: nc.vector.tensor_tensor(out=tA3, in0=src3,
                                                                in1=rs_ap.unsqueeze(2).to_broadcast([128, H, dh]),
                                                                op=ALU.mult), r=rsrc, w=[bA], n=H * dh)
                    k.op("dve", lambda: nc.vector.tensor_tensor(out=tA3, in0=tA3,
                                                                in1=g_ap.unsqueeze(1).to_broadcast([128, H, dh]),
                                                                op=ALU.mult), r=[bA, bgv], w=[bA], n=H * dh)
                    r0 = dh - nrot
                    hf = nrot // 2
                    if r0 > 0:
                        k.op("act", lambda: nc.scalar.copy(out=out3[:, :, 0:r0], in_=tA3[:, :, 0:r0]),
                             r=[bA], w=[bout], n=H * r0)
                    x1_ = tA3[:, :, r0:r0 + hf]
                    x2_ = tA3[:, :, r0 + hf:dh]
                    cb_ = cos_ap.unsqueeze(1).to_broadcast([128, H, hf])
                    sb_ = sin_ap.unsqueeze(1).to_broadcast([128, H, hf])
                    t13 = t1[ci][:, 0:H * hf].rearrange("p (h d) -> p h d", h=H)
                    t23 = t2[ci][:, 0:H * hf].rearrange("p (h d) -> p h d", h=H)
                    k.op("dve", lambda: nc.vector.tensor_tensor(out=t13, in0=x1_, in1=cb_, op=ALU.mult),
                         r=[bA, brt], w=[b1], n=H * hf)
                    k.op("pool", lambda: nc.gpsimd.tensor_tensor(out=t23, in0=x2_, in1=sb_, op=ALU.mult),
                         r=[bA, brt], w=[b2], n=H * hf)
                    k.op("dve", lambda: nc.vector.tensor_tensor(out=out3[:, :, r0:r0 + hf], in0=t13, in1=t23,
                                                                op=ALU.subtract), r=[b1, b2], w=[bout], n=H * hf)
                    k.op("dve", lambda: nc.vector.tensor_tensor(out=t13, in0=x2_, in1=cb_, op=ALU.mult),
                         r=[bA, brt], w=[b1], n=H * hf)
                    k.op("pool", lambda: nc.gpsimd.tensor_tensor(out=t23, in0=x1_, in1=sb_, op=ALU.mult),
                         r=[bA, brt], w=[b2], n=H * hf)
                    k.op("dve", lambda: nc.vector.tensor_tensor(out=out3[:, :, r0 + hf:dh], in0=t13, in1=t23,
                                                                op=ALU.add), r=[b1, b2], w=[bout], n=H * hf)

                G_CKV, G_CQ, G_KNM, G_QNM, G_KNG, G_QNG = (gv[:, 0:256], gv[:, 256:640], gv[:, 640:736],
                                                           gv[:, 736:832], gv[:, 832:896], gv[:, 896:960])
                blocks = [(0, 512, "p"), (512, 512, "p"), (1024, 416, "p"), (1440, 512, "ab"), (1952, 512, "ab")]
                blocks += [(2464 + 512 * j, 512, "g") for j in range(6)]

                def stageA(i):
                    s_ = i % 2
                    tok = slice(i * 128, (i + 1) * 128)
                    ic = 1 if i < 2 else 0
                    k.dma("sp", xt[s_][:], xcur[tok, :], w=[B["xt"][s_]])
                    k.dma("sp", rt[i % 3][:], rope[tok, :], w=[B["rt"][i % 3]])
                    k.op("act", lambda: nc.scalar.activation(out=tmp[:], in_=xt[s_][:], func=AF.Square),
                         r=[B["xt"][s_]], w=[B["tmp"]], n=1024)
                    k.op("dve", lambda: nc.vector.tensor_reduce(out=stx[:, 0:1], in_=tmp[:], axis=AX.X, op=ALU.add),
                         r=[B["tmp"]], w=[B["stx"]], n=1024)
                    rstd_from(stx, rsx, 0, 1, D, B["stx"], B["rsx"])
                    k.op("act", lambda: nc.scalar.activation(out=rsx[:, 0:1], in_=rsx[:, 0:1], func=AF.Exp,
                                                             scale=-0.5), r=[B["rsx"]], w=[B["rsx"]], n=16)
                    k.op("dve", lambda: nc.vector.scalar_tensor_tensor(out=tmp[:], in0=xt[s_][:],
                                                                       scalar=rsx[:, 0:1], in1=gm[ic][:],
                                                                       op0=ALU.mult, op1=ALU.mult),
                         r=[B["xt"][s_], B["rsx"], bgm[ic]], w=[B["tmp"]], n=1024)
                    k.op("dve", lambda: nc.vector.tensor_tensor(out=hb[s_][:], in0=tmp[:], in1=sh[ic][:],
                                                                op=ALU.add),
                         r=[B["tmp"], bsh[ic]], w=[B["hb"][s_]], n=1024)
                    for kk in range(8):
                        k.op("pe", lambda kk=kk: nc.tensor.transpose(out=psbf(7)[:, kk * 128:(kk + 1) * 128],
                                                                     in_=hb[s_][:, kk * 128:(kk + 1) * 128],
                                                                     identity=identb[:]),
                             r=[B["hb"][s_], idB], w=[PSB[7]], n=128)
                    k.op("act", lambda: nc.scalar.copy(out=hT[s_][:], in_=psbf(7)[:, 0:1024]), r=[PSB[7]],
                         w=[B["hT"][s_]], n=1024)

                own = set(range(NT)) if l < depth - 1 else set(range(2, 19))
                blocks_kv = [(0, 512, "p"), (512, 32, "p"), (1440, 512, "ab"), (1952, 512, "ab")]

                def stageB(i):
                    s_ = i % 2
                    tok = slice(i * 128, (i + 1) * 128)
                    for bi, (c0, wd, kind) in enumerate(blocks if i in own else blocks_kv):
                        pb = bi % 4
                        for kk in range(8):
                            wb_ = bwin[kk] if kind == "p" else (bwab if kind == "ab" else bwin2[kk])
                            k.op("pe", lambda kk=kk, pb=pb, c0=c0, wd=wd: nc.tensor.matmul(
                                PS[pb][:, 0:wd], lhsT=hT[s_][:, kk * 128:(kk + 1) * 128], rhs=win[:, kk, c0:c0 + wd],
                                start=(kk == 0), stop=(kk == 7)), r=[B["hT"][s_], wb_], w=[PSB[pb]], n=wd)
                        if kind == "p":
                            k.op("dve", lambda pb=pb, c0=c0, wd=wd: nc.vector.tensor_copy(
                                out=proj[i % 3][:, c0:c0 + wd], in_=PS[pb][:, 0:wd]), r=[PSB[pb]],
                                 w=[B["proj"][i % 3]], n=wd)
                        elif kind == "ab":
                            k.op("act", lambda pb=pb, c0=c0: nc.scalar.copy(
                                out=abb[s_][:, c0 - 1440:c0 - 1440 + 512], in_=PS[pb][:, 0:512]),
                                 r=[PSB[pb]], w=[B["abb"][s_]], n=512)
                        else:
                            k.op("act", lambda pb=pb, c0=c0: nc.scalar.activation(
                                out=gts[s_][:, c0 - 2464:c0 - 2464 + 512], in_=PS[pb][:, 0:512], func=AF.Sigmoid),
                                 r=[PSB[pb]], w=[B["gts"][s_]], n=512)
                    k.dma("sp", ABd[tok, :], abb[s_][:], r=[B["abb"][s_]])
                    if i in own:
                        k.dma("sp", Gd[tok, :], gts[s_][:], r=[B["gts"][s_]])

                def cvars(i):
                    s_ = i % 2
                    return (slice(i * 128, (i + 1) * 128), proj[i % 3], B["proj"][i % 3], B["rt"][i % 3], rt[i % 3],
                            st_[s_], rs_[s_], kvsb_[s_], qm_[s_],
                            {"st": B["st"][s_], "rs": B["rs"][s_], "kvsb": B["kvsb"][s_], "qm": B["qm"][s_]})

                def stageC1(i):
                    tok, pj, bpj, brt, rtt, st, rs, kvsb, qm, BB = cvars(i)
                    B = dict(B_all)
                    B.update(BB)
                    full = i in own
                    wsq = 1440 if full else 544
                    k.op("dve", lambda: nc.vector.tensor_tensor(out=sq[:, 0:wsq], in0=pj[:, 0:wsq], in1=pj[:, 0:wsq],
                                                                op=ALU.mult),
                         r=[bpj], w=[B["sq"]], n=wsq)
                    k.op("dve", lambda: nc.vector.tensor_reduce(out=st[:, 0:1], in_=sq[:, 0:256], axis=AX.X,
                                                                op=ALU.add), r=[B["sq"]], w=[B["st"]], n=256)
                    if full:
                        k.op("dve", lambda: nc.vector.tensor_reduce(out=st[:, 1:2], in_=sq[:, 544:928], axis=AX.X,
                                                                    op=ALU.add), r=[B["sq"]], w=[B["st"]], n=384)
                    k.op("dve", lambda: nc.vector.tensor_reduce(
                        out=st[:, 2:4], in_=sq[:, 288:416].rearrange("p (h d) -> p h d", h=2), axis=AX.X,
                        op=ALU.add), r=[B["sq"]], w=[B["st"]], n=128)
                    if full:
                        k.op("dve", lambda: nc.vector.tensor_reduce(
                            out=st[:, 4:12], in_=sq[:, 928:1440].rearrange("p (h d) -> p h d", h=8), axis=AX.X,
                            op=ALU.add), r=[B["sq"]], w=[B["st"]], n=512)
                    rstd_from(st, rs, 0, 1, 256, B["st"], B["rs"])
                    if full:
                        rstd_from(st, rs, 1, 2, 384, B["st"], B["rs"])
                        rstd_from(st, rs, 2, 12, 64, B["st"], B["rs"])
                        k.op("act", lambda: nc.scalar.activation(out=rs[:, 0:12], in_=rs[:, 0:12], func=AF.Exp,
                                                                 scale=-0.5), r=[B["rs"]], w=[B["rs"]], n=16)
                    else:
                        rstd_from(st, rs, 2, 4, 64, B["st"], B["rs"])
                        k.op("act", lambda: nc.scalar.activation(out=rs[:, 0:1], in_=rs[:, 0:1], func=AF.Exp,
                                                                 scale=-0.5), r=[B["rs"]], w=[B["rs"]], n=16)
                        k.op("act", lambda: nc.scalar.activation(out=rs[:, 2:4], in_=rs[:, 2:4], func=AF.Exp,
                                                                 scale=-0.5), r=[B["rs"]], w=[B["rs"]], n=16)
                    k.op("dve", lambda: nc.vector.scalar_tensor_tensor(out=cn[:, 0:256], in0=pj[:, 0:256],
                                                                       scalar=rs[:, 0:1], in1=G_CKV, op0=ALU.mult,
                                                                       op1=ALU.mult),
                         r=[bpj, B["rs"], bgv], w=[B["cn"]], n=256)
                    if full:
                        k.op("dve", lambda: nc.vector.scalar_tensor_tensor(out=cn[:, 256:640], in0=pj[:, 544:928],
                                                                           scalar=rs[:, 1:2], in1=G_CQ,
                                                                           op0=ALU.mult, op1=ALU.mult),
                             r=[bpj, B["rs"], bgv], w=[B["cn"]], n=384)
                    ncn = 5 if full else 2
                    for kk in range(ncn):
                        k.op("pe", lambda kk=kk: nc.tensor.transpose(out=psbf(6)[:, kk * 128:(kk + 1) * 128],
                                                                     in_=cn[:, kk * 128:(kk + 1) * 128],
                                                                     identity=identb[:]),
                             r=[B["cn"], idB], w=[PSB[6]], n=128)
                    k.op("act", lambda: nc.scalar.copy(out=cT5[:, 0:ncn * 128], in_=psbf(6)[:, 0:ncn * 128]),
                         r=[PSB[6]], w=[B["cT5"]], n=ncn * 128)
                    for half in range(2):
                        for kk in range(2):
                            k.op("pe", lambda kk=kk, half=half: nc.tensor.matmul(
                                PS[4 + half][:, 0:512], lhsT=cT5[:, kk * 128:(kk + 1) * 128],
                                rhs=wukv[:, kk, half * 512:(half + 1) * 512], start=(kk == 0), stop=(kk == 1)),
                                 r=[B["cT5"], bwk], w=[PSB[4 + half]], n=512)
                        k.op("act", lambda half=half: nc.scalar.copy(out=kvsb[:, half * 512:(half + 1) * 512],
                                                                     in_=PS[4 + half][:, 0:512]),
                             r=[PSB[4 + half]], w=[B["kvsb"]], n=512)
                    for half, (c0, wd) in enumerate([(0, 512), (512, 256)] if full else []):
                        for kk in range(3):
                            k.op("pe", lambda kk=kk, half=half, c0=c0, wd=wd: nc.tensor.matmul(
                                PS[4 + half][:, 0:wd], lhsT=cT5[:, 256 + kk * 128:256 + (kk + 1) * 128],
                                rhs=wuq[:, kk, c0:c0 + wd], start=(kk == 0), stop=(kk == 2)),
                                 r=[B["cT5"], bwq], w=[PSB[4 + half]], n=wd)
                        k.op("act", lambda half=half, c0=c0, wd=wd: nc.scalar.copy(out=qm[:, c0:c0 + wd],
                                                                                  in_=PS[4 + half][:, 0:wd]),
                             r=[PSB[4 + half]], w=[B["qm"]], n=wd)

                def stageC2(i):
                    tok, pj, bpj, brt, rtt, st, rs, kvsb, qm, BB = cvars(i)
                    B = dict(B_all)
                    B.update(BB)
                    full = i in own
                    kv3 = kvsb[:].rearrange("p (h d) -> p h d", h=8)
                    kc3 = kcat[:].rearrange("p (h d) -> p h d", h=8)
                    k.op("act", lambda: nc.scalar.copy(out=kc3[:, :, 0:64], in_=kv3[:, :, 0:64]),
                         r=[B["kvsb"]], w=[B["kcat"]], n=512)
                    k.op("pool", lambda: nc.gpsimd.tensor_copy(
                        out=kc3[:, :, 64:96], in_=pj[:, 256:288].unsqueeze(1).to_broadcast([128, 8, 32])),
                         r=[bpj, B["kcat"]], w=[B["kcat"]], n=256)
                    k.op("act", lambda: nc.scalar.copy(out=vb[:].rearrange("p (h d) -> p h d", h=8),
                                                       in_=kv3[:, :, 64:128]), r=[B["kvsb"]], w=[B["vb"]], n=512)
                    k.dma("sp", Vm[tok, :, :].rearrange("t h d -> t (h d)"), vb[:], r=[B["vb"]])
                    for ci, (src, s0) in enumerate(((kcat, 0), (qm, 8)) if full else ((kcat, 0),)):
                        bsrc = B["kcat"] if s0 == 0 else B["qm"]
                        k.op("pool", lambda src=src, ci=ci: nc.gpsimd.tensor_tensor(
                            out=tA[ci][:], in0=src[:], in1=src[:], op=ALU.mult), r=[bsrc], w=[B["tA"][ci]], n=768)
                        k.op("dve", lambda s0=s0, ci=ci: nc.vector.tensor_reduce(
                            out=st2[:, s0:s0 + 8], in_=tA[ci][:].rearrange("p (h d) -> p h d", h=8), axis=AX.X,
                            op=ALU.add), r=[B["tA"][ci]], w=[B["st2"]], n=768)
                    nr2 = 16 if full else 8
                    rstd_from(st2, rs2, 0, nr2, 96, B["st2"], B["rs2"])
                    k.op("act", lambda: nc.scalar.activation(out=rs2[:, 0:nr2], in_=rs2[:, 0:nr2], func=AF.Exp,
                                                             scale=-0.5),
                         r=[B["rs2"]], w=[B["rs2"]], n=16)
                    kb3 = kb[:].rearrange("p (h d) -> p h d", h=8)
                    qb3 = qb[:].rearrange("p (h d) -> p h d", h=8)
                    norm_rope(kc3, 8, 96, rs2[:, 0:8], G_KNM, 32, rtt[:, 0:16], rtt[:, 16:32], kb3,
                              [B["kcat"], B["rs2"]], B["kb"], 0, brt)
                    if full:
                        norm_rope(qm[:].rearrange("p (h d) -> p h d", h=8), 8, 96, rs2[:, 8:16], G_QNM, 32,
                                  rtt[:, 0:16], rtt[:, 16:32], qb3, [B["qm"], B["rs2"]], B["qb"], 1, brt)
                    kq_list = ((kb3, ktb, B["kb"], B["ktb"], KTm), (qb3, qtb, B["qb"], B["qtb"], QTm))
                    for pi_, (srcb, dstt, bs, bd, dram) in enumerate(kq_list if full else kq_list[:1]):
                        pbk = 4 + pi_
                        for h in range(8):
                            k.op("pe", lambda h=h, srcb=srcb, pbk=pbk: nc.tensor.transpose(
                                out=psbf(pbk)[0:96, h * 128:(h + 1) * 128], in_=srcb[:, h, :], identity=identb[:]),
                                 r=[bs, idB], w=[PSB[pbk]], n=128)
                        k.op("act", lambda dstt=dstt, pbk=pbk: nc.scalar.copy(out=dstt[0:96, :],
                                                                            in_=psbf(pbk)[0:96, 0:1024]),
                             r=[PSB[pbk]], w=[bd], n=1024)
                        k.dma("sp", dram[:, :, tok].rearrange("h d t -> d h t"),
                              dstt[0:96, :].rearrange("p (h t) -> p h t", h=8), r=[bd])
                    norm_rope(pj[:, 288:416].rearrange("p (h d) -> p h d", h=2), 2, 64, rs[:, 2:4], G_KNG, 64,
                              rtt[:, 32:64], rtt[:, 64:96], kgb[:].rearrange("p (h d) -> p h d", h=2),
                              [bpj, B["rs"]], B["kgb"], 0, brt)
                    k.op("act", lambda: nc.scalar.copy(out=vgb[:], in_=pj[:, 416:544]), r=[bpj],
                         w=[B["vgb"]], n=128)
                    k.dma("sp", Vg[tok, :, :].rearrange("t h d -> t (h d)"), vgb[:], r=[B["vgb"]])
                    k.op("pe", lambda: nc.tensor.transpose(out=psbf(6)[:, 0:128], in_=kgb[:], identity=identb[:]),
                         r=[B["kgb"], idB], w=[PSB[6]], n=128)
                    k.op("act", lambda: nc.scalar.copy(out=kgt[:], in_=psbf(6)[:, 0:128]), r=[PSB[6]],
                         w=[B["kgt"]], n=128)
                    k.dma("sp", KTg[:, :, tok].rearrange("j d t -> (j d) t"), kgt[:], r=[B["kgt"]])
                    if not full:
                        return
                    norm_rope(pj[:, 928:1440].rearrange("p (h d) -> p h d", h=8), 8, 64, rs[:, 4:12], G_QNG, 64,
                              rtt[:, 32:64], rtt[:, 64:96], qgb[:].rearrange("p (h d) -> p h d", h=8),
                              [bpj, B["rs"]], B["qgb"], 1, brt)
                    for c in range(4):
                        k.op("pe", lambda c=c: nc.tensor.transpose(out=psbf(6)[:, c * 128:(c + 1) * 128],
                                                                   in_=qgb[:, c * 128:(c + 1) * 128],
                                                                   identity=identb[:]),
                             r=[B["qgb"], idB], w=[PSB[6]], n=128)
                    k.op("act", lambda: nc.scalar.copy(out=qgt[:], in_=psbf(6)[:, 0:512]), r=[PSB[6]],
                         w=[B["qgt"]], n=512)
                    k.dma("sp", QTg[:, :, tok].rearrange("(c two) d t -> (two d) c t", two=2),
                          qgt[:].rearrange("p (c t) -> p c t", c=4), r=[B["qgt"]])

                B_all = B
                stageA(0)
                stageB(0)
                for i in range(NT):
                    if i + 1 < NT:
                        stageA(i + 1)
                        stageB(i + 1)
                    stageC1(i)
                    if i >= 1:
                        stageC2(i - 1)
                stageC2(NT - 1)
                k.barrier()

        def phase2(l, prefetch=None):
            with ExitStack() as es:
                E = es.enter_context
                GK = 3
                kt_sb = [E(sbt(f"p2_kt{i}", [96, NTOK], BF16)) for i in range(2)]
                qt_sb = [E(sbt(f"p2_qt{i}", [96, NTOK], BF16)) for i in range(2)]
                va_sb = [E(sbt(f"p2_va{i}", [128, NT, 128], BF16)) for i in range(2)]
                pT = [E(sbt(f"p2_pT{i}", [128, GK * 512], BF16)) for i in range(2)]
                rc = E(sbt("p2_rc", [128, 512], F32))
                ot = [E(sbt(f"p2_ot{i}", [128, NTOK], BF16)) for i in range(2)]
                bkt, bqt, bva = [Buf(), Buf()], [Buf(), Buf()], [Buf(), Buf()]
                bpT = [Buf(), Buf()]
                brc = Buf()
                bot = [Buf(), Buf()]
                bSG = [Buf(), Buf()]
                for i in range(2):
                    k.op("pool", lambda i=i: nc.gpsimd.memset(va_sb[i][:, :, 64:128], 1.0), w=[bva[i]])
                if l < depth - 1:
                    qblocks = [(CTX + 512 * j, 512, NT) for j in range(8)]
                    qblocks.append((0, 256, 2))
                    ot_ranges = [(0, NTOK)]
                else:
                    qblocks = [(CTX + 448 * j, 448, NT) for j in range(4)] + [(CTX + 1792, 384, NT)]
                    ot_ranges = [(CTX, 2432)]
                scale_m = 96 ** -0.5
                scale_g = 64 ** -0.5
                state = {"acc": 0, "g": 0}

                def attend(KT, dq, QT, VA, scale, otile, prow, deps, bo):
                    for (q0_, qn_, nk_) in qblocks:
                        attend_block(KT, dq, QT, VA, scale, otile, prow, deps, bo, q0_, qn_, nk_)

                def attend_block(KT, dq, QT, VA, scale, otile, prow, deps, bo, q0, qn, nk):
                    if True:
                        ob = 6 + state["acc"] % 2
                        state["acc"] += 1
                        groups = [list(range(a, min(a + GK, nk))) for a in range(0, nk, GK)]
                        gslot = []

                        def s_grp(gi):
                            sl = state["g"] % 2
                            state["g"] += 1
                            gslot.append(sl)
                            for ii, kt in enumerate(groups[gi]):
                                c0 = sl * GK * 512 + ii * 512
                                k.op("pe", lambda kt=kt, c0=c0: nc.tensor.matmul(
                                    PSALL[:, c0:c0 + qn], lhsT=KT[0:dq, kt * 128:(kt + 1) * 128],
                                    rhs=QT[0:dq, q0:q0 + qn], start=True, stop=True), r=deps, w=[bSG[sl]], n=qn)

                        s_grp(0)
                        for gi, grp in enumerate(groups):
                            if gi + 1 < len(groups):
                                s_grp(gi + 1)
                            sl = gslot[gi]
                            ng = len(grp)
                            src = PSALL[:, sl * GK * 512:sl * GK * 512 + ng * 512].rearrange(
                                "p (g q) -> p g q", q=512)[:, :, 0:qn]
                            dst = pT[sl][:, 0:ng * 512].rearrange("p (g q) -> p g q", q=512)[:, :, 0:qn]
                            k.op("act", lambda src=src, dst=dst: nc.scalar.activation(
                                out=dst, in_=src, func=AF.Exp, scale=scale), r=[bSG[sl]], w=[bpT[sl]], n=ng * qn)
                            for ii, kt in enumerate(grp):
                                k.op("pe", lambda kt=kt, ii=ii, sl=sl: nc.tensor.matmul(
                                    PS[ob][:, 0:qn], lhsT=VA[:, kt, :], rhs=pT[sl][:, ii * 512:ii * 512 + qn],
                                    start=(kt == 0), stop=(kt == nk - 1)), r=deps + [bpT[sl]], w=[PSB[ob]], n=qn)
                        k.op("dve", lambda: nc.vector.reciprocal(out=rc[64:128, 0:qn], in_=PS[ob][64:128, 0:qn]),
                             r=[PSB[ob]], w=[brc])
                        k.op("dve", lambda: nc.vector.tensor_tensor(
                            out=otile[prow:prow + 64, q0:q0 + qn], in0=PS[ob][0:64, 0:qn], in1=rc[64:128, 0:qn],
                            op=ALU.mult), r=[PSB[ob], brc], w=[bo])

                nload = 0
                for h in range(8):
                    j = nload % 2
                    nload += 1
                    k.dma("sp", kt_sb[j][0:96, :], KTm[h], w=[bkt[j]])
                    k.dma("sp", qt_sb[j][0:96, :], QTm[h], w=[bqt[j]])
                    vmv = Vm[:, h, :].rearrange("(t p) d -> p t d", p=128)
                    for t0 in range(0, NT, 12):
                        t1_ = min(NT, t0 + 12)
                        k.dma("sp", va_sb[j][:, t0:t1_, 0:64], vmv[:, t0:t1_, :], w=[bva[j]])
                    c = h // 2
                    oj = c % 2
                    attend(kt_sb[j], 96, qt_sb[j], va_sb[j], scale_m, ot[oj], (h % 2) * 64,
                           [bkt[j], bqt[j], bva[j]], bot[oj])
                    if h % 2 == 1:
                        for (qa, qb_) in ot_ranges:
                            k.dma("sp", OTm[c, :, qa:qb_], ot[oj][:, qa:qb_], r=[bot[oj]])
                    if h == 1 and prefetch is not None:
                        prefetch(bot[0])
                for jkv in range(2):
                    j = nload % 2
                    nload += 1
                    k.dma("sp", kt_sb[j][0:64, :], KTg[jkv], w=[bkt[j]])
                    vgv = Vg[:, jkv, :].rearrange("(t p) d -> p t d", p=128)
                    for t0 in range(0, NT, 12):
                        t1_ = min(NT, t0 + 12)
                        k.dma("sp", va_sb[j][:, t0:t1_, 0:64], vgv[:, t0:t1_, :], w=[bva[j]])
                    for i in range(4):
                        h = jkv * 4 + i
                        jq = h % 2
                        k.dma("sp", qt_sb[jq][0:64, :], QTg[h], w=[bqt[jq]])
                        c = h // 2
                        oj = c % 2
                        attend(kt_sb[j], 64, qt_sb[jq], va_sb[j], scale_g, ot[oj], (h % 2) * 64,
                               [bkt[j], bqt[jq], bva[j]], bot[oj])
                        if h % 2 == 1:
                            for (qa, qb_) in ot_ranges:
                                k.dma("sp", OTg[c, :, qa:qb_], ot[oj][:, qa:qb_], r=[bot[oj]])
                k.barrier()

        def phase3(l, ab_pre=None):
            with ExitStack() as es:
                E = es.enter_context
                ab = E(sbt("p3_ab", [128, NT, 1024], BF16)) if ab_pre is None else ab_pre
                cb = [E(sbt(f"p3_c{i}", [128, 32, 256], BF16)) for i in range(2)]
                sbb = [E(sbt(f"p3_s{i}", [128, 32, 256], BF16)) for i in range(2)]
                of = [E(sbt(f"p3_o{i}", [128, 4, 256], BF16)) for i in range(2)]
                bab = [Buf() for _ in range(NT // 2)]
                bcb = [[Buf() for _ in range(4)] for _ in range(2)]
                bsb = [[Buf() for _ in range(4)] for _ in range(2)]
                bof = [Buf(), Buf()]
                abv = ABd.rearrange("(t p) n -> p t n", p=128)
                if ab_pre is None:
                    for t0 in range(0, NT, 2):
                        k.dma("sp", ab[:, t0:t0 + 2, :], abv[:, t0:t0 + 2, :], w=[bab[t0 // 2]])
                dc = dftc.rearrange("(t p) l -> p t l", p=128)
                ds = dfts.rearrange("(t p) l -> p t l", p=128)
                jbs = list(range(16)) if l < depth - 1 else list(range(9))
                jobs = [(dc, ds, 2, 32, jb * 256, CTX + jb * 256) for jb in jbs]
                if l < depth - 1:
                    jobs.append((dftc_c.rearrange("(t p) l -> p t l", p=128),
                                 dfts_c.rearrange("(t p) l -> p t l", p=128), 0, 2, 0, 0))
                nacc_ = [0]

                def job(n, mc, ms, t_off, ntl, c0, tok0):
                    nacc = nacc_[0]
                    j = n % 2
                    for ta_ in range(0, ntl, 8):
                        tb_ = min(ntl, ta_ + 8)
                        k.dma("sp", cb[j][:, ta_:tb_, :], mc[:, ta_:tb_, c0:c0 + 256], w=[bcb[j][ta_ // 8]])
                        k.dma("sp", sbb[j][:, ta_:tb_, :], ms[:, ta_:tb_, c0:c0 + 256], w=[bsb[j][ta_ // 8]])
                    for g in range(4):
                        pb = nacc % 2
                        nacc += 1
                        for t in range(ntl):
                            k.op("pe", lambda g=g, t=t, pb=pb: nc.tensor.matmul(
                                PS[pb][:, 0:256], lhsT=ab[:, t_off + t, g * 128:(g + 1) * 128], rhs=cb[j][:, t, :],
                                start=(t == 0), stop=False), r=[bab[(t_off + t) // 2], bcb[j][t // 8]],
                                 w=[PSB[pb]], n=256)
                            k.op("pe", lambda g=g, t=t, pb=pb: nc.tensor.matmul(
                                PS[pb][:, 0:256], lhsT=ab[:, t_off + t, 512 + g * 128:512 + (g + 1) * 128],
                                rhs=sbb[j][:, t, :], start=False, stop=(t == ntl - 1)),
                                 r=[bab[(t_off + t) // 2], bsb[j][t // 8]], w=[PSB[pb]], n=256)
                        k.op("act", lambda g=g, pb=pb: nc.scalar.copy(out=of[j][:, g, :], in_=PS[pb][:, 0:256]),
                             r=[PSB[pb]], w=[bof[j]], n=256)
                    k.dma("sp", OTf[:, :, tok0:tok0 + 256].rearrange("g p t -> p g t"), of[j][:], r=[bof[j]])
                    nacc_[0] = nacc

                for n, jb_ in enumerate(jobs):
                    job(n, *jb_)
                k.barrier()

        def phase4(l, xcur, w_pre=None):
            with ExitStack() as es:
                E = es.enter_context

                def sb(name, shape, dt=F32):
                    return E(sbt("p4_" + name, list(shape), dt))

                if w_pre is None:
                    wbr = [sb(f"wbr{i}", [128, 4, D], BF16) for i in range(3)]
                    wo = sb("wo", [128, 8, D], BF16)
                else:
                    wbr, wo = w_pre
                gt1 = [sb("gt1_0", [128, D]), sb("gt1_1", [128, D])]
                gm2 = [sb("gm2_0", [128, D]), sb("gm2_1", [128, D])]
                sh2 = [sb("sh2_0", [128, D]), sb("sh2_1", [128, D])]
                bwbr = [Buf() for _ in range(3)]
                bwo, = [Buf()]
                bgt1, bgm2, bsh2 = [Buf(), Buf()], [Buf(), Buf()], [Buf(), Buf()]
                if w_pre is None:
                    for i in range(3):
                        k.dma("pool", wbr[i][:], w_br[i][l].rearrange("(k p) n -> p k n", p=128), w=[bwbr[i]])
                    k.dma("pool", wo[:], w_o[l].rearrange("(k p) n -> p k n", p=128), w=[bwo])
                for j in range(2):
                    k.dma("sp", gt1[j][:], modv[l, j, 2 * D:3 * D].partition_broadcast(128), w=[bgt1[j]])
                    k.dma("sp", sh2[j][:], modv[l, j, 3 * D:4 * D].partition_broadcast(128), w=[bsh2[j]])
                    k.dma("sp", gm2[j][:], modv[l, j, 4 * D:5 * D].partition_broadcast(128), w=[bgm2[j]])
                names = ["ot0", "ot1", "ot2", "gts", "xt", "ya", "yt", "yb", "yT", "xm", "tmp", "st", "rs", "hb", "hT"]
                shapes = {"ot0": ([128, 4, 128], BF16), "ot1": ([128, 4, 128], BF16), "ot2": ([128, 4, 128], BF16),
                          "gts": ([128, 3072], BF16), "xt": ([128, D], F32), "ya": ([128, D], F32),
                          "yt": ([128, 512], F32), "yb": ([128, D], BF16), "yT": ([128, D], BF16),
                          "xm": ([128, D], F32), "tmp": ([128, D], F32), "st": ([128, 2], F32),
                          "rs": ([128, 2], F32), "hb": ([128, D], BF16), "hT": ([128, D], BF16)}
                NSET = 3
                TS = [{n_: sb(f"{n_}_{s_}", *shapes[n_]) for n_ in names} for s_ in range(NSET)]
                BS = [{n_: Buf(n_) for n_ in names} for s_ in range(NSET)]
                OTs = [OTm, OTg, OTf]
                first = 0 if l < depth - 1 else 2
                rr = [0]

                def nb():
                    b_ = rr[0]
                    rr[0] = (b_ + 1) % 8
                    return b_

                def tile(i, n_):
                    T = TS[n_ % NSET]
                    B = BS[n_ % NSET]
                    tok = slice(i * 128, (i + 1) * 128)
                    ic = 1 if i < 2 else 0
                    otin = [T["ot0"], T["ot1"], T["ot2"]]
                    gts, xt, ya, yt, yb, yT, xm, tmp, st, rs, hb, hT = (T[n_] for n_ in names[3:])
                    for bi in range(3):
                        k.dma("sp", otin[bi][:], OTs[bi][:, :, tok].rearrange("c p t -> p c t"), w=[B[f"ot{bi}"]])
                    k.dma("sp", gts[:], Gd[tok, :], w=[B["gts"]])
                    k.dma("sp", xt[:], xcur[tok, :], w=[B["xt"]])
                    for half in range(2):
                        cs = slice(half * 512, (half + 1) * 512)
                        pbs = [nb(), nb(), nb()]
                        for bi in range(3):
                            for c in range(4):
                                k.op("pe", lambda bi=bi, c=c, cs=cs, pbs=pbs: nc.tensor.matmul(
                                    PS[pbs[bi]][:, 0:512], lhsT=otin[bi][:, c, :], rhs=wbr[bi][:, c, cs],
                                    start=(c == 0), stop=(c == 3)), r=[B[f"ot{bi}"], bwbr[bi]], w=[PSB[pbs[bi]]])
                        k.op("dve", lambda cs=cs, half=half, pbs=pbs: nc.vector.tensor_tensor(
                            out=ya[:, cs], in0=PS[pbs[0]][:, 0:512], in1=gts[:, half * 512:(half + 1) * 512],
                            op=ALU.mult), r=[PSB[pbs[0]], B["gts"]], w=[B["ya"]])
                        for bi in (1, 2):
                            k.op("dve", lambda bi=bi, half=half, pbs=pbs: nc.vector.tensor_tensor(
                                out=yt[:], in0=PS[pbs[bi]][:, 0:512],
                                in1=gts[:, bi * D + half * 512:bi * D + (half + 1) * 512], op=ALU.mult),
                                 r=[PSB[pbs[bi]], B["gts"]], w=[B["yt"]])
                            k.op("dve", lambda cs=cs: nc.vector.tensor_tensor(out=ya[:, cs], in0=ya[:, cs],
                                                                              in1=yt[:], op=ALU.add),
                                 r=[B["ya"], B["yt"]], w=[B["ya"]])
                    pt = nb()
                    k.op("act", lambda: nc.scalar.copy(out=yb[:], in_=ya[:]), r=[B["ya"]], w=[B["yb"]], n=1024)
                    for kk in range(8):
                        k.op("pe", lambda kk=kk: nc.tensor.transpose(out=psbf(pt)[:, kk * 128:(kk + 1) * 128],
                                                                     in_=yb[:, kk * 128:(kk + 1) * 128],
                                                                     identity=identb[:]),
                             r=[B["yb"], idB], w=[PSB[pt]], n=128)
                    k.op("act", lambda: nc.scalar.copy(out=yT[:], in_=psbf(pt)[:, 0:1024]), r=[PSB[pt]],
                         w=[B["yT"]], n=1024)
                    for half in range(2):
                        cs = slice(half * 512, (half + 1) * 512)
                        pw = nb()
                        for kk in range(8):
                            k.op("pe", lambda kk=kk, cs=cs, pw=pw: nc.tensor.matmul(
                                PS[pw][:, 0:512], lhsT=yT[:, kk * 128:(kk + 1) * 128], rhs=wo[:, kk, cs],
                                start=(kk == 0), stop=(kk == 7)), r=[B["yT"], bwo], w=[PSB[pw]])
                        k.op("dve", lambda cs=cs, pw=pw: nc.vector.tensor_tensor(
                            out=tmp[:, cs], in0=PS[pw][:, 0:512], in1=gt1[ic][:, cs], op=ALU.mult),
                             r=[PSB[pw], bgt1[ic]], w=[B["tmp"]])
                    k.op("dve", lambda: nc.vector.tensor_tensor(out=xm[:], in0=tmp[:], in1=xt[:], op=ALU.add),
                         r=[B["tmp"], B["xt"]], w=[B["xm"]], n=1024)
                    k.dma("sp", xmid[tok, :], xm[:], r=[B["xm"]])
                    k.op("act", lambda: nc.scalar.activation(out=tmp[:], in_=xm[:], func=AF.Square),
                         r=[B["xm"]], w=[B["tmp"]], n=1024)
                    k.op("dve", lambda: nc.vector.tensor_reduce(out=st[:, 0:1], in_=tmp[:], axis=AX.X, op=ALU.add),
                         r=[B["tmp"]], w=[B["st"]], n=1024)
                    k.op("act", lambda: nc.scalar.activation(out=rs[:, 0:1], in_=st[:, 0:1], func=AF.Ln,
                                                             scale=1.0 / D, bias=EPS), r=[B["st"]], w=[B["rs"]],
                         n=16)
                    k.op("act", lambda: nc.scalar.activation(out=rs[:, 0:1], in_=rs[:, 0:1], func=AF.Exp,
                                                             scale=-0.5), r=[B["rs"]], w=[B["rs"]], n=16)
                    k.op("dve", lambda: nc.vector.scalar_tensor_tensor(out=tmp[:], in0=xm[:],
                                                                       scalar=rs[:, 0:1], in1=gm2[ic][:],
                                                                       op0=ALU.mult, op1=ALU.mult),
                         r=[B["xm"], B["rs"], bgm2[ic]], w=[B["tmp"]], n=1024)
                    k.op("pool", lambda: nc.gpsimd.tensor_tensor(out=hb[:], in0=tmp[:], in1=sh2[ic][:],
                                                                 op=ALU.add),
                         r=[B["tmp"], bsh2[ic]], w=[B["hb"]], n=1024)
                    pt2 = nb()
                    for kk in range(8):
                        k.op("pe", lambda kk=kk: nc.tensor.transpose(out=psbf(pt2)[:, kk * 128:(kk + 1) * 128],
                                                                     in_=hb[:, kk * 128:(kk + 1) * 128],
                                                                     identity=identb[:]),
                             r=[B["hb"], idB], w=[PSB[pt2]], n=128)
                    k.op("act", lambda: nc.scalar.copy(out=hT[:], in_=psbf(pt2)[:, 0:1024]), r=[PSB[pt2]],
                         w=[B["hT"]], n=1024)
                    k.dma("sp", H2T[:, :, tok].rearrange("k p t -> p k t"),
                          hT[:].rearrange("p (k t) -> p k t", k=8), r=[B["hT"]])

                tiles = list(range(first, NT)) if l < depth - 1 else list(range(2, 19))
                for n_, i in enumerate(tiles):
                    tile(i, n_)
                k.barrier()

        def phase5(l, xdst, last):
            with ExitStack() as es:
                E = es.enter_context

                def sb(name, shape, dt=F32):
                    return E(sbt("p5_" + name, list(shape), dt))

                wup = sb("wup", [128, 8, 2 * DFF], BF16)
                wdn = sb("wdn", [128, 22, D], BF16)
                cw = sb("cw", [128, 44, 3])
                cbi = sb("cb", [128, 44])
                gt2 = [sb("gt2_0", [128, D]), sb("gt2_1", [128, D])]
                cgrp = [(0, 6), (6, 12), (12, 17), (17, 22)]
                bwup = [[Buf() for _ in cgrp] for _ in range(2)]
                cg_of = {}
                for gi_, (ca, cb2) in enumerate(cgrp):
                    for c_ in range(ca, cb2):
                        cg_of[c_] = gi_
                bwdn = [Buf() for _ in range(11)]
                bcw, bcb_ = Buf(), Buf()
                bgt2 = [Buf(), Buf()]
                wu = w_up[l].rearrange("(k p) n -> p k n", p=128)
                for gi_, (ca, cb2) in enumerate(cgrp):
                    for part_ in range(2):
                        c0_ = (ca + 22 * part_) * 128
                        c1_ = (cb2 + 22 * part_) * 128
                        k.dma("pool", wup[:, :, c0_:c1_], wu[:, :, c0_:c1_], w=[bwup[part_][gi_]])
                wd_ = w_down[l].rearrange("(k p) n -> p k n", p=128)
                for c in range(0, 22, 2):
                    k.dma("pool", wdn[:, c:c + 2, :], wd_[:, c:c + 2, :], w=[bwdn[c // 2]])
                k.dma("sp", cw[:], convw[l], w=[bcw])
                k.dma("sp", cbi[:], convb[l], w=[bcb_])
                for j in range(2):
                    k.dma("sp", gt2[j][:], modv[l, j, 5 * D:6 * D].partition_broadcast(128), w=[bgt2[j]])
                FB = 256
                NU = 4
                h2 = [sb(f"h2_{i}", [128, 8, FB + 2], BF16) for i in range(2)]
                us = [sb(f"us{i}", [128, FB + 2]) for i in range(NU)]
                ta = [sb(f"ta{i}", [128, FB]) for i in range(2)]
                tv = [sb(f"tv{i}", [128, FB]) for i in range(2)]
                sa = [sb(f"sa{i}", [128, FB]) for i in range(2)]
                gT = [sb(f"gT{i}", [128, 22, FB], BF16) for i in range(2)]
                xm = [sb(f"xm{i}", [128, D]) for i in range(2)]
                xo = [sb(f"xo{i}", [128, D]) for i in range(2)]
                tmp = [sb(f"tmp{i}", [128, D]) for i in range(2)]
                bh2 = [Buf(), Buf()]
                bus = [Buf() for _ in range(NU)]
                bta, btv, bsa = [Buf(), Buf()], [Buf(), Buf()], [Buf(), Buf()]
                bgT = [[Buf() for _ in range(22)] for _ in range(2)]
                btmp = [Buf(), Buf()]
                bxm, bxo = [Buf(), Buf()], [Buf(), Buf()]
                cm = sb("cm", [128, 2])
                bcm = Buf()
                k.dma("sp", cm[:], cmask, w=[bcm])
                nblk = SEQ // FB
                blocks = []
                for j in range(nblk if not last else nblk // 2):
                    left, right = "c", "c"
                    if j == 0:
                        left = "z"
                    if j == nblk - 1:
                        right = "z"
                    blocks.append((CTX + FB * j, FB, left, right))
                if not last:
                    blocks.append((0, 256, "z", "z"))
                h2v = H2T.rearrange("k p t -> p k t")
                cnt = {"u": 0, "x": 0}

                def unit(j, tn, c, part, gs):
                    ch = c + 22 * part
                    ui = cnt["u"]
                    cnt["u"] += 1
                    pb = ui % 4
                    u = us[ui % NU]
                    bu = bus[ui % NU]
                    cs_ = c % 2
                    tdst, btd = ((ta[cs_], bta[cs_]), (tv[cs_], btv[cs_]))[part]
                    for kk in range(8):
                        k.op("pe", lambda kk=kk: nc.tensor.matmul(
                            PS[pb][:, 0:tn + 2], lhsT=wup[:, kk, ch * 128:(ch + 1) * 128],
                            rhs=h2[j][:, kk, 0:tn + 2], start=(kk == 0), stop=(kk == 7)),
                             r=[bwup[part][cg_of[c]], bh2[j]], w=[PSB[pb]], n=tn + 2)
                    k.op("act", lambda: nc.scalar.activation(
                        out=tdst[:, 0:tn], in_=PS[pb][:, 1:tn + 1], func=AF.Identity, scale=cw[:, ch, 1:2],
                        bias=cbi[:, ch:ch + 1]), r=[PSB[pb], bcw, bcb_], w=[btd], n=tn)
                    k.op("dve", lambda: nc.vector.scalar_tensor_tensor(
                        out=tdst[:, 0:tn], in0=PS[pb][:, 0:tn], scalar=cw[:, ch, 0:1], in1=tdst[:, 0:tn],
                        op0=ALU.mult, op1=ALU.add), r=[PSB[pb], bcw, btd], w=[btd], n=tn)
                    k.op("dve", lambda: nc.vector.scalar_tensor_tensor(
                        out=tdst[:, 0:tn], in0=PS[pb][:, 2:tn + 2], scalar=cw[:, ch, 2:3], in1=tdst[:, 0:tn],
                        op0=ALU.mult, op1=ALU.add), r=[PSB[pb], bcw, btd], w=[btd], n=tn)
                    if part == 1:
                        k.op("act", lambda: nc.scalar.activation(out=sa[cs_][:, 0:tn], in_=ta[cs_][:, 0:tn],
                                                                 func=AF.Silu), r=[bta[cs_]], w=[bsa[cs_]], n=tn)
                        k.op("pool", lambda: nc.gpsimd.tensor_tensor(out=gT[gs][:, c, 0:tn], in0=sa[cs_][:, 0:tn],
                                                                     in1=tv[cs_][:, 0:tn], op=ALU.mult),
                             r=[bsa[cs_], btv[cs_]], w=[bgT[gs][c]], n=tn)

                def down(gs, tn, t0, tt, ic):
                    jx = cnt["x"] % 2
                    cnt["x"] += 1
                    tok = slice(t0 + tt * 128, t0 + (tt + 1) * 128)
                    k.dma("sp", xm[jx][:], xmid[tok, :], w=[bxm[jx]])
                    for half in range(2):
                        cs = slice(half * 512, (half + 1) * 512)
                        for c in range(22):
                            k.op("pe", lambda c=c, cs=cs, half=half: nc.tensor.matmul(
                                PS[4 + half][:, 0:512], lhsT=gT[gs][:, c, tt * 128:(tt + 1) * 128],
                                rhs=wdn[:, c, cs], start=(c == 0), stop=(c == 21)),
                                 r=[bgT[gs][c], bwdn[c // 2]], w=[PSB[4 + half]])
                        k.op("dve", lambda half=half, cs=cs: nc.vector.tensor_tensor(
                            out=tmp[jx][:, cs], in0=PS[4 + half][:, 0:512], in1=gt2[ic][:, cs], op=ALU.mult),
                             r=[PSB[4 + half], bgt2[ic]], w=[btmp[jx]])
                    k.op("pool", lambda: nc.gpsimd.tensor_tensor(out=xo[jx][:], in0=tmp[jx][:], in1=xm[jx][:],
                                                                 op=ALU.add),
                         r=[btmp[jx], bxm[jx]], w=[bxo[jx]], n=1024)
                    if last:
                        dst = xdst[t0 - CTX + tt * 128:t0 - CTX + (tt + 1) * 128, :]
                    else:
                        dst = xdst[tok, :]
                    k.dma("sp", dst, xo[jx][:], r=[bxo[jx]])

                def block(n, t0, tn, left, right):
                    j = n % 2
                    gs = n % 2
                    ic = 1 if t0 < CTX else 0
                    lo = t0 - 1 if left == "c" else t0
                    hi = t0 + tn + 1 if right == "c" else t0 + tn
                    k.dma("sp", h2[j][:, :, lo - (t0 - 1):hi - (t0 - 1)], h2v[:, :, lo:hi], w=[bh2[j]])
                    for spec, col in ((left, 0), (right, tn + 1)):
                        if spec == "c":
                            continue
                        if spec == "z":
                            k.op("pool", lambda col=col: nc.gpsimd.memset(h2[j][:, :, col:col + 1], 0.0),
                                 w=[bh2[j]], n=8)
                            continue
                        src_tok, mi = spec
                        k.dma("sp", h2[j][:, :, col:col + 1], h2v[:, :, src_tok:src_tok + 1], w=[bh2[j]],
                              slow=True)
                        k.op("dve", lambda col=col, mi=mi: nc.vector.tensor_scalar(
                            out=h2[j][:, :, col:col + 1], in0=h2[j][:, :, col:col + 1], scalar1=cm[:, mi:mi + 1],
                            scalar2=None, op0=ALU.mult), r=[bh2[j], bcm], w=[bh2[j]], n=8)
                    for c in range(22):
                        for part in range(2):
                            unit(j, tn, c, part, gs)
                    for tt in range(tn // 128):
                        down(gs, tn, t0, tt, ic)

                for n, bl_ in enumerate(blocks):
                    block(n, *bl_)
                k.barrier()

        xcur = xin
        for l in range(depth):
            last = (l == depth - 1)
            phase0(l)
            if check_stop(f"p0_{l}"):
                break
            phase1(l, xcur)
            if check_stop(f"p1_{l}"):
                break
            with ExitStack() as esw:
                wbr_p = [esw.enter_context(sbt(f"pre_wbr{i}", [128, 4, D], BF16)) for i in range(3)]
                wo_p = esw.enter_context(sbt("pre_wo", [128, 8, D], BF16))
                with ExitStack() as esa:
                    ab_p = esa.enter_context(sbt("pre_ab", [128, NT, 1024], BF16))

                    def prefetch(bdep, l=l):
                        abv_ = ABd.rearrange("(t p) n -> p t n", p=128)
                        for t0 in range(0, NT, 2):
                            k.dma("sp", ab_p[:, t0:t0 + 2, :], abv_[:, t0:t0 + 2, :], r=[bdep], w=[Buf()])
                        for i in range(3):
                            k.dma("pool", wbr_p[i][:], w_br[i][l].rearrange("(k p) n -> p k n", p=128),
                                  r=[bdep], w=[Buf()])
                        k.dma("pool", wo_p[:], w_o[l].rearrange("(k p) n -> p k n", p=128), r=[bdep], w=[Buf()])

                    phase2(l, prefetch)
                    if check_stop(f"p2_{l}"):
                        break
                    phase3(l, ab_p)
                    if check_stop(f"p3_{l}"):
                        break
                phase4(l, xcur, (wbr_p, wo_p))
                if check_stop(f"p4_{l}"):
                    break
            phase5(l, y_out if last else x1, last)
            if check_stop(f"p5_{l}"):
                break
            xcur = x1
        k.barrier()
    return nc, k


def _consts():
    GRID_W = 64
    rows = SEQ // GRID_W
    row = np.repeat(np.arange(rows, dtype=np.float32), GRID_W)
    col = np.tile(np.arange(GRID_W, dtype=np.float32), rows)

    def tab(rot):
        n_f = rot // 4
        inv = (np.float32(10000.0) ** (-np.arange(n_f, dtype=np.float32) / np.float32(n_f))).astype(np.float32)
        ang = np.concatenate([row[:, None] * inv, col[:, None] * inv], axis=-1).astype(np.float32)
        return np.cos(ang).astype(np.float32), np.sin(ang).astype(np.float32)

    cm, sm = tab(32)
    cg, sg = tab(64)
    rope = np.zeros((NTOK, 96), np.float32)
    rope[:CTX, 0:16] = 1.0
    rope[:CTX, 32:64] = 1.0
    rope[CTX:, 0:16] = cm
    rope[CTX:, 16:32] = sm
    rope[CTX:, 32:64] = cg
    rope[CTX:, 64:96] = sg

    def dft(n, scale):
        idx = (np.outer(np.arange(n, dtype=np.int64), np.arange(n, dtype=np.int64)) % n).astype(np.float64)
        ang = 2.0 * np.pi * idx / n
        return np.cos(ang) * scale, np.sin(ang) * scale

    c128, s128 = dft(128, 128 ** -0.5)
    ccsc = np.concatenate([c128, s128], axis=1).astype(np.float32)
    cL, sL = dft(SEQ, 1.0 / 64)
    cC, sC = dft(CTX, 1.0 / 16)
    bf = ml_dtypes.bfloat16
    return dict(rope=rope, ccsc=ccsc, dftc=cL.astype(np.float32).astype(bf), dfts=(-sL).astype(np.float32).astype(bf),
                dftc_c=cC.astype(np.float32).astype(bf), dfts_c=(-sC).astype(np.float32).astype(bf))


_CONSTS = None


def make_in_maps(inputs, depth=DEPTH, n_cores=8, parity=0):
    global _CONSTS
    if _CONSTS is None:
        _CONSTS = _consts()
    f = lambda a: np.ascontiguousarray(np.asarray(a, dtype=np.float32))
    x, c, ctx, c_ctx = f(inputs["x"]), f(inputs["c"]), f(inputs["ctx"]), f(inputs["c_ctx"])
    shared = {
        "w_mod": f(inputs["w_mod"])[:depth], "b_mod": f(inputs["b_mod"])[:depth],
        "g_norm1": f(inputs["g_norm1"])[:depth], "g_norm2": f(inputs["g_norm2"])[:depth],
        "w_in": f(inputs["w_in"])[:depth],
        "gvec": np.ascontiguousarray(np.concatenate(
            [f(inputs[n]) for n in ("g_ckv", "g_cq", "g_kn_mla", "g_qn_mla", "g_kn_gqa", "g_qn_gqa")],
            axis=1)[:depth]),
        "w_uq": f(inputs["w_uq"])[:depth], "w_ukv": f(inputs["w_ukv"])[:depth],
        "w_br_mla": f(inputs["w_br_mla"])[:depth], "w_br_gqa": f(inputs["w_br_gqa"])[:depth],
        "w_four": f(inputs["w_four"])[:depth], "w_o": f(inputs["w_o"])[:depth],
        "w_up": f(inputs["w_up"])[:depth],
        "convw": np.ascontiguousarray(
            f(inputs["conv_w"])[:depth].reshape(depth, 3, 44, 128).transpose(0, 3, 2, 1)),
        "convb": np.ascontiguousarray(f(inputs["conv_b"])[:depth].reshape(depth, 44, 128).transpose(0, 2, 1)),
        "w_down": f(inputs["w_down"])[:depth],
    }
    shared.update(_CONSTS)
    in_maps = []
    rev = np.arange(SEQ - 1, -1, -1)
    revc = np.arange(CTX - 1, -1, -1)
    odd_consts = None
    for core in range(n_cores):
        par = (core % 2) if n_cores == 8 else parity
        b = (core // 2 if n_cores == 8 else core) % x.shape[0]
        m = dict(shared)
        cT = np.stack([c[b].reshape(8, 128).T, c_ctx.reshape(8, 128).T], axis=-1)
        m["cT"] = np.ascontiguousarray(cT.astype(np.float32))
        m["cmask"] = np.zeros((128, 2), np.float32)
        if par == 0:
            m["xin"] = np.ascontiguousarray(np.concatenate([ctx[b], x[b]], axis=0))
        else:
            if odd_consts is None:
                rp = _CONSTS["rope"]
                odd_consts = {
                    "rope": np.ascontiguousarray(np.concatenate([rp[:CTX][revc], rp[CTX:][rev]], axis=0)),
                    "dftc": np.ascontiguousarray(_CONSTS["dftc"][rev][:, rev]),
                    "dfts": np.ascontiguousarray(_CONSTS["dfts"][rev][:, rev]),
                    "dftc_c": np.ascontiguousarray(_CONSTS["dftc_c"][revc][:, revc]),
                    "dfts_c": np.ascontiguousarray(_CONSTS["dfts_c"][revc][:, revc]),
                    "convw": np.ascontiguousarray(shared["convw"][..., ::-1]),
                }
            m.update(odd_consts)
            m["xin"] = np.ascontiguousarray(np.concatenate([ctx[b][revc], x[b][rev]], axis=0))
        in_maps.append(m)
    return in_maps


def kernel(**inputs):
    nc, _ = build()
    in_maps = make_in_maps(inputs)
    res = run_bass_kernel_spmd(nc, in_maps, core_ids=list(range(8)))
    nb = np.asarray(inputs["x"]).shape[0]
    out = np.stack([np.concatenate([np.asarray(res.results[2 * b]["y"], dtype=np.float32),
                                    np.asarray(res.results[2 * b + 1]["y"], dtype=np.float32)[::-1]], axis=0)
                    for b in range(nb)], axis=0)
    return out
```

```python
import numpy as np
import ml_dtypes
from contextlib import ExitStack
import concourse.bass as bass
import concourse.mybir as mybir
from concourse.bass_utils import run_bass_kernel_spmd

F32 = mybir.dt.float32
BF16 = mybir.dt.bfloat16
ALU = mybir.AluOpType
AF = mybir.ActivationFunctionType
AX = mybir.AxisListType

D = 1024
NTOK = 4352
CTX = 256
SEQ = 4096
NT = NTOK // 128
DEPTH = 2
DFF = 2816
EPS = 1e-6
WCOLS = 1440 + 1024 + 3072


class Buf:
    __slots__ = ("wi", "ri", "name")

    def __init__(self, name=""):
        self.wi = set()
        self.ri = set()
        self.name = name


class K:
    NDMA = 24
    EPOCH = 60000
    WINDOW = 96
    LAT = 250.0

    def __init__(self, nc):
        self.nc = nc
        self.es = ExitStack()
        self.engs = {"pe": nc.tensor, "act": nc.scalar, "dve": nc.vector, "pool": nc.gpsimd, "sp": nc.sync}
        self.sems = {}
        self.val = {}
        self.waited = {e: {} for e in self.engs}
        self.epoch = {e: 0 for e in ("pe", "act", "dve", "pool")}
        self.cur = {}
        for e in ("pe", "act", "dve", "pool"):
            self._new_epoch(e)
        self.dma_names = []
        for i in range(self.NDMA):
            n = f"dma{i}"
            self._newsem(n)
            self.dma_names.append(n)
        self.dma_rr = 0
        self.ninst = 0
        self.recs = []
        self.touched = set()
        self.sim_log = []

    def _newsem(self, name):
        self.sems[name] = self.es.enter_context(self.nc.semaphore(name))
        self.val[name] = 0

    def _new_epoch(self, e):
        n = f"{e}_{self.epoch[e]}"
        self.epoch[e] += 1
        self._newsem(n)
        self.cur[e] = n

    def wait(self, eng, name, v):
        if v <= 0:
            return
        if self.waited[eng].get(name, 0) >= v:
            return
        self.engs[eng].wait_ge(self.sems[name], v)
        self.waited[eng][name] = v

    def _record(self, kind, eng, payload, r, w, n):
        i = len(self.recs)
        raw = set()
        order = set()
        for b in r:
            raw |= b.wi
        for b in w:
            order |= b.wi
            order |= b.ri
        order -= raw
        order.discard(i)
        self.recs.append([kind, eng, payload, raw, order, float(n), None, None])
        for b in r:
            b.ri.add(i)
            self.touched.add(b)
        for b in w:
            b.wi = {i}
            b.ri = set()
            self.touched.add(b)

    def op(self, eng, fn, r=(), w=(), n=512):
        self._record("op", eng, fn, r, w, n)

    def dma(self, q, out, in_, r=(), w=(), slow=False):
        nb = max(out.nbytes(), in_.nbytes())
        self._record("dma", q, (out, in_, slow), r, w, nb)

    @staticmethod
    def _dur(kind, eng, n):
        if kind == "dma":
            return 1000.0 if eng == "pool" else 60.0
        if eng == "pe":
            return max(n, 64.0) / 2.4 + 8.0
        if eng == "act":
            return 224.0 + 0.833 * n
        if eng == "dve":
            return 70.0 + 1.05 * n
        return 120.0 + 2.1 * n

    def flush(self):
        recs = self.recs
        N = len(recs)
        if N == 0:
            return
        self._busy = {}
        queues = {e: [] for e in self.engs}
        for i, rc_ in enumerate(recs):
            queues[rc_[1]].append(i)
        head = {e: 0 for e in queues}
        issued = [False] * N
        done = [None] * N
        eng_free = {e: 0.0 for e in queues}
        dma_pipe = 0.0
        remaining = N
        W = self.WINDOW
        LAT = self.LAT
        while remaining:
            best = None
            for e, q in queues.items():
                h = head[e]
                L = len(q)
                while h < L and issued[q[h]]:
                    h += 1
                head[e] = h
                cnt = 0
                j = h
                ef = eng_free[e]
                while j < L and cnt < W:
                    idx = q[j]
                    j += 1
                    if issued[idx]:
                        continue
                    cnt += 1
                    rc_ = recs[idx]
                    ready = 0.0
                    ok = True
                    for d in rc_[3]:
                        t = done[d]
                        if t is None:
                            ok = False
                            break
                        if t + LAT > ready:
                            ready = t + LAT
                    if not ok:
                        continue
                    for d in rc_[4]:
                        t = done[d]
                        if t is None:
                            ok = False
                            break
                        if recs[d][1] != e or recs[d][0] == "dma":
                            if t + LAT > ready:
                                ready = t + LAT
                    if not ok:
                        continue
                    start = ready if ready > ef else ef
                    if best is None or start < best[0] or (start == best[0] and idx < best[2]):
                        best = (start, e, idx)
                    if start <= ef:
                        break
            assert best is not None, "scheduler deadlock"
            start, e, idx = best
            rc_ = recs[idx]
            kind, _, payload, raw, order, n = rc_[0], rc_[1], rc_[2], rc_[3], rc_[4], rc_[5]
            dur = self._dur(kind, e, n)
            self._busy[e] = self._busy.get(e, 0.0) + dur
            eng_free[e] = start + dur
            if kind == "dma":
                xs = max(start + dur, dma_pipe)
                dma_pipe = xs + n / 180.0
                done[idx] = xs + 2000.0 + n / 180.0
            else:
                done[idx] = start + dur
            issued[idx] = True
            remaining -= 1
            self._emit(idx, rc_)
        self.sim_log.append((N, max(t for t in done if t is not None), dict(self._busy)))
        self.recs = []
        for b in self.touched:
            b.wi = set()
            b.ri = set()
        self.touched = set()

    def _emit(self, idx, rc_):
        kind, eng, payload, raw, order = rc_[0], rc_[1], rc_[2], rc_[3], rc_[4]
        recs = self.recs
        for d in raw:
            name, v = recs[d][6]
            if recs[d][7] == eng and eng == "pe":
                continue
            self.wait(eng, name, v)
        for d in order:
            if recs[d][7] == eng:
                continue
            name, v = recs[d][6]
            self.wait(eng, name, v)
        if kind == "op":
            if self.val[self.cur[eng]] >= self.EPOCH:
                self._new_epoch(eng)
            name = self.cur[eng]
            ins = payload()
            self.val[name] += 1
            ins.then_inc(self.sems[name], 1)
            rc_[6] = (name, self.val[name])
            rc_[7] = eng
        else:
            name = self.dma_names[self.dma_rr]
            self.dma_rr = (self.dma_rr + 1) % self.NDMA
            self.wait(eng, name, self.val[name])
            out, in_, slow = payload
            if slow:
                ins = self.engs[eng].dma_start(out=out, in_=in_, allow_slow_non_contiguous=True)
            else:
                ins = self.engs[eng].dma_start(out=out, in_=in_)
            self.val[name] += 16
            ins.then_inc(self.sems[name], 16)
            rc_[6] = (name, self.val[name])
            rc_[7] = "dma"
        rc_[2] = None
        self.ninst += 1

    def barrier(self):
        self.flush()
        for e in self.engs:
            for name, v in self.val.items():
                self.wait(e, name, v)


def build(depth=DEPTH, debug=(), stop_after=None):
    nc = bass.Bass("TRN2", target_bir_lowering=False)
    k = K(nc)
    dbg = set(debug)
    _uid = [0]

    def sbt(name, shape, dt):
        _uid[0] += 1
        return nc.sbuf_tensor(f"{name}_u{_uid[0]}", shape, dt)

    def din(name, shape, dt=F32):
        return nc.dram_tensor(name, list(shape), dt, kind="ExternalInput").ap()

    def dscr(name, shape, dt=F32):
        kind = "ExternalOutput" if name in dbg else "Internal"
        return nc.dram_tensor(name, list(shape), dt, kind=kind).ap()

    xin = din("xin", [NTOK, D])
    cT_in = din("cT", [128, 8, 2])
    w_mod = din("w_mod", [depth, D, 6 * D])
    b_mod = din("b_mod", [depth, 6 * D])
    g_norm1 = din("g_norm1", [depth, D])
    g_norm2 = din("g_norm2", [depth, D])
    w_in = din("w_in", [depth, D, 5024])
    gvec = din("gvec", [depth, 960])
    w_uq = din("w_uq", [depth, 384, 768])
    w_ukv = din("w_ukv", [depth, 256, 1024])
    w_br = [din("w_br_mla", [depth, 512, D]), din("w_br_gqa", [depth, 512, D]), din("w_four", [depth, 512, D])]
    w_o = din("w_o", [depth, D, D])
    w_up = din("w_up", [depth, D, 2 * DFF])
    convw = din("convw", [depth, 128, 44, 3])
    convb = din("convb", [depth, 128, 44])
    w_down = din("w_down", [depth, DFF, D])
    rope = din("rope", [NTOK, 96])
    ccsc = din("ccsc", [128, 256])
    dftc = din("dftc", [SEQ, SEQ], BF16)
    dfts = din("dfts", [SEQ, SEQ], BF16)
    dftc_c = din("dftc_c", [CTX, CTX], BF16)
    dfts_c = din("dfts_c", [CTX, CTX], BF16)
    y_out = nc.dram_tensor("y", [SEQ // 2, D], F32, kind="ExternalOutput").ap()
    cmask = din("cmask", [128, 2])

    modv = dscr("modv", [depth, 2, 6 * D])
    x1 = dscr("x1", [NTOK, D])
    xmid = dscr("xmid", [NTOK, D])
    KTm = dscr("KTm", [8, 96, NTOK], BF16)
    QTm = dscr("QTm", [8, 96, NTOK], BF16)
    Vm = dscr("Vm", [NTOK, 8, 64], BF16)
    KTg = dscr("KTg", [2, 64, NTOK], BF16)
    QTg = dscr("QTg", [8, 64, NTOK], BF16)
    Vg = dscr("Vg", [NTOK, 2, 64], BF16)
    ABd = dscr("ABd", [NTOK, 1024], BF16)
    Gd = dscr("Gd", [NTOK, 3072], BF16)
    OTm = dscr("OTm", [4, 128, NTOK], BF16)
    OTg = dscr("OTg", [4, 128, NTOK], BF16)
    OTf = dscr("OTf", [4, 128, NTOK], BF16)
    H2T = dscr("H2T", [8, 128, NTOK], BF16)

    with k.es:
        gE = k.es.enter_context
        PSALL = gE(nc.psum_tensor("psall", [128, 4096], F32))

        class Bank:
            def __init__(self, i):
                self.i = i

            def __getitem__(self, key):
                b0 = self.i * 512
                if isinstance(key, slice):
                    return PSALL[:, b0:b0 + 512]
                rows, cols = key
                c0 = b0 + (cols.start or 0)
                c1 = b0 + (512 if cols.stop is None else cols.stop)
                return PSALL[rows, c0:c1]

        PS = [Bank(i) for i in range(8)]
        PSB = [Buf(f"ps{i}") for i in range(8)]
        identb = gE(sbt("identb", [128, 128], BF16))
        ident32 = gE(sbt("ident32", [128, 128], F32))
        idB = Buf("ident")
        k.op("pool", lambda: nc.gpsimd.memset(identb[:], 0.0), w=[idB])
        k.op("pool", lambda: nc.gpsimd.affine_select(out=identb[:], in_=identb[:], pattern=[[-1, 128]],
                                                     compare_op=ALU.not_equal, fill=1.0, base=0,
                                                     channel_multiplier=1), r=[idB], w=[idB])
        k.op("pool", lambda: nc.gpsimd.memset(ident32[:], 0.0), w=[idB])
        k.op("pool", lambda: nc.gpsimd.affine_select(out=ident32[:], in_=ident32[:], pattern=[[-1, 128]],
                                                     compare_op=ALU.not_equal, fill=1.0, base=0,
                                                     channel_multiplier=1), r=[idB], w=[idB])
        k.barrier()

        def psbf(i):
            return PS[i][:].bitcast(BF16)

        done = [False]

        def check_stop(tag):
            if stop_after == tag:
                done[0] = True
            return done[0]

        def phase0(l, E_outer=None, banks=(0, 1), cbw=512):
            with ExitStack() as es:
                E = es.enter_context if E_outer is None else E_outer
                cT = E(sbt("p0_cT", [128, 8, 2], F32))
                sc = E(sbt("p0_sc", [128, 8, 2], F32))
                ob = [E(sbt(f"p0_o{i}", [2, cbw], F32)) for i in range(2)]
                bm = [E(sbt(f"p0_bm{i}", [2, cbw], F32)) for i in range(2)]
                gg = [E(sbt(f"p0_g{i}", [2, cbw], F32)) for i in range(2)]
                wb = [E(sbt(f"p0_w{i}", [128, 8, cbw], F32)) for i in range(2)]
                bcT, bsc = Buf(), Buf()
                bob, bbm, bgg, bwb = [Buf(), Buf()], [Buf(), Buf()], [Buf(), Buf()], [Buf(), Buf()]
                k.dma("sp", cT[:], cT_in, w=[bcT])
                k.op("act", lambda: nc.scalar.activation(out=sc[:], in_=cT[:], func=AF.Silu), r=[bcT], w=[bsc],
                     n=16)
                wm = w_mod[l].rearrange("(k p) n -> p k n", p=128)

                def colblock(cb):
                    j = cb % 2
                    c0 = cb * cbw
                    k.dma("sp", wb[j][:], wm[:, :, c0:c0 + cbw], w=[bwb[j]])
                    k.dma("sp", bm[j][:], b_mod[l, c0:c0 + cbw].partition_broadcast(2), w=[bbm[j]])
                    gsrc = None
                    if D <= c0 < 2 * D:
                        gsrc = g_norm1[l, c0 - D:c0 - D + cbw]
                    elif 4 * D <= c0 < 5 * D:
                        gsrc = g_norm2[l, c0 - 4 * D:c0 - 4 * D + cbw]
                    if gsrc is not None:
                        k.dma("sp", gg[j][:], gsrc.partition_broadcast(2), w=[bgg[j]])
                    for kk in range(8):
                        k.op("pe", lambda kk=kk: nc.tensor.matmul(PS[banks[j]][0:2, 0:cbw], lhsT=sc[:, kk, :],
                                                                  rhs=wb[j][:, kk, :], start=(kk == 0),
                                                                  stop=(kk == 7)),
                             r=[bsc, bwb[j]], w=[PSB[banks[j]]], n=4 * cbw)
                    k.op("dve", lambda: nc.vector.tensor_tensor(out=ob[j][:], in0=PS[banks[j]][0:2, 0:cbw],
                                                                in1=bm[j][:], op=ALU.add),
                         r=[PSB[banks[j]], bbm[j]], w=[bob[j]], n=cbw)
                    if gsrc is not None:
                        k.op("dve", lambda: nc.vector.scalar_tensor_tensor(out=ob[j][:], in0=ob[j][:], scalar=1.0,
                                                                           in1=gg[j][:], op0=ALU.add,
                                                                           op1=ALU.mult),
                             r=[bob[j], bgg[j]], w=[bob[j]], n=cbw)
                    k.dma("sp", modv[l, :, c0:c0 + cbw], ob[j][:], r=[bob[j]])

                for cb in range(6 * D // cbw):
                    colblock(cb)
                if E_outer is None:
                    k.barrier()

        def phase1(l, xcur):
            with ExitStack() as es:
                E = es.enter_context

                def sb(name, shape, dt=F32):
                    return E(sbt("p1_" + name, list(shape), dt))

                win = sb("win", [128, 8, WCOLS], BF16)
                wukv = sb("wukv", [128, 2, 1024], BF16)
                wuq = sb("wuq", [128, 3, 768], BF16)
                gv = sb("gv", [128, 960])
                gm = [sb("gm0", [128, D]), sb("gm1", [128, D])]
                sh = [sb("sh0", [128, D]), sb("sh1", [128, D])]
                bwin = [Buf() for _ in range(8)]
                bwin2 = [Buf() for _ in range(8)]
                bwk, bwq, bgv = Buf(), Buf(), Buf()
                bgm = [Buf(), Buf()]
                bsh = [Buf(), Buf()]
                wi = w_in[l].rearrange("(k p) n -> p k n", p=128)
                for kk in range(8):
                    k.dma("pool", win[:, kk, 0:1440], wi[:, kk, 0:1440], w=[bwin[kk]])
                k.dma("pool", wukv[:], w_ukv[l].rearrange("(k p) n -> p k n", p=128), w=[bwk])
                k.dma("pool", wuq[:], w_uq[l].rearrange("(k p) n -> p k n", p=128), w=[bwq])
                k.dma("sp", gv[:], gvec[l].partition_broadcast(128), w=[bgv])
                for j in range(2):
                    k.dma("sp", sh[j][:], modv[l, j, 0:D].partition_broadcast(128), w=[bsh[j]])
                    k.dma("sp", gm[j][:], modv[l, j, D:2 * D].partition_broadcast(128), w=[bgm[j]])
                bwab = Buf()
                with ExitStack() as es2:
                    E2 = es2.enter_context
                    wf = E2(sbt("p1_wf", [128, 8, 512], F32))
                    cc = E2(sbt("p1_cc", [128, 256], F32))
                    wft = [E2(sbt(f"p1_wft{i}", [128, 128], F32)) for i in range(2)]
                    bwf, bcc = Buf(), Buf()
                    bwft = [Buf(), Buf()]
                    k.dma("sp", wf[:], wi[:, :, 1440:1952], w=[bwf])
                    k.dma("sp", cc[:], ccsc, w=[bcc])
                    n = 0
                    for g in range(4):
                        for kk in range(8):
                            j = n % 2
                            n += 1
                            k.op("pe", lambda g=g, kk=kk, j=j: nc.tensor.transpose(
                                out=PS[j][:, 0:128], in_=wf[:, kk, g * 128:(g + 1) * 128], identity=ident32[:]),
                                 r=[bwf, idB], w=[PSB[j]], n=512)
                            k.op("act", lambda j=j: nc.scalar.copy(out=wft[j][:], in_=PS[j][:, 0:128]),
                                 r=[PSB[j]], w=[bwft[j]], n=128)
                            k.op("pe", lambda j=j: nc.tensor.matmul(PS[2 + j][:, 0:256], lhsT=wft[j][:], rhs=cc[:],
                                                                    start=True, stop=True),
                                 r=[bwft[j], bcc], w=[PSB[2 + j]], n=1024)
                            k.op("dve", lambda g=g, kk=kk, j=j: nc.vector.tensor_copy(
                                out=win[:, kk, 1440 + g * 128:1440 + (g + 1) * 128], in_=PS[2 + j][:, 0:128]),
                                 r=[PSB[2 + j]], w=[bwab], n=128)
                            k.op("dve", lambda g=g, kk=kk, j=j: nc.vector.tensor_copy(
                                out=win[:, kk, 1952 + g * 128:1952 + (g + 1) * 128], in_=PS[2 + j][:, 128:256]),
                                 r=[PSB[2 + j]], w=[bwab], n=128)
                    k.barrier()
                for kk in range(8):
                    k.dma("pool", win[:, kk, 2464:WCOLS], wi[:, kk, 1952:5024], w=[bwin2[kk]])
                WALL = bwin + bwin2 + [bwab]

                xt = [sb(f"xt{i}", [128, D]) for i in range(2)]
                rt = [sb(f"rt{i}", [128, 96]) for i in range(3)]
                hb = [sb(f"hb{i}", [128, D], BF16) for i in range(2)]
                hT = [sb(f"hT{i}", [128, D], BF16) for i in range(2)]
                proj = [sb(f"proj{i}", [128, 1440]) for i in range(3)]
                abb_ = sb("abb", [128, 1024], BF16)
                gts_ = sb("gts", [128, 3072], BF16)
                abb = [abb_, abb_]
                gts = [gts_, gts_]
                tmp = sb("tmp", [128, D])
                stx = sb("stx", [128, 2])
                rsx = sb("rsx", [128, 2])
                sq = sb("sq", [128, 1440], BF16)
                st_ = [sb(f"st{i}", [128, 16]) for i in range(2)]
                rs_ = [sb(f"rs{i}", [128, 16]) for i in range(2)]
                st2 = sb("st2", [128, 16])
                rs2 = sb("rs2", [128, 16])
                cn = sb("cn", [128, 640], BF16)
                cT5 = sb("cT5", [128, 640], BF16)
                kvsb_ = [sb(f"kvsb{i}", [128, 1024]) for i in range(2)]
                qm_ = [sb(f"qm{i}", [128, 768]) for i in range(2)]
                kcat = sb("kcat", [128, 768])
                tA = [sb(f"tA{i}", [128, 768]) for i in range(2)]
                t1 = [sb(f"t1_{i}", [128, 256]) for i in range(2)]
                t2 = [sb(f"t2_{i}", [128, 256]) for i in range(2)]
                kb = sb("kb", [128, 768], BF16)
                qb = sb("qb", [128, 768], BF16)
                vb = sb("vb", [128, 512], BF16)
                kgb = sb("kgb", [128, 128], BF16)
                vgb = sb("vgb", [128, 128], BF16)
                qgb = sb("qgb", [128, 512], BF16)
                ktb = sb("ktb", [128, 1024], BF16)
                qtb = sb("qtb", [128, 1024], BF16)
                kgt = sb("kgt", [128, 128], BF16)
                qgt = sb("qgt", [128, 512], BF16)
                names2 = ["xt", "rt", "hb", "hT", "proj", "abb", "gts", "tA", "t1", "t2"]
                B = {n_: [Buf(n_ + "0"), Buf(n_ + "1"), Buf(n_ + "2")] for n_ in names2 + ["st", "rs", "kvsb", "qm"]}
                B["abb"][1] = B["abb"][0]
                B["gts"][1] = B["gts"][0]
                for n_ in ["tmp", "stx", "rsx", "sq", "st2", "rs2", "cn", "cT5", "kcat",
                           "kb", "qb", "vb", "kgb", "vgb", "qgb", "ktb", "qtb", "kgt", "qgt"]:
                    B[n_] = Buf(n_)

                def rstd_from(stt, rst, lo, hi, n, bs, br):
                    k.op("act", lambda: nc.scalar.activation(out=rst[:, lo:hi], in_=stt[:, lo:hi], func=AF.Ln,
                                                             scale=1.0 / n, bias=EPS), r=[bs], w=[br], n=16)

                def norm_rope(src3, H, dh, rs_ap, g_ap, nrot, cos_ap, sin_ap, out3, rsrc, bout, ci, brt):
                    tA3 = tA[ci][:, 0:H * dh].rearrange("p (h d) -> p h d", h=H)
                    bA, b1, b2 = B["tA"][ci], B["t1"][ci], B["t2"][ci]
                    k.op("dve", lambda: nc.vector.tensor_tensor(out=tA3, in0=src3,
                                                                in1=rs_ap.unsqueeze(2).to_broadcast([128, H, dh]),
                                                                op=ALU.mult), r=rsrc, w=[bA], n=H * dh)
                    k.op("dve", lambda: nc.vector.tensor_tensor(out=tA3, in0=tA3,
                                                                in1=g_ap.unsqueeze(1).to_broadcast([128, H, dh]),
                                                                op=ALU.mult), r=[bA, bgv], w=[bA], n=H * dh)
                    r0 = dh - nrot
                    hf = nrot // 2
                    if r0 > 0:
                        k.op("act", lambda: nc.scalar.copy(out=out3[:, :, 0:r0], in_=tA3[:, :, 0:r0]),
                             r=[bA], w=[bout], n=H * r0)
                    x1_ = tA3[:, :, r0:r0 + hf]
                    x2_ = tA3[:, :, r0 + hf:dh]
                    cb_ = cos_ap.unsqueeze(1).to_broadcast([128, H, hf])
                    sb_ = sin_ap.unsqueeze(1).to_broadcast([128, H, hf])
                    t13 = t1[ci][:, 0:H * hf].rearrange("p (h d) -> p h d", h=H)
                    t23 = t2[ci][:, 0:H * hf].rearrange("p (h d) -> p h d", h=H)
                    k.op("dve", lambda: nc.vector.tensor_tensor(out=t13, in0=x1_, in1=cb_, op=ALU.mult),
                         r=[bA, brt], w=[b1], n=H * hf)
                    k.op("pool", lambda: nc.gpsimd.tensor_tensor(out=t23, in0=x2_, in1=sb_, op=ALU.mult),
                         r=[bA, brt], w=[b2], n=H * hf)
                    k.op("dve", lambda: nc.vector.tensor_tensor(out=out3[:, :, r0:r0 + hf], in0=t13, in1=t23,
                                                                op=ALU.subtract), r=[b1, b2], w=[bout], n=H * hf)
                    k.op("dve", lambda: nc.vector.tensor_tensor(out=t13, in0=x2_, in1=cb_, op=ALU.mult),
                         r=[bA, brt], w=[b1], n=H * hf)
                    k.op("pool", lambda: nc.gpsimd.tensor_tensor(out=t23, in0=x1_, in1=sb_, op=ALU.mult),
                         r=[bA, brt], w=[b2], n=H * hf)
                    k.op("dve", lambda: nc.vector.tensor_tensor(out=out3[:, :, r0 + hf:dh], in0=t13, in1=t23,
                                                                op=ALU.add), r=[b1, b2], w=[bout], n=H * hf)

                G_CKV, G_CQ, G_KNM, G_QNM, G_KNG, G_QNG = (gv[:, 0:256], gv[:, 256:640], gv[:, 640:736],
                                                           gv[:, 736:832], gv[:, 832:896], gv[:, 896:960])
                blocks = [(0, 512, "p"), (512, 512, "p"), (1024, 416, "p"), (1440, 512, "ab"), (1952, 512, "ab")]
                blocks += [(2464 + 512 * j, 512, "g") for j in range(6)]

                def stageA(i):
                    s_ = i % 2
                    tok = slice(i * 128, (i + 1) * 128)
                    ic = 1 if i < 2 else 0
                    k.dma("sp", xt[s_][:], xcur[tok, :], w=[B["xt"][s_]])
                    k.dma("sp", rt[i % 3][:], rope[tok, :], w=[B["rt"][i % 3]])
                    k.op("act", lambda: nc.scalar.activation(out=tmp[:], in_=xt[s_][:], func=AF.Square),
                         r=[B["xt"][s_]], w=[B["tmp"]], n=1024)
                    k.op("dve", lambda: nc.vector.tensor_reduce(out=stx[:, 0:1], in_=tmp[:], axis=AX.X, op=ALU.add),
                         r=[B["tmp"]], w=[B["stx"]], n=1024)
                    rstd_from(stx, rsx, 0, 1, D, B["stx"], B["rsx"])
                    k.op("act", lambda: nc.scalar.activation(out=rsx[:, 0:1], in_=rsx[:, 0:1], func=AF.Exp,
                                                             scale=-0.5), r=[B["rsx"]], w=[B["rsx"]], n=16)
                    k.op("dve", lambda: nc.vector.scalar_tensor_tensor(out=tmp[:], in0=xt[s_][:],
                                                                       scalar=rsx[:, 0:1], in1=gm[ic][:],
                                                                       op0=ALU.mult, op1=ALU.mult),
                         r=[B["xt"][s_], B["rsx"], bgm[ic]], w=[B["tmp"]], n=1024)
                    k.op("dve", lambda: nc.vector.tensor_tensor(out=hb[s_][:], in0=tmp[:], in1=sh[ic][:],
                                                                op=ALU.add),
                         r=[B["tmp"], bsh[ic]], w=[B["hb"][s_]], n=1024)
                    for kk in range(8):
                        k.op("pe", lambda kk=kk: nc.tensor.transpose(out=psbf(7)[:, kk * 128:(kk + 1) * 128],
                                                                     in_=hb[s_][:, kk * 128:(kk + 1) * 128],
                                                                     identity=identb[:]),
                             r=[B["hb"][s_], idB], w=[PSB[7]], n=128)
                    k.op("act", lambda: nc.scalar.copy(out=hT[s_][:], in_=psbf(7)[:, 0:1024]), r=[PSB[7]],
                         w=[B["hT"][s_]], n=1024)

                own = set(range(NT)) if l < depth - 1 else set(range(2, 19))
                blocks_kv = [(0, 512, "p"), (512, 32, "p"), (1440, 512, "ab"), (1952, 512, "ab")]

                def stageB(i):
                    s_ = i % 2
                    tok = slice(i * 128, (i + 1) * 128)
                    for bi, (c0, wd, kind) in enumerate(blocks if i in own else blocks_kv):
                        pb = bi % 4
                        for kk in range(8):
                            wb_ = bwin[kk] if kind == "p" else (bwab if kind == "ab" else bwin2[kk])
                            k.op("pe", lambda kk=kk, pb=pb, c0=c0, wd=wd: nc.tensor.matmul(
                                PS[pb][:, 0:wd], lhsT=hT[s_][:, kk * 128:(kk + 1) * 128], rhs=win[:, kk, c0:c0 + wd],
                                start=(kk == 0), stop=(kk == 7)), r=[B["hT"][s_], wb_], w=[PSB[pb]], n=wd)
                        if kind == "p":
                            k.op("dve", lambda pb=pb, c0=c0, wd=wd: nc.vector.tensor_copy(
                                out=proj[i % 3][:, c0:c0 + wd], in_=PS[pb][:, 0:wd]), r=[PSB[pb]],
                                 w=[B["proj"][i % 3]], n=wd)
                        elif kind == "ab":
                            k.op("act", lambda pb=pb, c0=c0: nc.scalar.copy(
                                out=abb[s_][:, c0 - 1440:c0 - 1440 + 512], in_=PS[pb][:, 0:512]),
                                 r=[PSB[pb]], w=[B["abb"][s_]], n=512)
                        else:
                            k.op("act", lambda pb=pb, c0=c0: nc.scalar.activation(
                                out=gts[s_][:, c0 - 2464:c0 - 2464 + 512], in_=PS[pb][:, 0:512], func=AF.Sigmoid),
                                 r=[PSB[pb]], w=[B["gts"][s_]], n=512)
                    k.dma("sp", ABd[tok, :], abb[s_][:], r=[B["abb"][s_]])
                    if i in own:
                        k.dma("sp", Gd[tok, :], gts[s_][:], r=[B["gts"][s_]])

                def cvars(i):
                    s_ = i % 2
                    return (slice(i * 128, (i + 1) * 128), proj[i % 3], B["proj"][i % 3], B["rt"][i % 3], rt[i % 3],
                            st_[s_], rs_[s_], kvsb_[s_], qm_[s_],
                            {"st": B["st"][s_], "rs": B["rs"][s_], "kvsb": B["kvsb"][s_], "qm": B["qm"][s_]})

                def stageC1(i):
                    tok, pj, bpj, brt, rtt, st, rs, kvsb, qm, BB = cvars(i)
                    B = dict(B_all)
                    B.update(BB)
                    full = i in own
                    wsq = 1440 if full else 544
                    k.op("dve", lambda: nc.vector.tensor_tensor(out=sq[:, 0:wsq], in0=pj[:, 0:wsq], in1=pj[:, 0:wsq],
                                                                op=ALU.mult),
                         r=[bpj], w=[B["sq"]], n=wsq)
                    k.op("dve", lambda: nc.vector.tensor_reduce(out=st[:, 0:1], in_=sq[:, 0:256], axis=AX.X,
                                                                op=ALU.add), r=[B["sq"]], w=[B["st"]], n=256)
                    if full:
                        k.op("dve", lambda: nc.vector.tensor_reduce(out=st[:, 1:2], in_=sq[:, 544:928], axis=AX.X,
                                                                    op=ALU.add), r=[B["sq"]], w=[B["st"]], n=384)
                    k.op("dve", lambda: nc.vector.tensor_reduce(
                        out=st[:, 2:4], in_=sq[:, 288:416].rearrange("p (h d) -> p h d", h=2), axis=AX.X,
                        op=ALU.add), r=[B["sq"]], w=[B["st"]], n=128)
                    if full:
                        k.op("dve", lambda: nc.vector.tensor_reduce(
                            out=st[:, 4:12], in_=sq[:, 928:1440].rearrange("p (h d) -> p h d", h=8), axis=AX.X,
                            op=ALU.add), r=[B["sq"]], w=[B["st"]], n=512)
                    rstd_from(st, rs, 0, 1, 256, B["st"], B["rs"])
                    if full:
                        rstd_from(st, rs, 1, 2, 384, B["st"], B["rs"])
                        rstd_from(st, rs, 2, 12, 64, B["st"], B["rs"])
                        k.op("act", lambda: nc.scalar.activation(out=rs[:, 0:12], in_=rs[:, 0:12], func=AF.Exp,
                                                                 scale=-0.5), r=[B["rs"]], w=[B["rs"]], n=16)
                    else:
                        rstd_from(st, rs, 2, 4, 64, B["st"], B["rs"])
                        k.op("act", lambda: nc.scalar.activation(out=rs[:, 0:1], in_=rs[:, 0:1], func=AF.Exp,
                                                                 scale=-0.5), r=[B["rs"]], w=[B["rs"]], n=16)
                        k.op("act", lambda: nc.scalar.activation(out=rs[:, 2:4], in_=rs[:, 2:4], func=AF.Exp,
                                                                 scale=-0.5), r=[B["rs"]], w=[B["rs"]], n=16)
                    k.op("dve", lambda: nc.vector.scalar_tensor_tensor(out=cn[:, 0:256], in0=pj[:, 0:256],
                                                                       scalar=rs[:, 0:1], in1=G_CKV, op0=ALU.mult,
                                                                       op1=ALU.mult),
                         r=[bpj, B["rs"], bgv], w=[B["cn"]], n=256)
                    if full:
                        k.op("dve", lambda: nc.vector.scalar_tensor_tensor(out=cn[:, 256:640], in0=pj[:, 544:928],
                                                                           scalar=rs[:, 1:2], in1=G_CQ,
                                                                           op0=ALU.mult, op1=ALU.mult),
                             r=[bpj, B["rs"], bgv], w=[B["cn"]], n=384)
                    ncn = 5 if full else 2
                    for kk in range(ncn):
                        k.op("pe", lambda kk=kk: nc.tensor.transpose(out=psbf(6)[:, kk * 128:(kk + 1) * 128],
                                                                     in_=cn[:, kk * 128:(kk + 1) * 128],
                                                                     identity=identb[:]),
                             r=[B["cn"], idB], w=[PSB[6]], n=128)
                    k.op("act", lambda: nc.scalar.copy(out=cT5[:, 0:ncn * 128], in_=psbf(6)[:, 0:ncn * 128]),
                         r=[PSB[6]], w=[B["cT5"]], n=ncn * 128)
                    for half in range(2):
                        for kk in range(2):
                            k.op("pe", lambda kk=kk, half=half: nc.tensor.matmul(
                                PS[4 + half][:, 0:512], lhsT=cT5[:, kk * 128:(kk + 1) * 128],
                                rhs=wukv[:, kk, half * 512:(half + 1) * 512], start=(kk == 0), stop=(kk == 1)),
                                 r=[B["cT5"], bwk], w=[PSB[4 + half]], n=512)
                        k.op("act", lambda half=half: nc.scalar.copy(out=kvsb[:, half * 512:(half + 1) * 512],
                                                                     in_=PS[4 + half][:, 0:512]),
                             r=[PSB[4 + half]], w=[B["kvsb"]], n=512)
                    for half, (c0, wd) in enumerate([(0, 512), (512, 256)] if full else []):
                        for kk in range(3):
                            k.op("pe", lambda kk=kk, half=half, c0=c0, wd=wd: nc.tensor.matmul(
                                PS[4 + half][:, 0:wd], lhsT=cT5[:, 256 + kk * 128:256 + (kk + 1) * 128],
                                rhs=wuq[:, kk, c0:c0 + wd], start=(kk == 0), stop=(kk == 2)),
                                 r=[B["cT5"], bwq], w=[PSB[4 + half]], n=wd)
                        k.op("act", lambda half=half, c0=c0, wd=wd: nc.scalar.copy(out=qm[:, c0:c0 + wd],
                                                                                  in_=PS[4 + half][:, 0:wd]),
                             r=[PSB[4 + half]], w=[B["qm"]], n=wd)

                def stageC2(i):
                    tok, pj, bpj, brt, rtt, st, rs, kvsb, qm, BB = cvars(i)
                    B = dict(B_all)
                    B.update(BB)
                    full = i in own
                    kv3 = kvsb[:].rearrange("p (h d) -> p h d", h=8)
                    kc3 = kcat[:].rearrange("p (h d) -> p h d", h=8)
                    k.op("act", lambda: nc.scalar.copy(out=kc3[:, :, 0:64], in_=kv3[:, :, 0:64]),
                         r=[B["kvsb"]], w=[B["kcat"]], n=512)
                    k.op("pool", lambda: nc.gpsimd.tensor_copy(
                        out=kc3[:, :, 64:96], in_=pj[:, 256:288].unsqueeze(1).to_broadcast([128, 8, 32])),
                         r=[bpj, B["kcat"]], w=[B["kcat"]], n=256)
                    k.op("act", lambda: nc.scalar.copy(out=vb[:].rearrange("p (h d) -> p h d", h=8),
                                                       in_=kv3[:, :, 64:128]), r=[B["kvsb"]], w=[B["vb"]], n=512)
                    k.dma("sp", Vm[tok, :, :].rearrange("t h d -> t (h d)"), vb[:], r=[B["vb"]])
                    for ci, (src, s0) in enumerate(((kcat, 0), (qm, 8)) if full else ((kcat, 0),)):
                        bsrc = B["kcat"] if s0 == 0 else B["qm"]
                        k.op("pool", lambda src=src, ci=ci: nc.gpsimd.tensor_tensor(
                            out=tA[ci][:], in0=src[:], in1=src[:], op=ALU.mult), r=[bsrc], w=[B["tA"][ci]], n=768)
                        k.op("dve", lambda s0=s0, ci=ci: nc.vector.tensor_reduce(
                            out=st2[:, s0:s0 + 8], in_=tA[ci][:].rearrange("p (h d) -> p h d", h=8), axis=AX.X,
                            op=ALU.add), r=[B["tA"][ci]], w=[B["st2"]], n=768)
                    nr2 = 16 if full else 8
                    rstd_from(st2, rs2, 0, nr2, 96, B["st2"], B["rs2"])
                    k.op("act", lambda: nc.scalar.activation(out=rs2[:, 0:nr2], in_=rs2[:, 0:nr2], func=AF.Exp,
                                                             scale=-0.5),
                         r=[B["rs2"]], w=[B["rs2"]], n=16)
                    kb3 = kb[:].rearrange("p (h d) -> p h d", h=8)
                    qb3 = qb[:].rearrange("p (h d) -> p h d", h=8)
                    norm_rope(kc3, 8, 96, rs2[:, 0:8], G_KNM, 32, rtt[:, 0:16], rtt[:, 16:32], kb3,
                              [B["kcat"], B["rs2"]], B["kb"], 0, brt)
                    if full:
                        norm_rope(qm[:].rearrange("p (h d) -> p h d", h=8), 8, 96, rs2[:, 8:16], G_QNM, 32,
                                  rtt[:, 0:16], rtt[:, 16:32], qb3, [B["qm"], B["rs2"]], B["qb"], 1, brt)
                    kq_list = ((kb3, ktb, B["kb"], B["ktb"], KTm), (qb3, qtb, B["qb"], B["qtb"], QTm))
                    for pi_, (srcb, dstt, bs, bd, dram) in enumerate(kq_list if full else kq_list[:1]):
                        pbk = 4 + pi_
                        for h in range(8):
                            k.op("pe", lambda h=h, srcb=srcb, pbk=pbk: nc.tensor.transpose(
                                out=psbf(pbk)[0:96, h * 128:(h + 1) * 128], in_=srcb[:, h, :], identity=identb[:]),
                                 r=[bs, idB], w=[PSB[pbk]], n=128)
                        k.op("act", lambda dstt=dstt, pbk=pbk: nc.scalar.copy(out=dstt[0:96, :],
                                                                            in_=psbf(pbk)[0:96, 0:1024]),
                             r=[PSB[pbk]], w=[bd], n=1024)
                        k.dma("sp", dram[:, :, tok].rearrange("h d t -> d h t"),
                              dstt[0:96, :].rearrange("p (h t) -> p h t", h=8), r=[bd])
                    norm_rope(pj[:, 288:416].rearrange("p (h d) -> p h d", h=2), 2, 64, rs[:, 2:4], G_KNG, 64,
                              rtt[:, 32:64], rtt[:, 64:96], kgb[:].rearrange("p (h d) -> p h d", h=2),
                              [bpj, B["rs"]], B["kgb"], 0, brt)
                    k.op("act", lambda: nc.scalar.copy(out=vgb[:], in_=pj[:, 416:544]), r=[bpj],
                         w=[B["vgb"]], n=128)
                    k.dma("sp", Vg[tok, :, :].rearrange("t h d -> t (h d)"), vgb[:], r=[B["vgb"]])
                    k.op("pe", lambda: nc.tensor.transpose(out=psbf(6)[:, 0:128], in_=kgb[:], identity=identb[:]),
                         r=[B["kgb"], idB], w=[PSB[6]], n=128)
                    k.op("act", lambda: nc.scalar.copy(out=kgt[:], in_=psbf(6)[:, 0:128]), r=[PSB[6]],
                         w=[B["kgt"]], n=128)
                    k.dma("sp", KTg[:, :, tok].rearrange("j d t -> (j d) t"), kgt[:], r=[B["kgt"]])
                    if not full:
                        return
                    norm_rope(pj[:, 928:1440].rearrange("p (h d) -> p h d", h=8), 8, 64, rs[:, 4:12], G_QNG, 64,
                              rtt[:, 32:64], rtt[:, 64:96], qgb[:].rearrange("p (h d) -> p h d", h=8),
                              [bpj, B["rs"]], B["qgb"], 1, brt)
                    for c in range(4):
                        k.op("pe", lambda c=c: nc.tensor.transpose(out=psbf(6)[:, c * 128:(c + 1) * 128],
                                                                   in_=qgb[:, c * 128:(c + 1) * 128],
                                                                   identity=identb[:]),
                             r=[B["qgb"], idB], w=[PSB[6]], n=128)
                    k.op("act", lambda: nc.scalar.copy(out=qgt[:], in_=psbf(6)[:, 0:512]), r=[PSB[6]],
                         w=[B["qgt"]], n=512)
                    k.dma("sp", QTg[:, :, tok].rearrange("(c two) d t -> (two d) c t", two=2),
                          qgt[:].rearrange("p (c t) -> p c t", c=4), r=[B["qgt"]])

                B_all = B
                stageA(0)
                stageB(0)
                for i in range(NT):
                    if i + 1 < NT:
                        stageA(i + 1)
                        stageB(i + 1)
                    stageC1(i)
                    if i >= 1:
                        stageC2(i - 1)
                stageC2(NT - 1)
                k.barrier()

        def phase2(l, prefetch=None):
            with ExitStack() as es:
                E = es.enter_context
                GK = 3
                kt_sb = [E(sbt(f"p2_kt{i}", [96, NTOK], BF16)) for i in range(2)]
                qt_sb = [E(sbt(f"p2_qt{i}", [96, NTOK], BF16)) for i in range(2)]
                va_sb = [E(sbt(f"p2_va{i}", [128, NT, 128], BF16)) for i in range(2)]
                pT = [E(sbt(f"p2_pT{i}", [128, GK * 512], BF16)) for i in range(2)]
                rc = E(sbt("p2_rc", [128, 512], F32))
                ot = [E(sbt(f"p2_ot{i}", [128, NTOK], BF16)) for i in range(2)]
                bkt, bqt, bva = [Buf(), Buf()], [Buf(), Buf()], [Buf(), Buf()]
                bpT = [Buf(), Buf()]
                brc = Buf()
                bot = [Buf(), Buf()]
                bSG = [Buf(), Buf()]
                for i in range(2):
                    k.op("pool", lambda i=i: nc.gpsimd.memset(va_sb[i][:, :, 64:128], 1.0), w=[bva[i]])
                if l < depth - 1:
                    qblocks = [(CTX + 512 * j, 512, NT) for j in range(8)]
                    qblocks.append((0, 256, 2))
                    ot_ranges = [(0, NTOK)]
                else:
                    qblocks = [(CTX + 448 * j, 448, NT) for j in range(4)] + [(CTX + 1792, 384, NT)]
                    ot_ranges = [(CTX, 2432)]
                scale_m = 96 ** -0.5
                scale_g = 64 ** -0.5
                state = {"acc": 0, "g": 0}

                def attend(KT, dq, QT, VA, scale, otile, prow, deps, bo):
                    for (q0_, qn_, nk_) in qblocks:
                        attend_block(KT, dq, QT, VA, scale, otile, prow, deps, bo, q0_, qn_, nk_)

                def attend_block(KT, dq, QT, VA, scale, otile, prow, deps, bo, q0, qn, nk):
                    if True:
                        ob = 6 + state["acc"] % 2
                        state["acc"] += 1
                        groups = [list(range(a, min(a + GK, nk))) for a in range(0, nk, GK)]
                        gslot = []

                        def s_grp(gi):
                            sl = state["g"] % 2
                            state["g"] += 1
                            gslot.append(sl)
                            for ii, kt in enumerate(groups[gi]):
                                c0 = sl * GK * 512 + ii * 512
                                k.op("pe", lambda kt=kt, c0=c0: nc.tensor.matmul(
                                    PSALL[:, c0:c0 + qn], lhsT=KT[0:dq, kt * 128:(kt + 1) * 128],
                                    rhs=QT[0:dq, q0:q0 + qn], start=True, stop=True), r=deps, w=[bSG[sl]], n=qn)

                        s_grp(0)
                        for gi, grp in enumerate(groups):
                            if gi + 1 < len(groups):
                                s_grp(gi + 1)
                            sl = gslot[gi]
                            ng = len(grp)
                            src = PSALL[:, sl * GK * 512:sl * GK * 512 + ng * 512].rearrange(
                                "p (g q) -> p g q", q=512)[:, :, 0:qn]
                            dst = pT[sl][:, 0:ng * 512].rearrange("p (g q) -> p g q", q=512)[:, :, 0:qn]
                            k.op("act", lambda src=src, dst=dst: nc.scalar.activation(
                                out=dst, in_=src, func=AF.Exp, scale=scale), r=[bSG[sl]], w=[bpT[sl]], n=ng * qn)
                            for ii, kt in enumerate(grp):
                                k.op("pe", lambda kt=kt, ii=ii, sl=sl: nc.tensor.matmul(
                                    PS[ob][:, 0:qn], lhsT=VA[:, kt, :], rhs=pT[sl][:, ii * 512:ii * 512 + qn],
                                    start=(kt == 0), stop=(kt == nk - 1)), r=deps + [bpT[sl]], w=[PSB[ob]], n=qn)
                        k.op("dve", lambda: nc.vector.reciprocal(out=rc[64:128, 0:qn], in_=PS[ob][64:128, 0:qn]),
                             r=[PSB[ob]], w=[brc])
                        k.op("dve", lambda: nc.vector.tensor_tensor(
                            out=otile[prow:prow + 64, q0:q0 + qn], in0=PS[ob][0:64, 0:qn], in1=rc[64:128, 0:qn],
                            op=ALU.mult), r=[PSB[ob], brc], w=[bo])

                nload = 0
                for h in range(8):
                    j = nload % 2
                    nload += 1
                    k.dma("sp", kt_sb[j][0:96, :], KTm[h], w=[bkt[j]])
                    k.dma("sp", qt_sb[j][0:96, :], QTm[h], w=[bqt[j]])
                    vmv = Vm[:, h, :].rearrange("(t p) d -> p t d", p=128)
                    for t0 in range(0, NT, 12):
                        t1_ = min(NT, t0 + 12)
                        k.dma("sp", va_sb[j][:, t0:t1_, 0:64], vmv[:, t0:t1_, :], w=[bva[j]])
                    c = h // 2
                    oj = c % 2
                    attend(kt_sb[j], 96, qt_sb[j], va_sb[j], scale_m, ot[oj], (h % 2) * 64,
                           [bkt[j], bqt[j], bva[j]], bot[oj])
                    if h % 2 == 1:
                        for (qa, qb_) in ot_ranges:
                            k.dma("sp", OTm[c, :, qa:qb_], ot[oj][:, qa:qb_], r=[bot[oj]])
                    if h == 1 and prefetch is not None:
                        prefetch(bot[0])
                for jkv in range(2):
                    j = nload % 2
                    nload += 1
                    k.dma("sp", kt_sb[j][0:64, :], KTg[jkv], w=[bkt[j]])
                    vgv = Vg[:, jkv, :].rearrange("(t p) d -> p t d", p=128)
                    for t0 in range(0, NT, 12):
                        t1_ = min(NT, t0 + 12)
                        k.dma("sp", va_sb[j][:, t0:t1_, 0:64], vgv[:, t0:t1_, :], w=[bva[j]])
                    for i in range(4):
                        h = jkv * 4 + i
                        jq = h % 2
                        k.dma("sp", qt_sb[jq][0:64, :], QTg[h], w=[bqt[jq]])
                        c = h // 2
                        oj = c % 2
                        attend(kt_sb[j], 64, qt_sb[jq], va_sb[j], scale_g, ot[oj], (h % 2) * 64,
                               [bkt[j], bqt[jq], bva[j]], bot[oj])
                        if h % 2 == 1:
                            for (qa, qb_) in ot_ranges:
                                k.dma("sp", OTg[c, :, qa:qb_], ot[oj][:, qa:qb_], r=[bot[oj]])
                k.barrier()

        def phase3(l, ab_pre=None):
            with ExitStack() as es:
                E = es.enter_context
                ab = E(sbt("p3_ab", [128, NT, 1024], BF16)) if ab_pre is None else ab_pre
                cb = [E(sbt(f"p3_c{i}", [128, 32, 256], BF16)) for i in range(2)]
                sbb = [E(sbt(f"p3_s{i}", [128, 32, 256], BF16)) for i in range(2)]
                of = [E(sbt(f"p3_o{i}", [128, 4, 256], BF16)) for i in range(2)]
                bab = [Buf() for _ in range(NT // 2)]
                bcb = [[Buf() for _ in range(4)] for _ in range(2)]
                bsb = [[Buf() for _ in range(4)] for _ in range(2)]
                bof = [Buf(), Buf()]
                abv = ABd.rearrange("(t p) n -> p t n", p=128)
                if ab_pre is None:
                    for t0 in range(0, NT, 2):
                        k.dma("sp", ab[:, t0:t0 + 2, :], abv[:, t0:t0 + 2, :], w=[bab[t0 // 2]])
                dc = dftc.rearrange("(t p) l -> p t l", p=128)
                ds = dfts.rearrange("(t p) l -> p t l", p=128)
                jbs = list(range(16)) if l < depth - 1 else list(range(9))
                jobs = [(dc, ds, 2, 32, jb * 256, CTX + jb * 256) for jb in jbs]
                if l < depth - 1:
                    jobs.append((dftc_c.rearrange("(t p) l -> p t l", p=128),
                                 dfts_c.rearrange("(t p) l -> p t l", p=128), 0, 2, 0, 0))
                nacc_ = [0]

                def job(n, mc, ms, t_off, ntl, c0, tok0):
                    nacc = nacc_[0]
                    j = n % 2
                    for ta_ in range(0, ntl, 8):
                        tb_ = min(ntl, ta_ + 8)
                        k.dma("sp", cb[j][:, ta_:tb_, :], mc[:, ta_:tb_, c0:c0 + 256], w=[bcb[j][ta_ // 8]])
                        k.dma("sp", sbb[j][:, ta_:tb_, :], ms[:, ta_:tb_, c0:c0 + 256], w=[bsb[j][ta_ // 8]])
                    for g in range(4):
                        pb = nacc % 2
                        nacc += 1
                        for t in range(ntl):
                            k.op("pe", lambda g=g, t=t, pb=pb: nc.tensor.matmul(
                                PS[pb][:, 0:256], lhsT=ab[:, t_off + t, g * 128:(g + 1) * 128], rhs=cb[j][:, t, :],
                                start=(t == 0), stop=False), r=[bab[(t_off + t) // 2], bcb[j][t // 8]],
                                 w=[PSB[pb]], n=256)
                            k.op("pe", lambda g=g, t=t, pb=pb: nc.tensor.matmul(
                                PS[pb][:, 0:256], lhsT=ab[:, t_off + t, 512 + g * 128:512 + (g + 1) * 128],
                                rhs=sbb[j][:, t, :], start=False, stop=(t == ntl - 1)),
                                 r=[bab[(t_off + t) // 2], bsb[j][t // 8]], w=[PSB[pb]], n=256)
                        k.op("act", lambda g=g, pb=pb: nc.scalar.copy(out=of[j][:, g, :], in_=PS[pb][:, 0:256]),
                             r=[PSB[pb]], w=[bof[j]], n=256)
                    k.dma("sp", OTf[:, :, tok0:tok0 + 256].rearrange("g p t -> p g t"), of[j][:], r=[bof[j]])
                    nacc_[0] = nacc

                for n, jb_ in enumerate(jobs):
                    job(n, *jb_)
                k.barrier()

        def phase4(l, xcur, w_pre=None):
            with ExitStack() as es:
                E = es.enter_context

                def sb(name, shape, dt=F32):
                    return E(sbt("p4_" + name, list(shape), dt))

                if w_pre is None:
                    wbr = [sb(f"wbr{i}", [128, 4, D], BF16) for i in range(3)]
                    wo = sb("wo", [128, 8, D], BF16)
                else:
                    wbr, wo = w_pre
                gt1 = [sb("gt1_0", [128, D]), sb("gt1_1", [128, D])]
                gm2 = [sb("gm2_0", [128, D]), sb("gm2_1", [128, D])]
                sh2 = [sb("sh2_0", [128, D]), sb("sh2_1", [128, D])]
                bwbr = [Buf() for _ in range(3)]
                bwo, = [Buf()]
                bgt1, bgm2, bsh2 = [Buf(), Buf()], [Buf(), Buf()], [Buf(), Buf()]
                if w_pre is None:
                    for i in range(3):
                        k.dma("pool", wbr[i][:], w_br[i][l].rearrange("(k p) n -> p k n", p=128), w=[bwbr[i]])
                    k.dma("pool", wo[:], w_o[l].rearrange("(k p) n -> p k n", p=128), w=[bwo])
                for j in range(2):
                    k.dma("sp", gt1[j][:], modv[l, j, 2 * D:3 * D].partition_broadcast(128), w=[bgt1[j]])
                    k.dma("sp", sh2[j][:], modv[l, j, 3 * D:4 * D].partition_broadcast(128), w=[bsh2[j]])
                    k.dma("sp", gm2[j][:], modv[l, j, 4 * D:5 * D].partition_broadcast(128), w=[bgm2[j]])
                names = ["ot0", "ot1", "ot2", "gts", "xt", "ya", "yt", "yb", "yT", "xm", "tmp", "st", "rs", "hb", "hT"]
                shapes = {"ot0": ([128, 4, 128], BF16), "ot1": ([128, 4, 128], BF16), "ot2": ([128, 4, 128], BF16),
                          "gts": ([128, 3072], BF16), "xt": ([128, D], F32), "ya": ([128, D], F32),
                          "yt": ([128, 512], F32), "yb": ([128, D], BF16), "yT": ([128, D], BF16),
                          "xm": ([128, D], F32), "tmp": ([128, D], F32), "st": ([128, 2], F32),
                          "rs": ([128, 2], F32), "hb": ([128, D], BF16), "hT": ([128, D], BF16)}
                NSET = 3
                TS = [{n_: sb(f"{n_}_{s_}", *shapes[n_]) for n_ in names} for s_ in range(NSET)]
                BS = [{n_: Buf(n_) for n_ in names} for s_ in range(NSET)]
                OTs = [OTm, OTg, OTf]
                first = 0 if l < depth - 1 else 2
                rr = [0]

                def nb():
                    b_ = rr[0]
                    rr[0] = (b_ + 1) % 8
                    return b_

                def tile(i, n_):
                    T = TS[n_ % NSET]
                    B = BS[n_ % NSET]
                    tok = slice(i * 128, (i + 1) * 128)
                    ic = 1 if i < 2 else 0
                    otin = [T["ot0"], T["ot1"], T["ot2"]]
                    gts, xt, ya, yt, yb, yT, xm, tmp, st, rs, hb, hT = (T[n_] for n_ in names[3:])
                    for bi in range(3):
                        k.dma("sp", otin[bi][:], OTs[bi][:, :, tok].rearrange("c p t -> p c t"), w=[B[f"ot{bi}"]])
                    k.dma("sp", gts[:], Gd[tok, :], w=[B["gts"]])
                    k.dma("sp", xt[:], xcur[tok, :], w=[B["xt"]])
                    for half in range(2):
                        cs = slice(half * 512, (half + 1) * 512)
                        pbs = [nb(), nb(), nb()]
                        for bi in range(3):
                            for c in range(4):
                                k.op("pe", lambda bi=bi, c=c, cs=cs, pbs=pbs: nc.tensor.matmul(
                                    PS[pbs[bi]][:, 0:512], lhsT=otin[bi][:, c, :], rhs=wbr[bi][:, c, cs],
                                    start=(c == 0), stop=(c == 3)), r=[B[f"ot{bi}"], bwbr[bi]], w=[PSB[pbs[bi]]])
                        k.op("dve", lambda cs=cs, half=half, pbs=pbs: nc.vector.tensor_tensor(
                            out=ya[:, cs], in0=PS[pbs[0]][:, 0:512], in1=gts[:, half * 512:(half + 1) * 512],
                            op=ALU.mult), r=[PSB[pbs[0]], B["gts"]], w=[B["ya"]])
                        for bi in (1, 2):
                            k.op("dve", lambda bi=bi, half=half, pbs=pbs: nc.vector.tensor_tensor(
                                out=yt[:], in0=PS[pbs[bi]][:, 0:512],
                                in1=gts[:, bi * D + half * 512:bi * D + (half + 1) * 512], op=ALU.mult),
                                 r=[PSB[pbs[bi]], B["gts"]], w=[B["yt"]])
                            k.op("dve", lambda cs=cs: nc.vector.tensor_tensor(out=ya[:, cs], in0=ya[:, cs],
                                                                              in1=yt[:], op=ALU.add),
                                 r=[B["ya"], B["yt"]], w=[B["ya"]])
                    pt = nb()
                    k.op("act", lambda: nc.scalar.copy(out=yb[:], in_=ya[:]), r=[B["ya"]], w=[B["yb"]], n=1024)
                    for kk in range(8):
                        k.op("pe", lambda kk=kk: nc.tensor.transpose(out=psbf(pt)[:, kk * 128:(kk + 1) * 128],
                                                                     in_=yb[:, kk * 128:(kk + 1) * 128],
                                                                     identity=identb[:]),
                             r=[B["yb"], idB], w=[PSB[pt]], n=128)
                    k.op("act", lambda: nc.scalar.copy(out=yT[:], in_=psbf(pt)[:, 0:1024]), r=[PSB[pt]],
                         w=[B["yT"]], n=1024)
                    for half in range(2):
                        cs = slice(half * 512, (half + 1) * 512)
                        pw = nb()
                        for kk in range(8):
                            k.op("pe", lambda kk=kk, cs=cs, pw=pw: nc.tensor.matmul(
                                PS[pw][:, 0:512], lhsT=yT[:, kk * 128:(kk + 1) * 128], rhs=wo[:, kk, cs],
                                start=(kk == 0), stop=(kk == 7)), r=[B["yT"], bwo], w=[PSB[pw]])
                        k.op("dve", lambda cs=cs, pw=pw: nc.vector.tensor_tensor(
                            out=tmp[:, cs], in0=PS[pw][:, 0:512], in1=gt1[ic][:, cs], op=ALU.mult),
                             r=[PSB[pw], bgt1[ic]], w=[B["tmp"]])
                    k.op("dve", lambda: nc.vector.tensor_tensor(out=xm[:], in0=tmp[:], in1=xt[:], op=ALU.add),
                         r=[B["tmp"], B["xt"]], w=[B["xm"]], n=1024)
                    k.dma("sp", xmid[tok, :], xm[:], r=[B["xm"]])
                    k.op("act", lambda: nc.scalar.activation(out=tmp[:], in_=xm[:], func=AF.Square),
                         r=[B["xm"]], w=[B["tmp"]], n=1024)
                    k.op("dve", lambda: nc.vector.tensor_reduce(out=st[:, 0:1], in_=tmp[:], axis=AX.X, op=ALU.add),
                         r=[B["tmp"]], w=[B["st"]], n=1024)
                    k.op("act", lambda: nc.scalar.activation(out=rs[:, 0:1], in_=st[:, 0:1], func=AF.Ln,
                                                             scale=1.0 / D, bias=EPS), r=[B["st"]], w=[B["rs"]],
                         n=16)
                    k.op("act", lambda: nc.scalar.activation(out=rs[:, 0:1], in_=rs[:, 0:1], func=AF.Exp,
                                                             scale=-0.5), r=[B["rs"]], w=[B["rs"]], n=16)
                    k.op("dve", lambda: nc.vector.scalar_tensor_tensor(out=tmp[:], in0=xm[:],
                                                                       scalar=rs[:, 0:1], in1=gm2[ic][:],
                                                                       op0=ALU.mult, op1=ALU.mult),
                         r=[B["xm"], B["rs"], bgm2[ic]], w=[B["tmp"]], n=1024)
                    k.op("pool", lambda: nc.gpsimd.tensor_tensor(out=hb[:], in0=tmp[:], in1=sh2[ic][:],
                                                                 op=ALU.add),
                         r=[B["tmp"], bsh2[ic]], w=[B["hb"]], n=1024)
                    pt2 = nb()
                    for kk in range(8):
                        k.op("pe", lambda kk=kk: nc.tensor.transpose(out=psbf(pt2)[:, kk * 128:(kk + 1) * 128],
                                                                     in_=hb[:, kk * 128:(kk + 1) * 128],
                                                                     identity=identb[:]),
                             r=[B["hb"], idB], w=[PSB[pt2]], n=128)
                    k.op("act", lambda: nc.scalar.copy(out=hT[:], in_=psbf(pt2)[:, 0:1024]), r=[PSB[pt2]],
                         w=[B["hT"]], n=1024)
                    k.dma("sp", H2T[:, :, tok].rearrange("k p t -> p k t"),
                          hT[:].rearrange("p (k t) -> p k t", k=8), r=[B["hT"]])

                tiles = list(range(first, NT)) if l < depth - 1 else list(range(2, 19))
                for n_, i in enumerate(tiles):
                    tile(i, n_)
                k.barrier()

        def phase5(l, xdst, last):
            with ExitStack() as es:
                E = es.enter_context

                def sb(name, shape, dt=F32):
                    return E(sbt("p5_" + name, list(shape), dt))

                wup = sb("wup", [128, 8, 2 * DFF], BF16)
                wdn = sb("wdn", [128, 22, D], BF16)
                cw = sb("cw", [128, 44, 3])
                cbi = sb("cb", [128, 44])
                gt2 = [sb("gt2_0", [128, D]), sb("gt2_1", [128, D])]
                cgrp = [(0, 6), (6, 12), (12, 17), (17, 22)]
                bwup = [[Buf() for _ in cgrp] for _ in range(2)]
                cg_of = {}
                for gi_, (ca, cb2) in enumerate(cgrp):
                    for c_ in range(ca, cb2):
                        cg_of[c_] = gi_
                bwdn = [Buf() for _ in range(11)]
                bcw, bcb_ = Buf(), Buf()
                bgt2 = [Buf(), Buf()]
                wu = w_up[l].rearrange("(k p) n -> p k n", p=128)
                for gi_, (ca, cb2) in enumerate(cgrp):
                    for part_ in range(2):
                        c0_ = (ca + 22 * part_) * 128
                        c1_ = (cb2 + 22 * part_) * 128
                        k.dma("pool", wup[:, :, c0_:c1_], wu[:, :, c0_:c1_], w=[bwup[part_][gi_]])
                wd_ = w_down[l].rearrange("(k p) n -> p k n", p=128)
                for c in range(0, 22, 2):
                    k.dma("pool", wdn[:, c:c + 2, :], wd_[:, c:c + 2, :], w=[bwdn[c // 2]])
                k.dma("sp", cw[:], convw[l], w=[bcw])
                k.dma("sp", cbi[:], convb[l], w=[bcb_])
                for j in range(2):
                    k.dma("sp", gt2[j][:], modv[l, j, 5 * D:6 * D].partition_broadcast(128), w=[bgt2[j]])
                FB = 256
                NU = 4
                h2 = [sb(f"h2_{i}", [128, 8, FB + 2], BF16) for i in range(2)]
                us = [sb(f"us{i}", [128, FB + 2]) for i in range(NU)]
                ta = [sb(f"ta{i}", [128, FB]) for i in range(2)]
                tv = [sb(f"tv{i}", [128, FB]) for i in range(2)]
                sa = [sb(f"sa{i}", [128, FB]) for i in range(2)]
                gT = [sb(f"gT{i}", [128, 22, FB], BF16) for i in range(2)]
                xm = [sb(f"xm{i}", [128, D]) for i in range(2)]
                xo = [sb(f"xo{i}", [128, D]) for i in range(2)]
                tmp = [sb(f"tmp{i}", [128, D]) for i in range(2)]
                bh2 = [Buf(), Buf()]
                bus = [Buf() for _ in range(NU)]
                bta, btv, bsa = [Buf(), Buf()], [Buf(), Buf()], [Buf(), Buf()]
                bgT = [[Buf() for _ in range(22)] for _ in range(2)]
                btmp = [Buf(), Buf()]
                bxm, bxo = [Buf(), Buf()], [Buf(), Buf()]
                cm = sb("cm", [128, 2])
                bcm = Buf()
                k.dma("sp", cm[:], cmask, w=[bcm])
                nblk = SEQ // FB
                blocks = []
                for j in range(nblk if not last else nblk // 2):
                    left, right = "c", "c"
                    if j == 0:
                        left = "z"
                    if j == nblk - 1:
                        right = "z"
                    blocks.append((CTX + FB * j, FB, left, right))
                if not last:
                    blocks.append((0, 256, "z", "z"))
                h2v = H2T.rearrange("k p t -> p k t")
                cnt = {"u": 0, "x": 0}

                def unit(j, tn, c, part, gs):
                    ch = c + 22 * part
                    ui = cnt["u"]
                    cnt["u"] += 1
                    pb = ui % 4
                    u = us[ui % NU]
                    bu = bus[ui % NU]
                    cs_ = c % 2
                    tdst, btd = ((ta[cs_], bta[cs_]), (tv[cs_], btv[cs_]))[part]
                    for kk in range(8):
                        k.op("pe", lambda kk=kk: nc.tensor.matmul(
                            PS[pb][:, 0:tn + 2], lhsT=wup[:, kk, ch * 128:(ch + 1) * 128],
                            rhs=h2[j][:, kk, 0:tn + 2], start=(kk == 0), stop=(kk == 7)),
                             r=[bwup[part][cg_of[c]], bh2[j]], w=[PSB[pb]], n=tn + 2)
                    k.op("act", lambda: nc.scalar.activation(
                        out=tdst[:, 0:tn], in_=PS[pb][:, 1:tn + 1], func=AF.Identity, scale=cw[:, ch, 1:2],
                        bias=cbi[:, ch:ch + 1]), r=[PSB[pb], bcw, bcb_], w=[btd], n=tn)
                    k.op("dve", lambda: nc.vector.scalar_tensor_tensor(
                        out=tdst[:, 0:tn], in0=PS[pb][:, 0:tn], scalar=cw[:, ch, 0:1], in1=tdst[:, 0:tn],
                        op0=ALU.mult, op1=ALU.add), r=[PSB[pb], bcw, btd], w=[btd], n=tn)
                    k.op("dve", lambda: nc.vector.scalar_tensor_tensor(
                        out=tdst[:, 0:tn], in0=PS[pb][:, 2:tn + 2], scalar=cw[:, ch, 2:3], in1=tdst[:, 0:tn],
                        op0=ALU.mult, op1=ALU.add), r=[PSB[pb], bcw, btd], w=[btd], n=tn)
                    if part == 1:
                        k.op("act", lambda: nc.scalar.activation(out=sa[cs_][:, 0:tn], in_=ta[cs_][:, 0:tn],
                                                                 func=AF.Silu), r=[bta[cs_]], w=[bsa[cs_]], n=tn)
                        k.op("pool", lambda: nc.gpsimd.tensor_tensor(out=gT[gs][:, c, 0:tn], in0=sa[cs_][:, 0:tn],
                                                                     in1=tv[cs_][:, 0:tn], op=ALU.mult),
                             r=[bsa[cs_], btv[cs_]], w=[bgT[gs][c]], n=tn)

                def down(gs, tn, t0, tt, ic):
                    jx = cnt["x"] % 2
                    cnt["x"] += 1
                    tok = slice(t0 + tt * 128, t0 + (tt + 1) * 128)
                    k.dma("sp", xm[jx][:], xmid[tok, :], w=[bxm[jx]])
                    for half in range(2):
                        cs = slice(half * 512, (half + 1) * 512)
                        for c in range(22):
                            k.op("pe", lambda c=c, cs=cs, half=half: nc.tensor.matmul(
                                PS[4 + half][:, 0:512], lhsT=gT[gs][:, c, tt * 128:(tt + 1) * 128],
                                rhs=wdn[:, c, cs], start=(c == 0), stop=(c == 21)),
                                 r=[bgT[gs][c], bwdn[c // 2]], w=[PSB[4 + half]])
                        k.op("dve", lambda half=half, cs=cs: nc.vector.tensor_tensor(
                            out=tmp[jx][:, cs], in0=PS[4 + half][:, 0:512], in1=gt2[ic][:, cs], op=ALU.mult),
                             r=[PSB[4 + half], bgt2[ic]], w=[btmp[jx]])
                    k.op("pool", lambda: nc.gpsimd.tensor_tensor(out=xo[jx][:], in0=tmp[jx][:], in1=xm[jx][:],
                                                                 op=ALU.add),
                         r=[btmp[jx], bxm[jx]], w=[bxo[jx]], n=1024)
                    if last:
                        dst = xdst[t0 - CTX + tt * 128:t0 - CTX + (tt + 1) * 128, :]
                    else:
                        dst = xdst[tok, :]
                    k.dma("sp", dst, xo[jx][:], r=[bxo[jx]])

                def block(n, t0, tn, left, right):
                    j = n % 2
                    gs = n % 2
                    ic = 1 if t0 < CTX else 0
                    lo = t0 - 1 if left == "c" else t0
                    hi = t0 + tn + 1 if right == "c" else t0 + tn
                    k.dma("sp", h2[j][:, :, lo - (t0 - 1):hi - (t0 - 1)], h2v[:, :, lo:hi], w=[bh2[j]])
                    for spec, col in ((left, 0), (right, tn + 1)):
                        if spec == "c":
                            continue
                        if spec == "z":
                            k.op("pool", lambda col=col: nc.gpsimd.memset(h2[j][:, :, col:col + 1], 0.0),
                                 w=[bh2[j]], n=8)
                            continue
                        src_tok, mi = spec
                        k.dma("sp", h2[j][:, :, col:col + 1], h2v[:, :, src_tok:src_tok + 1], w=[bh2[j]],
                              slow=True)
                        k.op("dve", lambda col=col, mi=mi: nc.vector.tensor_scalar(
                            out=h2[j][:, :, col:col + 1], in0=h2[j][:, :, col:col + 1], scalar1=cm[:, mi:mi + 1],
                            scalar2=None, op0=ALU.mult), r=[bh2[j], bcm], w=[bh2[j]], n=8)
                    for c in range(22):
                        for part in range(2):
                            unit(j, tn, c, part, gs)
                    for tt in range(tn // 128):
                        down(gs, tn, t0, tt, ic)

                for n, bl_ in enumerate(blocks):
                    block(n, *bl_)
                k.barrier()

        xcur = xin
        for l in range(depth):
            last = (l == depth - 1)
            phase0(l)
            if check_stop(f"p0_{l}"):
                break
            phase1(l, xcur)
            if check_stop(f"p1_{l}"):
                break
            with ExitStack() as esw:
                wbr_p = [esw.enter_context(sbt(f"pre_wbr{i}", [128, 4, D], BF16)) for i in range(3)]
                wo_p = esw.enter_context(sbt("pre_wo", [128, 8, D], BF16))
                with ExitStack() as esa:
                    ab_p = esa.enter_context(sbt("pre_ab", [128, NT, 1024], BF16))

                    def prefetch(bdep, l=l):
                        abv_ = ABd.rearrange("(t p) n -> p t n", p=128)
                        for t0 in range(0, NT, 2):
                            k.dma("sp", ab_p[:, t0:t0 + 2, :], abv_[:, t0:t0 + 2, :], r=[bdep], w=[Buf()])
                        for i in range(3):
                            k.dma("pool", wbr_p[i][:], w_br[i][l].rearrange("(k p) n -> p k n", p=128),
                                  r=[bdep], w=[Buf()])
                        k.dma("pool", wo_p[:], w_o[l].rearrange("(k p) n -> p k n", p=128), r=[bdep], w=[Buf()])

                    phase2(l, prefetch)
                    if check_stop(f"p2_{l}"):
                        break
                    phase3(l, ab_p)
                    if check_stop(f"p3_{l}"):
                        break
                phase4(l, xcur, (wbr_p, wo_p))
                if check_stop(f"p4_{l}"):
                    break
            phase5(l, y_out if last else x1, last)
            if check_stop(f"p5_{l}"):
                break
            xcur = x1
        k.barrier()
    return nc, k


def _consts():
    GRID_W = 64
    rows = SEQ // GRID_W
    row = np.repeat(np.arange(rows, dtype=np.float32), GRID_W)
    col = np.tile(np.arange(GRID_W, dtype=np.float32), rows)

    def tab(rot):
        n_f = rot // 4
        inv = (np.float32(10000.0) ** (-np.arange(n_f, dtype=np.float32) / np.float32(n_f))).astype(np.float32)
        ang = np.concatenate([row[:, None] * inv, col[:, None] * inv], axis=-1).astype(np.float32)
        return np.cos(ang).astype(np.float32), np.sin(ang).astype(np.float32)

    cm, sm = tab(32)
    cg, sg = tab(64)
    rope = np.zeros((NTOK, 96), np.float32)
    rope[:CTX, 0:16] = 1.0
    rope[:CTX, 32:64] = 1.0
    rope[CTX:, 0:16] = cm
    rope[CTX:, 16:32] = sm
    rope[CTX:, 32:64] = cg
    rope[CTX:, 64:96] = sg

    def dft(n, scale):
        idx = (np.outer(np.arange(n, dtype=np.int64), np.arange(n, dtype=np.int64)) % n).astype(np.float64)
        ang = 2.0 * np.pi * idx / n
        return np.cos(ang) * scale, np.sin(ang) * scale

    c128, s128 = dft(128, 128 ** -0.5)
    ccsc = np.concatenate([c128, s128], axis=1).astype(np.float32)
    cL, sL = dft(SEQ, 1.0 / 64)
    cC, sC = dft(CTX, 1.0 / 16)
    bf = ml_dtypes.bfloat16
    return dict(rope=rope, ccsc=ccsc, dftc=cL.astype(np.float32).astype(bf), dfts=(-sL).astype(np.float32).astype(bf),
                dftc_c=cC.astype(np.float32).astype(bf), dfts_c=(-sC).astype(np.float32).astype(bf))


_CONSTS = None


def make_in_maps(inputs, depth=DEPTH, n_cores=8, parity=0):
    global _CONSTS
    if _CONSTS is None:
        _CONSTS = _consts()
    f = lambda a: np.ascontiguousarray(np.asarray(a, dtype=np.float32))
    x, c, ctx, c_ctx = f(inputs["x"]), f(inputs["c"]), f(inputs["ctx"]), f(inputs["c_ctx"])
    shared = {
        "w_mod": f(inputs["w_mod"])[:depth], "b_mod": f(inputs["b_mod"])[:depth],
        "g_norm1": f(inputs["g_norm1"])[:depth], "g_norm2": f(inputs["g_norm2"])[:depth],
        "w_in": f(inputs["w_in"])[:depth],
        "gvec": np.ascontiguousarray(np.concatenate(
            [f(inputs[n]) for n in ("g_ckv", "g_cq", "g_kn_mla", "g_qn_mla", "g_kn_gqa", "g_qn_gqa")],
            axis=1)[:depth]),
        "w_uq": f(inputs["w_uq"])[:depth], "w_ukv": f(inputs["w_ukv"])[:depth],
        "w_br_mla": f(inputs["w_br_mla"])[:depth], "w_br_gqa": f(inputs["w_br_gqa"])[:depth],
        "w_four": f(inputs["w_four"])[:depth], "w_o": f(inputs["w_o"])[:depth],
        "w_up": f(inputs["w_up"])[:depth],
        "convw": np.ascontiguousarray(
            f(inputs["conv_w"])[:depth].reshape(depth, 3, 44, 128).transpose(0, 3, 2, 1)),
        "convb": np.ascontiguousarray(f(inputs["conv_b"])[:depth].reshape(depth, 44, 128).transpose(0, 2, 1)),
        "w_down": f(inputs["w_down"])[:depth],
    }
    shared.update(_CONSTS)
    in_maps = []
    rev = np.arange(SEQ - 1, -1, -1)
    revc = np.arange(CTX - 1, -1, -1)
    odd_consts = None
    for core in range(n_cores):
        par = (core % 2) if n_cores == 8 else parity
        b = (core // 2 if n_cores == 8 else core) % x.shape[0]
        m = dict(shared)
        cT = np.stack([c[b].reshape(8, 128).T, c_ctx.reshape(8, 128).T], axis=-1)
        m["cT"] = np.ascontiguousarray(cT.astype(np.float32))
        m["cmask"] = np.zeros((128, 2), np.float32)
        if par == 0:
            m["xin"] = np.ascontiguousarray(np.concatenate([ctx[b], x[b]], axis=0))
        else:
            if odd_consts is None:
                rp = _CONSTS["rope"]
                odd_consts = {
                    "rope": np.ascontiguousarray(np.concatenate([rp[:CTX][revc], rp[CTX:][rev]], axis=0)),
                    "dftc": np.ascontiguousarray(_CONSTS["dftc"][rev][:, rev]),
                    "dfts": np.ascontiguousarray(_CONSTS["dfts"][rev][:, rev]),
                    "dftc_c": np.ascontiguousarray(_CONSTS["dftc_c"][revc][:, revc]),
                    "dfts_c": np.ascontiguousarray(_CONSTS["dfts_c"][revc][:, revc]),
                    "convw": np.ascontiguousarray(shared["convw"][..., ::-1]),
                }
            m.update(odd_consts)
            m["xin"] = np.ascontiguousarray(np.concatenate([ctx[b][revc], x[b][rev]], axis=0))
        in_maps.append(m)
    return in_maps


def kernel(**inputs):
    nc, _ = build()
    in_maps = make_in_maps(inputs)
    res = run_bass_kernel_spmd(nc, in_maps, core_ids=list(range(8)))
    nb = np.asarray(inputs["x"]).shape[0]
    out = np.stack([np.concatenate([np.asarray(res.results[2 * b]["y"], dtype=np.float32),
                                    np.asarray(res.results[2 * b + 1]["y"], dtype=np.float32)[::-1]], axis=0)
                    for b in range(nb)], axis=0)
    return out
```

```python
import numpy as np
import ml_dtypes
from contextlib import ExitStack
import concourse.bass as bass
import concourse.mybir as mybir
from concourse.bass_utils import run_bass_kernel_spmd

F32 = mybir.dt.float32
BF16 = mybir.dt.bfloat16
ALU = mybir.AluOpType
AF = mybir.ActivationFunctionType
AX = mybir.AxisListType

D = 1024
NTOK = 4352
CTX = 256
SEQ = 4096
NT = NTOK // 128
DEPTH = 2
DFF = 2816
EPS = 1e-6
WCOLS = 1440 + 1024 + 3072


class Buf:
    __slots__ = ("wi", "ri", "name")

    def __init__(self, name=""):
        self.wi = set()
        self.ri = set()
        self.name = name


class K:
    NDMA = 24
    EPOCH = 60000
    WINDOW = 96
    LAT = 400.0

    def __init__(self, nc):
        self.nc = nc
        self.es = ExitStack()
        self.engs = {"pe": nc.tensor, "act": nc.scalar, "dve": nc.vector, "pool": nc.gpsimd, "sp": nc.sync}
        self.sems = {}
        self.val = {}
        self.waited = {e: {} for e in self.engs}
        self.epoch = {e: 0 for e in ("pe", "act", "dve", "pool")}
        self.cur = {}
        for e in ("pe", "act", "dve", "pool"):
            self._new_epoch(e)
        self.dma_names = []
        for i in range(self.NDMA):
            n = f"dma{i}"
            self._newsem(n)
            self.dma_names.append(n)
        self.dma_rr = 0
        self.ninst = 0
        self.recs = []
        self.touched = set()
        self.sim_log = []

    def _newsem(self, name):
        self.sems[name] = self.es.enter_context(self.nc.semaphore(name))
        self.val[name] = 0

    def _new_epoch(self, e):
        n = f"{e}_{self.epoch[e]}"
        self.epoch[e] += 1
        self._newsem(n)
        self.cur[e] = n

    def wait(self, eng, name, v):
        if v <= 0:
            return
        if self.waited[eng].get(name, 0) >= v:
            return
        self.engs[eng].wait_ge(self.sems[name], v)
        self.waited[eng][name] = v

    def _record(self, kind, eng, payload, r, w, n):
        i = len(self.recs)
        raw = set()
        order = set()
        for b in r:
            raw |= b.wi
        for b in w:
            order |= b.wi
            order |= b.ri
        order -= raw
        order.discard(i)
        self.recs.append([kind, eng, payload, raw, order, float(n), None, None])
        for b in r:
            b.ri.add(i)
            self.touched.add(b)
        for b in w:
            b.wi = {i}
            b.ri = set()
            self.touched.add(b)

    def op(self, eng, fn, r=(), w=(), n=512):
        self._record("op", eng, fn, r, w, n)

    def dma(self, q, out, in_, r=(), w=(), slow=False):
        nb = max(out.nbytes(), in_.nbytes())
        self._record("dma", q, (out, in_, slow), r, w, nb)

    @staticmethod
    def _dur(kind, eng, n):
        if kind == "dma":
            return 1000.0 if eng == "pool" else 60.0
        if eng == "pe":
            return max(n, 64.0) / 2.4 + 8.0
        if eng == "act":
            return 224.0 + 0.833 * n
        if eng == "dve":
            return 70.0 + 1.05 * n
        return 120.0 + 2.1 * n

    def flush(self):
        recs = self.recs
        N = len(recs)
        if N == 0:
            return
        self._busy = {}
        queues = {e: [] for e in self.engs}
        for i, rc_ in enumerate(recs):
            queues[rc_[1]].append(i)
        head = {e: 0 for e in queues}
        issued = [False] * N
        done = [None] * N
        eng_free = {e: 0.0 for e in queues}
        dma_pipe = 0.0
        remaining = N
        W = self.WINDOW
        LAT = self.LAT
        while remaining:
            best = None
            for e, q in queues.items():
                h = head[e]
                L = len(q)
                while h < L and issued[q[h]]:
                    h += 1
                head[e] = h
                cnt = 0
                j = h
                ef = eng_free[e]
                while j < L and cnt < W:
                    idx = q[j]
                    j += 1
                    if issued[idx]:
                        continue
                    cnt += 1
                    rc_ = recs[idx]
                    ready = 0.0
                    ok = True
                    for d in rc_[3]:
                        t = done[d]
                        if t is None:
                            ok = False
                            break
                        if t + LAT > ready:
                            ready = t + LAT
                    if not ok:
                        continue
                    for d in rc_[4]:
                        t = done[d]
                        if t is None:
                            ok = False
                            break
                        if recs[d][1] != e or recs[d][0] == "dma":
                            if t + LAT > ready:
                                ready = t + LAT
                    if not ok:
                        continue
                    start = ready if ready > ef else ef
                    if best is None or start < best[0] or (start == best[0] and idx < best[2]):
                        best = (start, e, idx)
                    if start <= ef:
                        break
            assert best is not None, "scheduler deadlock"
            start, e, idx = best
            rc_ = recs[idx]
            kind, _, payload, raw, order, n = rc_[0], rc_[1], rc_[2], rc_[3], rc_[4], rc_[5]
            dur = self._dur(kind, e, n)
            self._busy[e] = self._busy.get(e, 0.0) + dur
            eng_free[e] = start + dur
            if kind == "dma":
                xs = max(start + dur, dma_pipe)
                dma_pipe = xs + n / 180.0
                done[idx] = xs + 2000.0 + n / 180.0
            else:
                done[idx] = start + dur
            issued[idx] = True
            remaining -= 1
            self._emit(idx, rc_)
        self.sim_log.append((N, max(t for t in done if t is not None), dict(self._busy)))
        self.recs = []
        for b in self.touched:
            b.wi = set()
            b.ri = set()
        self.touched = set()

    def _emit(self, idx, rc_):
        kind, eng, payload, raw, order = rc_[0], rc_[1], rc_[2], rc_[3], rc_[4]
        recs = self.recs
        for d in raw:
            name, v = recs[d][6]
            if recs[d][7] == eng and eng == "pe":
                continue
            self.wait(eng, name, v)
        for d in order:
            if recs[d][7] == eng:
                continue
            name, v = recs[d][6]
            self.wait(eng, name, v)
        if kind == "op":
            if self.val[self.cur[eng]] >= self.EPOCH:
                self._new_epoch(eng)
            name = self.cur[eng]
            ins = payload()
            self.val[name] += 1
            ins.then_inc(self.sems[name], 1)
            rc_[6] = (name, self.val[name])
            rc_[7] = eng
        else:
            name = self.dma_names[self.dma_rr]
            self.dma_rr = (self.dma_rr + 1) % self.NDMA
            self.wait(eng, name, self.val[name])
            out, in_, slow = payload
            if slow:
                ins = self.engs[eng].dma_start(out=out, in_=in_, allow_slow_non_contiguous=True)
            else:
                ins = self.engs[eng].dma_start(out=out, in_=in_)
            self.val[name] += 16
            ins.then_inc(self.sems[name], 16)
            rc_[6] = (name, self.val[name])
            rc_[7] = "dma"
        rc_[2] = None
        self.ninst += 1

    def barrier(self):
        self.flush()
        for e in self.engs:
            for name, v in self.val.items():
                self.wait(e, name, v)


def build(depth=DEPTH, debug=(), stop_after=None):
    nc = bass.Bass("TRN2", target_bir_lowering=False)
    k = K(nc)
    dbg = set(debug)
    _uid = [0]

    def sbt(name, shape, dt):
        _uid[0] += 1
        return nc.sbuf_tensor(f"{name}_u{_uid[0]}", shape, dt)

    def din(name, shape, dt=F32):
        return nc.dram_tensor(name, list(shape), dt, kind="ExternalInput").ap()

    def dscr(name, shape, dt=F32):
        kind = "ExternalOutput" if name in dbg else "Internal"
        return nc.dram_tensor(name, list(shape), dt, kind=kind).ap()

    xin = din("xin", [NTOK, D])
    cT_in = din("cT", [128, 8, 2])
    w_mod = din("w_mod", [depth, D, 6 * D])
    b_mod = din("b_mod", [depth, 6 * D])
    g_norm1 = din("g_norm1", [depth, D])
    g_norm2 = din("g_norm2", [depth, D])
    w_in = din("w_in", [depth, D, 5024])
    gvec = din("gvec", [depth, 960])
    w_uq = din("w_uq", [depth, 384, 768])
    w_ukv = din("w_ukv", [depth, 256, 1024])
    w_br = [din("w_br_mla", [depth, 512, D]), din("w_br_gqa", [depth, 512, D]), din("w_four", [depth, 512, D])]
    w_o = din("w_o", [depth, D, D])
    w_up = din("w_up", [depth, D, 2 * DFF])
    convw = din("convw", [depth, 128, 44, 3])
    convb = din("convb", [depth, 128, 44])
    w_down = din("w_down", [depth, DFF, D])
    rope = din("rope", [NTOK, 96])
    ccsc = din("ccsc", [128, 256])
    dftc = din("dftc", [SEQ, SEQ], BF16)
    dfts = din("dfts", [SEQ, SEQ], BF16)
    dftc_c = din("dftc_c", [CTX, CTX], BF16)
    dfts_c = din("dfts_c", [CTX, CTX], BF16)
    y_out = nc.dram_tensor("y", [SEQ // 2, D], F32, kind="ExternalOutput").ap()
    cmask = din("cmask", [128, 2])

    modv = dscr("modv", [depth, 2, 6 * D])
    x1 = dscr("x1", [NTOK, D])
    xmid = dscr("xmid", [NTOK, D])
    KTm = dscr("KTm", [8, 96, NTOK], BF16)
    QTm = dscr("QTm", [8, 96, NTOK], BF16)
    Vm = dscr("Vm", [NTOK, 8, 64], BF16)
    KTg = dscr("KTg", [2, 64, NTOK], BF16)
    QTg = dscr("QTg", [8, 64, NTOK], BF16)
    Vg = dscr("Vg", [NTOK, 2, 64], BF16)
    ABd = dscr("ABd", [NTOK, 1024], BF16)
    Gd = dscr("Gd", [NTOK, 3072], BF16)
    OTm = dscr("OTm", [4, 128, NTOK], BF16)
    OTg = dscr("OTg", [4, 128, NTOK], BF16)
    OTf = dscr("OTf", [4, 128, NTOK], BF16)
    H2T = dscr("H2T", [8, 128, NTOK], BF16)

    with k.es:
        gE = k.es.enter_context
        PSALL = gE(nc.psum_tensor("psall", [128, 4096], F32))

        class Bank:
            def __init__(self, i):
                self.i = i

            def __getitem__(self, key):
                b0 = self.i * 512
                if isinstance(key, slice):
                    return PSALL[:, b0:b0 + 512]
                rows, cols = key
                c0 = b0 + (cols.start or 0)
                c1 = b0 + (512 if cols.stop is None else cols.stop)
                return PSALL[rows, c0:c1]

        PS = [Bank(i) for i in range(8)]
        PSB = [Buf(f"ps{i}") for i in range(8)]
        identb = gE(sbt("identb", [128, 128], BF16))
        ident32 = gE(sbt("ident32", [128, 128], F32))
        idB = Buf("ident")
        k.op("pool", lambda: nc.gpsimd.memset(identb[:], 0.0), w=[idB])
        k.op("pool", lambda: nc.gpsimd.affine_select(out=identb[:], in_=identb[:], pattern=[[-1, 128]],
                                                     compare_op=ALU.not_equal, fill=1.0, base=0,
                                                     channel_multiplier=1), r=[idB], w=[idB])
        k.op("pool", lambda: nc.gpsimd.memset(ident32[:], 0.0), w=[idB])
        k.op("pool", lambda: nc.gpsimd.affine_select(out=ident32[:], in_=ident32[:], pattern=[[-1, 128]],
                                                     compare_op=ALU.not_equal, fill=1.0, base=0,
                                                     channel_multiplier=1), r=[idB], w=[idB])
        k.barrier()

        def psbf(i):
            return PS[i][:].bitcast(BF16)

        done = [False]

        def check_stop(tag):
            if stop_after == tag:
                done[0] = True
            return done[0]

        def phase0(l, E_outer=None, banks=(0, 1), cbw=512):
            with ExitStack() as es:
                E = es.enter_context if E_outer is None else E_outer
                cT = E(sbt("p0_cT", [128, 8, 2], F32))
                sc = E(sbt("p0_sc", [128, 8, 2], F32))
                ob = [E(sbt(f"p0_o{i}", [2, cbw], F32)) for i in range(2)]
                bm = [E(sbt(f"p0_bm{i}", [2, cbw], F32)) for i in range(2)]
                gg = [E(sbt(f"p0_g{i}", [2, cbw], F32)) for i in range(2)]
                wb = [E(sbt(f"p0_w{i}", [128, 8, cbw], F32)) for i in range(2)]
                bcT, bsc = Buf(), Buf()
                bob, bbm, bgg, bwb = [Buf(), Buf()], [Buf(), Buf()], [Buf(), Buf()], [Buf(), Buf()]
                k.dma("sp", cT[:], cT_in, w=[bcT])
                k.op("act", lambda: nc.scalar.activation(out=sc[:], in_=cT[:], func=AF.Silu), r=[bcT], w=[bsc],
                     n=16)
                wm = w_mod[l].rearrange("(k p) n -> p k n", p=128)

                def colblock(cb):
                    j = cb % 2
                    c0 = cb * cbw
                    k.dma("sp", wb[j][:], wm[:, :, c0:c0 + cbw], w=[bwb[j]])
                    k.dma("sp", bm[j][:], b_mod[l, c0:c0 + cbw].partition_broadcast(2), w=[bbm[j]])
                    gsrc = None
                    if D <= c0 < 2 * D:
                        gsrc = g_norm1[l, c0 - D:c0 - D + cbw]
                    elif 4 * D <= c0 < 5 * D:
                        gsrc = g_norm2[l, c0 - 4 * D:c0 - 4 * D + cbw]
                    if gsrc is not None:
                        k.dma("sp", gg[j][:], gsrc.partition_broadcast(2), w=[bgg[j]])
                    for kk in range(8):
                        k.op("pe", lambda kk=kk: nc.tensor.matmul(PS[banks[j]][0:2, 0:cbw], lhsT=sc[:, kk, :],
                                                                  rhs=wb[j][:, kk, :], start=(kk == 0),
                                                                  stop=(kk == 7)),
                             r=[bsc, bwb[j]], w=[PSB[banks[j]]], n=4 * cbw)
                    k.op("dve", lambda: nc.vector.tensor_tensor(out=ob[j][:], in0=PS[banks[j]][0:2, 0:cbw],
                                                                in1=bm[j][:], op=ALU.add),
                         r=[PSB[banks[j]], bbm[j]], w=[bob[j]], n=cbw)
                    if gsrc is not None:
                        k.op("dve", lambda: nc.vector.scalar_tensor_tensor(out=ob[j][:], in0=ob[j][:], scalar=1.0,
                                                                           in1=gg[j][:], op0=ALU.add,
                                                                           op1=ALU.mult),
                             r=[bob[j], bgg[j]], w=[bob[j]], n=cbw)
                    k.dma("sp", modv[l, :, c0:c0 + cbw], ob[j][:], r=[bob[j]])

                for cb in range(6 * D // cbw):
                    colblock(cb)
                if E_outer is None:
                    k.barrier()

        def phase1(l, xcur):
            with ExitStack() as es:
                E = es.enter_context

                def sb(name, shape, dt=F32):
                    return E(sbt("p1_" + name, list(shape), dt))

                win = sb("win", [128, 8, WCOLS], BF16)
                wukv = sb("wukv", [128, 2, 1024], BF16)
                wuq = sb("wuq", [128, 3, 768], BF16)
                gv = sb("gv", [128, 960])
                gm = [sb("gm0", [128, D]), sb("gm1", [128, D])]
                sh = [sb("sh0", [128, D]), sb("sh1", [128, D])]
                bwin = [Buf() for _ in range(8)]
                bwin2 = [Buf() for _ in range(8)]
                bwk, bwq, bgv = Buf(), Buf(), Buf()
                bgm = [Buf(), Buf()]
                bsh = [Buf(), Buf()]
                wi = w_in[l].rearrange("(k p) n -> p k n", p=128)
                for kk in range(8):
                    k.dma("pool", win[:, kk, 0:1440], wi[:, kk, 0:1440], w=[bwin[kk]])
                k.dma("pool", wukv[:], w_ukv[l].rearrange("(k p) n -> p k n", p=128), w=[bwk])
                k.dma("pool", wuq[:], w_uq[l].rearrange("(k p) n -> p k n", p=128), w=[bwq])
                k.dma("sp", gv[:], gvec[l].partition_broadcast(128), w=[bgv])
                for j in range(2):
                    k.dma("sp", sh[j][:], modv[l, j, 0:D].partition_broadcast(128), w=[bsh[j]])
                    k.dma("sp", gm[j][:], modv[l, j, D:2 * D].partition_broadcast(128), w=[bgm[j]])
                bwab = Buf()
                with ExitStack() as es2:
                    E2 = es2.enter_context
                    wf = E2(sbt("p1_wf", [128, 8, 512], F32))
                    cc = E2(sbt("p1_cc", [128, 256], F32))
                    wft = [E2(sbt(f"p1_wft{i}", [128, 128], F32)) for i in range(2)]
                    bwf, bcc = Buf(), Buf()
                    bwft = [Buf(), Buf()]
                    k.dma("sp", wf[:], wi[:, :, 1440:1952], w=[bwf])
                    k.dma("sp", cc[:], ccsc, w=[bcc])
                    n = 0
                    for g in range(4):
                        for kk in range(8):
                            j = n % 2
                            n += 1
                            k.op("pe", lambda g=g, kk=kk, j=j: nc.tensor.transpose(
                                out=PS[j][:, 0:128], in_=wf[:, kk, g * 128:(g + 1) * 128], identity=ident32[:]),
                                 r=[bwf, idB], w=[PSB[j]], n=512)
                            k.op("act", lambda j=j: nc.scalar.copy(out=wft[j][:], in_=PS[j][:, 0:128]),
                                 r=[PSB[j]], w=[bwft[j]], n=128)
                            k.op("pe", lambda j=j: nc.tensor.matmul(PS[2 + j][:, 0:256], lhsT=wft[j][:], rhs=cc[:],
                                                                    start=True, stop=True),
                                 r=[bwft[j], bcc], w=[PSB[2 + j]], n=1024)
                            k.op("dve", lambda g=g, kk=kk, j=j: nc.vector.tensor_copy(
                                out=win[:, kk, 1440 + g * 128:1440 + (g + 1) * 128], in_=PS[2 + j][:, 0:128]),
                                 r=[PSB[2 + j]], w=[bwab], n=128)
                            k.op("dve", lambda g=g, kk=kk, j=j: nc.vector.tensor_copy(
                                out=win[:, kk, 1952 + g * 128:1952 + (g + 1) * 128], in_=PS[2 + j][:, 128:256]),
                                 r=[PSB[2 + j]], w=[bwab], n=128)
                    k.barrier()
                for kk in range(8):
                    k.dma("pool", win[:, kk, 2464:WCOLS], wi[:, kk, 1952:5024], w=[bwin2[kk]])
                WALL = bwin + bwin2 + [bwab]

                xt = [sb(f"xt{i}", [128, D]) for i in range(2)]
                rt = [sb(f"rt{i}", [128, 96]) for i in range(3)]
                hb = [sb(f"hb{i}", [128, D], BF16) for i in range(2)]
                hT = [sb(f"hT{i}", [128, D], BF16) for i in range(2)]
                proj = [sb(f"proj{i}", [128, 1440]) for i in range(3)]
                abb_ = sb("abb", [128, 1024], BF16)
                gts_ = sb("gts", [128, 3072], BF16)
                abb = [abb_, abb_]
                gts = [gts_, gts_]
                tmp = sb("tmp", [128, D])
                stx = sb("stx", [128, 2])
                rsx = sb("rsx", [128, 2])
                sq = sb("sq", [128, 1440], BF16)
                st_ = [sb(f"st{i}", [128, 16]) for i in range(2)]
                rs_ = [sb(f"rs{i}", [128, 16]) for i in range(2)]
                st2 = sb("st2", [128, 16])
                rs2 = sb("rs2", [128, 16])
                cn = sb("cn", [128, 640], BF16)
                cT5 = sb("cT5", [128, 640], BF16)
                kvsb_ = [sb(f"kvsb{i}", [128, 1024]) for i in range(2)]
                qm_ = [sb(f"qm{i}", [128, 768]) for i in range(2)]
                kcat = sb("kcat", [128, 768])
                tA = [sb(f"tA{i}", [128, 768]) for i in range(2)]
                t1 = [sb(f"t1_{i}", [128, 256]) for i in range(2)]
                t2 = [sb(f"t2_{i}", [128, 256]) for i in range(2)]
                kb = sb("kb", [128, 768], BF16)
                qb = sb("qb", [128, 768], BF16)
                vb = sb("vb", [128, 512], BF16)
                kgb = sb("kgb", [128, 128], BF16)
                vgb = sb("vgb", [128, 128], BF16)
                qgb = sb("qgb", [128, 512], BF16)
                ktb = sb("ktb", [128, 1024], BF16)
                qtb = sb("qtb", [128, 1024], BF16)
                kgt = sb("kgt", [128, 128], BF16)
                qgt = sb("qgt", [128, 512], BF16)
                names2 = ["xt", "rt", "hb", "hT", "proj", "abb", "gts", "tA", "t1", "t2"]
                B = {n_: [Buf(n_ + "0"), Buf(n_ + "1"), Buf(n_ + "2")] for n_ in names2 + ["st", "rs", "kvsb", "qm"]}
                B["abb"][1] = B["abb"][0]
                B["gts"][1] = B["gts"][0]
                for n_ in ["tmp", "stx", "rsx", "sq", "st2", "rs2", "cn", "cT5", "kcat",
                           "kb", "qb", "vb", "kgb", "vgb", "qgb", "ktb", "qtb", "kgt", "qgt"]:
                    B[n_] = Buf(n_)

                def rstd_from(stt, rst, lo, hi, n, bs, br):
                    k.op("act", lambda: nc.scalar.activation(out=rst[:, lo:hi], in_=stt[:, lo:hi], func=AF.Ln,
                                                             scale=1.0 / n, bias=EPS), r=[bs], w=[br], n=16)

                def norm_rope(src3, H, dh, rs_ap, g_ap, nrot, cos_ap, sin_ap, out3, rsrc, bout, ci, brt):
                    tA3 = tA[ci][:, 0:H * dh].rearrange("p (h d) -> p h d", h=H)
                    bA, b1, b2 = B["tA"][ci], B["t1"][ci], B["t2"][ci]
                    k.op("dve", lambda: nc.vector.tensor_tensor(out=tA3, in0=src3,
                                                                in1=rs_ap.unsqueeze(2).to_broadcast([128, H, dh]),
                                                                op=ALU.mult), r=rsrc, w=[bA], n=H * dh)
                    k.op("dve", lambda: nc.vector.tensor_tensor(out=tA3, in0=tA3,
                                                                in1=g_ap.unsqueeze(1).to_broadcast([128, H, dh]),
                                                                op=ALU.mult), r=[bA, bgv], w=[bA], n=H * dh)
                    r0 = dh - nrot
                    hf = nrot // 2
                    if r0 > 0:
                        k.op("act", lambda: nc.scalar.copy(out=out3[:, :, 0:r0], in_=tA3[:, :, 0:r0]),
                             r=[bA], w=[bout], n=H * r0)
                    x1_ = tA3[:, :, r0:r0 + hf]
                    x2_ = tA3[:, :, r0 + hf:dh]
                    cb_ = cos_ap.unsqueeze(1).to_broadcast([128, H, hf])
                    sb_ = sin_ap.unsqueeze(1).to_broadcast([128, H, hf])
                    t13 = t1[ci][:, 0:H * hf].rearrange("p (h d) -> p h d", h=H)
                    t23 = t2[ci][:, 0:H * hf].rearrange("p (h d) -> p h d", h=H)
                    k.op("dve", lambda: nc.vector.tensor_tensor(out=t13, in0=x1_, in1=cb_, op=ALU.mult),
                         r=[bA, brt], w=[b1], n=H * hf)
                    k.op("pool", lambda: nc.gpsimd.tensor_tensor(out=t23, in0=x2_, in1=sb_, op=ALU.mult),
                         r=[bA, brt], w=[b2], n=H * hf)
                    k.op("dve", lambda: nc.vector.tensor_tensor(out=out3[:, :, r0:r0 + hf], in0=t13, in1=t23,
                                                                op=ALU.subtract), r=[b1, b2], w=[bout], n=H * hf)
                    k.op("dve", lambda: nc.vector.tensor_tensor(out=t13, in0=x2_, in1=cb_, op=ALU.mult),
                         r=[bA, brt], w=[b1], n=H * hf)
                    k.op("pool", lambda: nc.gpsimd.tensor_tensor(out=t23, in0=x1_, in1=sb_, op=ALU.mult),
                         r=[bA, brt], w=[b2], n=H * hf)
                    k.op("dve", lambda: nc.vector.tensor_tensor(out=out3[:, :, r0 + hf:dh], in0=t13, in1=t23,
                                                                op=ALU.add), r=[b1, b2], w=[bout], n=H * hf)

                G_CKV, G_CQ, G_KNM, G_QNM, G_KNG, G_QNG = (gv[:, 0:256], gv[:, 256:640], gv[:, 640:736],
                                                           gv[:, 736:832], gv[:, 832:896], gv[:, 896:960])
                blocks = [(0, 512, "p"), (512, 512, "p"), (1024, 416, "p"), (1440, 512, "ab"), (1952, 512, "ab")]
                blocks += [(2464 + 512 * j, 512, "g") for j in range(6)]

                def stageA(i):
                    s_ = i % 2
                    tok = slice(i * 128, (i + 1) * 128)
                    ic = 1 if i < 2 else 0
                    k.dma("sp", xt[s_][:], xcur[tok, :], w=[B["xt"][s_]])
                    k.dma("sp", rt[i % 3][:], rope[tok, :], w=[B["rt"][i % 3]])
                    k.op("act", lambda: nc.scalar.activation(out=tmp[:], in_=xt[s_][:], func=AF.Square),
                         r=[B["xt"][s_]], w=[B["tmp"]], n=1024)
                    k.op("dve", lambda: nc.vector.tensor_reduce(out=stx[:, 0:1], in_=tmp[:], axis=AX.X, op=ALU.add),
                         r=[B["tmp"]], w=[B["stx"]], n=1024)
                    rstd_from(stx, rsx, 0, 1, D, B["stx"], B["rsx"])
                    k.op("act", lambda: nc.scalar.activation(out=rsx[:, 0:1], in_=rsx[:, 0:1], func=AF.Exp,
                                                             scale=-0.5), r=[B["rsx"]], w=[B["rsx"]], n=16)
                    k.op("dve", lambda: nc.vector.scalar_tensor_tensor(out=tmp[:], in0=xt[s_][:],
                                                                       scalar=rsx[:, 0:1], in1=gm[ic][:],
                                                                       op0=ALU.mult, op1=ALU.mult),
                         r=[B["xt"][s_], B["rsx"], bgm[ic]], w=[B["tmp"]], n=1024)
                    k.op("dve", lambda: nc.vector.tensor_tensor(out=hb[s_][:], in0=tmp[:], in1=sh[ic][:],
                                                                op=ALU.add),
                         r=[B["tmp"], bsh[ic]], w=[B["hb"][s_]], n=1024)
                    for kk in range(8):
                        k.op("pe", lambda kk=kk: nc.tensor.transpose(out=psbf(7)[:, kk * 128:(kk + 1) * 128],
                                                                     in_=hb[s_][:, kk * 128:(kk + 1) * 128],
                                                                     identity=identb[:]),
                             r=[B["hb"][s_], idB], w=[PSB[7]], n=128)
                    k.op("act", lambda: nc.scalar.copy(out=hT[s_][:], in_=psbf(7)[:, 0:1024]), r=[PSB[7]],
                         w=[B["hT"][s_]], n=1024)

                own = set(range(NT)) if l < depth - 1 else set(range(2, 19))
                blocks_kv = [(0, 512, "p"), (512, 32, "p"), (1440, 512, "ab"), (1952, 512, "ab")]

                def stageB(i):
                    s_ = i % 2
                    tok = slice(i * 128, (i + 1) * 128)
                    for bi, (c0, wd, kind) in enumerate(blocks if i in own else blocks_kv):
                        pb = bi % 4
                        for kk in range(8):
                            wb_ = bwin[kk] if kind == "p" else (bwab if kind == "ab" else bwin2[kk])
                            k.op("pe", lambda kk=kk, pb=pb, c0=c0, wd=wd: nc.tensor.matmul(
                                PS[pb][:, 0:wd], lhsT=hT[s_][:, kk * 128:(kk + 1) * 128], rhs=win[:, kk, c0:c0 + wd],
                                start=(kk == 0), stop=(kk == 7)), r=[B["hT"][s_], wb_], w=[PSB[pb]], n=wd)
                        if kind == "p":
                            k.op("dve", lambda pb=pb, c0=c0, wd=wd: nc.vector.tensor_copy(
                                out=proj[i % 3][:, c0:c0 + wd], in_=PS[pb][:, 0:wd]), r=[PSB[pb]],
                                 w=[B["proj"][i % 3]], n=wd)
                        elif kind == "ab":
                            k.op("act", lambda pb=pb, c0=c0: nc.scalar.copy(
                                out=abb[s_][:, c0 - 1440:c0 - 1440 + 512], in_=PS[pb][:, 0:512]),
                                 r=[PSB[pb]], w=[B["abb"][s_]], n=512)
                        else:
                            k.op("act", lambda pb=pb, c0=c0: nc.scalar.activation(
                                out=gts[s_][:, c0 - 2464:c0 - 2464 + 512], in_=PS[pb][:, 0:512], func=AF.Sigmoid),
                                 r=[PSB[pb]], w=[B["gts"][s_]], n=512)
                    k.dma("sp", ABd[tok, :], abb[s_][:], r=[B["abb"][s_]])
                    if i in own:
                        k.dma("sp", Gd[tok, :], gts[s_][:], r=[B["gts"][s_]])

                def cvars(i):
                    s_ = i % 2
                    return (slice(i * 128, (i + 1) * 128), proj[i % 3], B["proj"][i % 3], B["rt"][i % 3], rt[i % 3],
                            st_[s_], rs_[s_], kvsb_[s_], qm_[s_],
                            {"st": B["st"][s_], "rs": B["rs"][s_], "kvsb": B["kvsb"][s_], "qm": B["qm"][s_]})

                def stageC1(i):
                    tok, pj, bpj, brt, rtt, st, rs, kvsb, qm, BB = cvars(i)
                    B = dict(B_all)
                    B.update(BB)
                    full = i in own
                    wsq = 1440 if full else 544
                    k.op("dve", lambda: nc.vector.tensor_tensor(out=sq[:, 0:wsq], in0=pj[:, 0:wsq], in1=pj[:, 0:wsq],
                                                                op=ALU.mult),
                         r=[bpj], w=[B["sq"]], n=wsq)
                    k.op("dve", lambda: nc.vector.tensor_reduce(out=st[:, 0:1], in_=sq[:, 0:256], axis=AX.X,
                                                                op=ALU.add), r=[B["sq"]], w=[B["st"]], n=256)
                    if full:
                        k.op("dve", lambda: nc.vector.tensor_reduce(out=st[:, 1:2], in_=sq[:, 544:928], axis=AX.X,
                                                                    op=ALU.add), r=[B["sq"]], w=[B["st"]], n=384)
                    k.op("dve", lambda: nc.vector.tensor_reduce(
                        out=st[:, 2:4], in_=sq[:, 288:416].rearrange("p (h d) -> p h d", h=2), axis=AX.X,
                        op=ALU.add), r=[B["sq"]], w=[B["st"]], n=128)
                    if full:
                        k.op("dve", lambda: nc.vector.tensor_reduce(
                            out=st[:, 4:12], in_=sq[:, 928:1440].rearrange("p (h d) -> p h d", h=8), axis=AX.X,
                            op=ALU.add), r=[B["sq"]], w=[B["st"]], n=512)
                    rstd_from(st, rs, 0, 1, 256, B["st"], B["rs"])
                    if full:
                        rstd_from(st, rs, 1, 2, 384, B["st"], B["rs"])
                        rstd_from(st, rs, 2, 12, 64, B["st"], B["rs"])
                        k.op("act", lambda: nc.scalar.activation(out=rs[:, 0:12], in_=rs[:, 0:12], func=AF.Exp,
                                                                 scale=-0.5), r=[B["rs"]], w=[B["rs"]], n=16)
                    else:
                        rstd_from(st, rs, 2, 4, 64, B["st"], B["rs"])
                        k.op("act", lambda: nc.scalar.activation(out=rs[:, 0:1], in_=rs[:, 0:1], func=AF.Exp,
                                                                 scale=-0.5), r=[B["rs"]], w=[B["rs"]], n=16)
                        k.op("act", lambda: nc.scalar.activation(out=rs[:, 2:4], in_=rs[:, 2:4], func=AF.Exp,
                                                                 scale=-0.5), r=[B["rs"]], w=[B["rs"]], n=16)
                    k.op("dve", lambda: nc.vector.scalar_tensor_tensor(out=cn[:, 0:256], in0=pj[:, 0:256],
                                                                       scalar=rs[:, 0:1], in1=G_CKV, op0=ALU.mult,
                                                                       op1=ALU.mult),
                         r=[bpj, B["rs"], bgv], w=[B["cn"]], n=256)
                    if full:
                        k.op("dve", lambda: nc.vector.scalar_tensor_tensor(out=cn[:, 256:640], in0=pj[:, 544:928],
                                                                           scalar=rs[:, 1:2], in1=G_CQ,
                                                                           op0=ALU.mult, op1=ALU.mult),
                             r=[bpj, B["rs"], bgv], w=[B["cn"]], n=384)
                    ncn = 5 if full else 2
                    for kk in range(ncn):
                        k.op("pe", lambda kk=kk: nc.tensor.transpose(out=psbf(6)[:, kk * 128:(kk + 1) * 128],
                                                                     in_=cn[:, kk * 128:(kk + 1) * 128],
                                                                     identity=identb[:]),
                             r=[B["cn"], idB], w=[PSB[6]], n=128)
                    k.op("act", lambda: nc.scalar.copy(out=cT5[:, 0:ncn * 128], in_=psbf(6)[:, 0:ncn * 128]),
                         r=[PSB[6]], w=[B["cT5"]], n=ncn * 128)
                    for half in range(2):
                        for kk in range(2):
                            k.op("pe", lambda kk=kk, half=half: nc.tensor.matmul(
                                PS[4 + half][:, 0:512], lhsT=cT5[:, kk * 128:(kk + 1) * 128],
                                rhs=wukv[:, kk, half * 512:(half + 1) * 512], start=(kk == 0), stop=(kk == 1)),
                                 r=[B["cT5"], bwk], w=[PSB[4 + half]], n=512)
                        k.op("act", lambda half=half: nc.scalar.copy(out=kvsb[:, half * 512:(half + 1) * 512],
                                                                     in_=PS[4 + half][:, 0:512]),
                             r=[PSB[4 + half]], w=[B["kvsb"]], n=512)
                    for half, (c0, wd) in enumerate([(0, 512), (512, 256)] if full else []):
                        for kk in range(3):
                            k.op("pe", lambda kk=kk, half=half, c0=c0, wd=wd: nc.tensor.matmul(
                                PS[4 + half][:, 0:wd], lhsT=cT5[:, 256 + kk * 128:256 + (kk + 1) * 128],
                                rhs=wuq[:, kk, c0:c0 + wd], start=(kk == 0), stop=(kk == 2)),
                                 r=[B["cT5"], bwq], w=[PSB[4 + half]], n=wd)
                        k.op("act", lambda half=half, c0=c0, wd=wd: nc.scalar.copy(out=qm[:, c0:c0 + wd],
                                                                                  in_=PS[4 + half][:, 0:wd]),
                             r=[PSB[4 + half]], w=[B["qm"]], n=wd)

                def stageC2(i):
                    tok, pj, bpj, brt, rtt, st, rs, kvsb, qm, BB = cvars(i)
                    B = dict(B_all)
                    B.update(BB)
                    full = i in own
                    kv3 = kvsb[:].rearrange("p (h d) -> p h d", h=8)
                    kc3 = kcat[:].rearrange("p (h d) -> p h d", h=8)
                    k.op("act", lambda: nc.scalar.copy(out=kc3[:, :, 0:64], in_=kv3[:, :, 0:64]),
                         r=[B["kvsb"]], w=[B["kcat"]], n=512)
                    k.op("pool", lambda: nc.gpsimd.tensor_copy(
                        out=kc3[:, :, 64:96], in_=pj[:, 256:288].unsqueeze(1).to_broadcast([128, 8, 32])),
                         r=[bpj, B["kcat"]], w=[B["kcat"]], n=256)
                    k.op("act", lambda: nc.scalar.copy(out=vb[:].rearrange("p (h d) -> p h d", h=8),
                                                       in_=kv3[:, :, 64:128]), r=[B["kvsb"]], w=[B["vb"]], n=512)
                    k.dma("sp", Vm[tok, :, :].rearrange("t h d -> t (h d)"), vb[:], r=[B["vb"]])
                    for ci, (src, s0) in enumerate(((kcat, 0), (qm, 8)) if full else ((kcat, 0),)):
                        bsrc = B["kcat"] if s0 == 0 else B["qm"]
                        k.op("pool", lambda src=src, ci=ci: nc.gpsimd.tensor_tensor(
                            out=tA[ci][:], in0=src[:], in1=src[:], op=ALU.mult), r=[bsrc], w=[B["tA"][ci]], n=768)
                        k.op("dve", lambda s0=s0, ci=ci: nc.vector.tensor_reduce(
                            out=st2[:, s0:s0 + 8], in_=tA[ci][:].rearrange("p (h d) -> p h d", h=8), axis=AX.X,
                            op=ALU.add), r=[B["tA"][ci]], w=[B["st2"]], n=768)
                    nr2 = 16 if full else 8
                    rstd_from(st2, rs2, 0, nr2, 96, B["st2"], B["rs2"])
                    k.op("act", lambda: nc.scalar.activation(out=rs2[:, 0:nr2], in_=rs2[:, 0:nr2], func=AF.Exp,
                                                             scale=-0.5),
                         r=[B["rs2"]], w=[B["rs2"]], n=16)
                    kb3 = kb[:].rearrange("p (h d) -> p h d", h=8)
                    qb3 = qb[:].rearrange("p (h d) -> p h d", h=8)
                    norm_rope(kc3, 8, 96, rs2[:, 0:8], G_KNM, 32, rtt[:, 0:16], rtt[:, 16:32], kb3,
                              [B["kcat"], B["rs2"]], B["kb"], 0, brt)
                    if full:
                        norm_rope(qm[:].rearrange("p (h d) -> p h d", h=8), 8, 96, rs2[:, 8:16], G_QNM, 32,
                                  rtt[:, 0:16], rtt[:, 16:32], qb3, [B["qm"], B["rs2"]], B["qb"], 1, brt)
                    kq_list = ((kb3, ktb, B["kb"], B["ktb"], KTm), (qb3, qtb, B["qb"], B["qtb"], QTm))
                    for pi_, (srcb, dstt, bs, bd, dram) in enumerate(kq_list if full else kq_list[:1]):
                        pbk = 4 + pi_
                        for h in range(8):
                            k.op("pe", lambda h=h, srcb=srcb, pbk=pbk: nc.tensor.transpose(
                                out=psbf(pbk)[0:96, h * 128:(h + 1) * 128], in_=srcb[:, h, :], identity=identb[:]),
                                 r=[bs, idB], w=[PSB[pbk]], n=128)
                        k.op("act", lambda dstt=dstt, pbk=pbk: nc.scalar.copy(out=dstt[0:96, :],
                                                                            in_=psbf(pbk)[0:96, 0:1024]),
                             r=[PSB[pbk]], w=[bd], n=1024)
                        k.dma("sp", dram[:, :, tok].rearrange("h d t -> d h t"),
                              dstt[0:96, :].rearrange("p (h t) -> p h t", h=8), r=[bd])
                    norm_rope(pj[:, 288:416].rearrange("p (h d) -> p h d", h=2), 2, 64, rs[:, 2:4], G_KNG, 64,
                              rtt[:, 32:64], rtt[:, 64:96], kgb[:].rearrange("p (h d) -> p h d", h=2),
                              [bpj, B["rs"]], B["kgb"], 0, brt)
                    k.op("act", lambda: nc.scalar.copy(out=vgb[:], in_=pj[:, 416:544]), r=[bpj],
                         w=[B["vgb"]], n=128)
                    k.dma("sp", Vg[tok, :, :].rearrange("t h d -> t (h d)"), vgb[:], r=[B["vgb"]])
                    k.op("pe", lambda: nc.tensor.transpose(out=psbf(6)[:, 0:128], in_=kgb[:], identity=identb[:]),
                         r=[B["kgb"], idB], w=[PSB[6]], n=128)
                    k.op("act", lambda: nc.scalar.copy(out=kgt[:], in_=psbf(6)[:, 0:128]), r=[PSB[6]],
                         w=[B["kgt"]], n=128)
                    k.dma("sp", KTg[:, :, tok].rearrange("j d t -> (j d) t"), kgt[:], r=[B["kgt"]])
                    if not full:
                        return
                    norm_rope(pj[:, 928:1440].rearrange("p (h d) -> p h d", h=8), 8, 64, rs[:, 4:12], G_QNG, 64,
                              rtt[:, 32:64], rtt[:, 64:96], qgb[:].rearrange("p (h d) -> p h d", h=8),
                              [bpj, B["rs"]], B["qgb"], 1, brt)
                    for c in range(4):
                        k.op("pe", lambda c=c: nc.tensor.transpose(out=psbf(6)[:, c * 128:(c + 1) * 128],
                                                                   in_=qgb[:, c * 128:(c + 1) * 128],
                                                                   identity=identb[:]),
                             r=[B["qgb"], idB], w=[PSB[6]], n=128)
                    k.op("act", lambda: nc.scalar.copy(out=qgt[:], in_=psbf(6)[:, 0:512]), r=[PSB[6]],
                         w=[B["qgt"]], n=512)
                    k.dma("sp", QTg[:, :, tok].rearrange("(c two) d t -> (two d) c t", two=2),
                          qgt[:].rearrange("p (c t) -> p c t", c=4), r=[B["qgt"]])

                B_all = B
                stageA(0)
                stageB(0)
                for i in range(NT):
                    if i + 1 < NT:
                        stageA(i + 1)
                        stageB(i + 1)
                    stageC1(i)
                    if i >= 1:
                        stageC2(i - 1)
                stageC2(NT - 1)
                k.barrier()

        def phase2(l, prefetch=None):
            with ExitStack() as es:
                E = es.enter_context
                GK = 3
                kt_sb = [E(sbt(f"p2_kt{i}", [96, NTOK], BF16)) for i in range(2)]
                qt_sb = [E(sbt(f"p2_qt{i}", [96, NTOK], BF16)) for i in range(2)]
                va_sb = [E(sbt(f"p2_va{i}", [128, NT, 128], BF16)) for i in range(2)]
                pT = [E(sbt(f"p2_pT{i}", [128, GK * 512], BF16)) for i in range(2)]
                rc = E(sbt("p2_rc", [128, 512], F32))
                ot = [E(sbt(f"p2_ot{i}", [128, NTOK], BF16)) for i in range(2)]
                bkt, bqt, bva = [Buf(), Buf()], [Buf(), Buf()], [Buf(), Buf()]
                bpT = [Buf(), Buf()]
                brc = Buf()
                bot = [Buf(), Buf()]
                bSG = [Buf(), Buf()]
                for i in range(2):
                    k.op("pool", lambda i=i: nc.gpsimd.memset(va_sb[i][:, :, 64:128], 1.0), w=[bva[i]])
                if l < depth - 1:
                    qblocks = [(CTX + 512 * j, 512, NT) for j in range(8)]
                    qblocks.append((0, 256, 2))
                    ot_ranges = [(0, NTOK)]
                else:
                    qblocks = [(CTX + 448 * j, 448, NT) for j in range(4)] + [(CTX + 1792, 384, NT)]
                    ot_ranges = [(CTX, 2432)]
                scale_m = 96 ** -0.5
                scale_g = 64 ** -0.5
                state = {"acc": 0, "g": 0}

                def attend(KT, dq, QT, VA, scale, otile, prow, deps, bo):
                    for (q0_, qn_, nk_) in qblocks:
                        attend_block(KT, dq, QT, VA, scale, otile, prow, deps, bo, q0_, qn_, nk_)

                def attend_block(KT, dq, QT, VA, scale, otile, prow, deps, bo, q0, qn, nk):
                    if True:
                        ob = 6 + state["acc"] % 2
                        state["acc"] += 1
                        groups = [list(range(a, min(a + GK, nk))) for a in range(0, nk, GK)]
                        gslot = []

                        def s_grp(gi):
                            sl = state["g"] % 2
                            state["g"] += 1
                            gslot.append(sl)
                            for ii, kt in enumerate(groups[gi]):
                                c0 = sl * GK * 512 + ii * 512
                                k.op("pe", lambda kt=kt, c0=c0: nc.tensor.matmul(
                                    PSALL[:, c0:c0 + qn], lhsT=KT[0:dq, kt * 128:(kt + 1) * 128],
                                    rhs=QT[0:dq, q0:q0 + qn], start=True, stop=True), r=deps, w=[bSG[sl]], n=qn)

                        s_grp(0)
                        for gi, grp in enumerate(groups):
                            if gi + 1 < len(groups):
                                s_grp(gi + 1)
                            sl = gslot[gi]
                            ng = len(grp)
                            src = PSALL[:, sl * GK * 512:sl * GK * 512 + ng * 512].rearrange(
                                "p (g q) -> p g q", q=512)[:, :, 0:qn]
                            dst = pT[sl][:, 0:ng * 512].rearrange("p (g q) -> p g q", q=512)[:, :, 0:qn]
                            k.op("act", lambda src=src, dst=dst: nc.scalar.activation(
                                out=dst, in_=src, func=AF.Exp, scale=scale), r=[bSG[sl]], w=[bpT[sl]], n=ng * qn)
                            for ii, kt in enumerate(grp):
                                k.op("pe", lambda kt=kt, ii=ii, sl=sl: nc.tensor.matmul(
                                    PS[ob][:, 0:qn], lhsT=VA[:, kt, :], rhs=pT[sl][:, ii * 512:ii * 512 + qn],
                                    start=(kt == 0), stop=(kt == nk - 1)), r=deps + [bpT[sl]], w=[PSB[ob]], n=qn)
                        k.op("dve", lambda: nc.vector.reciprocal(out=rc[64:128, 0:qn], in_=PS[ob][64:128, 0:qn]),
                             r=[PSB[ob]], w=[brc])
                        k.op("dve", lambda: nc.vector.tensor_tensor(
                            out=otile[prow:prow + 64, q0:q0 + qn], in0=PS[ob][0:64, 0:qn], in1=rc[64:128, 0:qn],
                            op=ALU.mult), r=[PSB[ob], brc], w=[bo])

                nload = 0
                for h in range(8):
                    j = nload % 2
                    nload += 1
                    k.dma("sp", kt_sb[j][0:96, :], KTm[h], w=[bkt[j]])
                    k.dma("sp", qt_sb[j][0:96, :], QTm[h], w=[bqt[j]])
                    vmv = Vm[:, h, :].rearrange("(t p) d -> p t d", p=128)
                    for t0 in range(0, NT, 12):
                        t1_ = min(NT, t0 + 12)
                        k.dma("sp", va_sb[j][:, t0:t1_, 0:64], vmv[:, t0:t1_, :], w=[bva[j]])
                    c = h // 2
                    oj = c % 2
                    attend(kt_sb[j], 96, qt_sb[j], va_sb[j], scale_m, ot[oj], (h % 2) * 64,
                           [bkt[j], bqt[j], bva[j]], bot[oj])
                    if h % 2 == 1:
                        for (qa, qb_) in ot_ranges:
                            k.dma("sp", OTm[c, :, qa:qb_], ot[oj][:, qa:qb_], r=[bot[oj]])
                    if h == 1 and prefetch is not None:
                        prefetch(bot[0])
                for jkv in range(2):
                    j = nload % 2
                    nload += 1
                    k.dma("sp", kt_sb[j][0:64, :], KTg[jkv], w=[bkt[j]])
                    vgv = Vg[:, jkv, :].rearrange("(t p) d -> p t d", p=128)
                    for t0 in range(0, NT, 12):
                        t1_ = min(NT, t0 + 12)
                        k.dma("sp", va_sb[j][:, t0:t1_, 0:64], vgv[:, t0:t1_, :], w=[bva[j]])
                    for i in range(4):
                        h = jkv * 4 + i
                        jq = h % 2
                        k.dma("sp", qt_sb[jq][0:64, :], QTg[h], w=[bqt[jq]])
                        c = h // 2
                        oj = c % 2
                        attend(kt_sb[j], 64, qt_sb[jq], va_sb[j], scale_g, ot[oj], (h % 2) * 64,
                               [bkt[j], bqt[jq], bva[j]], bot[oj])
                        if h % 2 == 1:
                            for (qa, qb_) in ot_ranges:
                                k.dma("sp", OTg[c, :, qa:qb_], ot[oj][:, qa:qb_], r=[bot[oj]])
                k.barrier()

        def phase3(l, ab_pre=None):
            with ExitStack() as es:
                E = es.enter_context
                ab = E(sbt("p3_ab", [128, NT, 1024], BF16)) if ab_pre is None else ab_pre
                cb = [E(sbt(f"p3_c{i}", [128, 32, 256], BF16)) for i in range(2)]
                sbb = [E(sbt(f"p3_s{i}", [128, 32, 256], BF16)) for i in range(2)]
                of = [E(sbt(f"p3_o{i}", [128, 4, 256], BF16)) for i in range(2)]
                bab = [Buf() for _ in range(NT // 2)]
                bcb = [[Buf() for _ in range(4)] for _ in range(2)]
                bsb = [[Buf() for _ in range(4)] for _ in range(2)]
                bof = [Buf(), Buf()]
                abv = ABd.rearrange("(t p) n -> p t n", p=128)
                if ab_pre is None:
                    for t0 in range(0, NT, 2):
                        k.dma("sp", ab[:, t0:t0 + 2, :], abv[:, t0:t0 + 2, :], w=[bab[t0 // 2]])
                dc = dftc.rearrange("(t p) l -> p t l", p=128)
                ds = dfts.rearrange("(t p) l -> p t l", p=128)
                jbs = list(range(16)) if l < depth - 1 else list(range(9))
                jobs = [(dc, ds, 2, 32, jb * 256, CTX + jb * 256) for jb in jbs]
                if l < depth - 1:
                    jobs.append((dftc_c.rearrange("(t p) l -> p t l", p=128),
                                 dfts_c.rearrange("(t p) l -> p t l", p=128), 0, 2, 0, 0))
                nacc_ = [0]

                def job(n, mc, ms, t_off, ntl, c0, tok0):
                    nacc = nacc_[0]
                    j = n % 2
                    for ta_ in range(0, ntl, 8):
                        tb_ = min(ntl, ta_ + 8)
                        k.dma("sp", cb[j][:, ta_:tb_, :], mc[:, ta_:tb_, c0:c0 + 256], w=[bcb[j][ta_ // 8]])
                        k.dma("sp", sbb[j][:, ta_:tb_, :], ms[:, ta_:tb_, c0:c0 + 256], w=[bsb[j][ta_ // 8]])
                    for g in range(4):
                        pb = nacc % 2
                        nacc += 1
                        for t in range(ntl):
                            k.op("pe", lambda g=g, t=t, pb=pb: nc.tensor.matmul(
                                PS[pb][:, 0:256], lhsT=ab[:, t_off + t, g * 128:(g + 1) * 128], rhs=cb[j][:, t, :],
                                start=(t == 0), stop=False), r=[bab[(t_off + t) // 2], bcb[j][t // 8]],
                                 w=[PSB[pb]], n=256)
                            k.op("pe", lambda g=g, t=t, pb=pb: nc.tensor.matmul(
                                PS[pb][:, 0:256], lhsT=ab[:, t_off + t, 512 + g * 128:512 + (g + 1) * 128],
                                rhs=sbb[j][:, t, :], start=False, stop=(t == ntl - 1)),
                                 r=[bab[(t_off + t) // 2], bsb[j][t // 8]], w=[PSB[pb]], n=256)
                        k.op("act", lambda g=g, pb=pb: nc.scalar.copy(out=of[j][:, g, :], in_=PS[pb][:, 0:256]),
                             r=[PSB[pb]], w=[bof[j]], n=256)
                    k.dma("sp", OTf[:, :, tok0:tok0 + 256].rearrange("g p t -> p g t"), of[j][:], r=[bof[j]])
                    nacc_[0] = nacc

                for n, jb_ in enumerate(jobs):
                    job(n, *jb_)
                k.barrier()

        def phase4(l, xcur, w_pre=None):
            with ExitStack() as es:
                E = es.enter_context

                def sb(name, shape, dt=F32):
                    return E(sbt("p4_" + name, list(shape), dt))

                if w_pre is None:
                    wbr = [sb(f"wbr{i}", [128, 4, D], BF16) for i in range(3)]
                    wo = sb("wo", [128, 8, D], BF16)
                else:
                    wbr, wo = w_pre
                gt1 = [sb("gt1_0", [128, D]), sb("gt1_1", [128, D])]
                gm2 = [sb("gm2_0", [128, D]), sb("gm2_1", [128, D])]
                sh2 = [sb("sh2_0", [128, D]), sb("sh2_1", [128, D])]
                bwbr = [Buf() for _ in range(3)]
                bwo, = [Buf()]
                bgt1, bgm2, bsh2 = [Buf(), Buf()], [Buf(), Buf()], [Buf(), Buf()]
                if w_pre is None:
                    for i in range(3):
                        k.dma("pool", wbr[i][:], w_br[i][l].rearrange("(k p) n -> p k n", p=128), w=[bwbr[i]])
                    k.dma("pool", wo[:], w_o[l].rearrange("(k p) n -> p k n", p=128), w=[bwo])
                for j in range(2):
                    k.dma("sp", gt1[j][:], modv[l, j, 2 * D:3 * D].partition_broadcast(128), w=[bgt1[j]])
                    k.dma("sp", sh2[j][:], modv[l, j, 3 * D:4 * D].partition_broadcast(128), w=[bsh2[j]])
                    k.dma("sp", gm2[j][:], modv[l, j, 4 * D:5 * D].partition_broadcast(128), w=[bgm2[j]])
                names = ["ot0", "ot1", "ot2", "gts", "xt", "ya", "yt", "yb", "yT", "xm", "tmp", "st", "rs", "hb", "hT"]
                shapes = {"ot0": ([128, 4, 128], BF16), "ot1": ([128, 4, 128], BF16), "ot2": ([128, 4, 128], BF16),
                          "gts": ([128, 3072], BF16), "xt": ([128, D], F32), "ya": ([128, D], F32),
                          "yt": ([128, 512], F32), "yb": ([128, D], BF16), "yT": ([128, D], BF16),
                          "xm": ([128, D], F32), "tmp": ([128, D], F32), "st": ([128, 2], F32),
                          "rs": ([128, 2], F32), "hb": ([128, D], BF16), "hT": ([128, D], BF16)}
                NSET = 3
                TS = [{n_: sb(f"{n_}_{s_}", *shapes[n_]) for n_ in names} for s_ in range(NSET)]
                BS = [{n_: Buf(n_) for n_ in names} for s_ in range(NSET)]
                OTs = [OTm, OTg, OTf]
                first = 0 if l < depth - 1 else 2
                rr = [0]

                def nb():
                    b_ = rr[0]
                    rr[0] = (b_ + 1) % 8
                    return b_

                def tile(i, n_):
                    T = TS[n_ % NSET]
                    B = BS[n_ % NSET]
                    tok = slice(i * 128, (i + 1) * 128)
                    ic = 1 if i < 2 else 0
                    otin = [T["ot0"], T["ot1"], T["ot2"]]
                    gts, xt, ya, yt, yb, yT, xm, tmp, st, rs, hb, hT = (T[n_] for n_ in names[3:])
                    for bi in range(3):
                        k.dma("sp", otin[bi][:], OTs[bi][:, :, tok].rearrange("c p t -> p c t"), w=[B[f"ot{bi}"]])
                    k.dma("sp", gts[:], Gd[tok, :], w=[B["gts"]])
                    k.dma("sp", xt[:], xcur[tok, :], w=[B["xt"]])
                    for half in range(2):
                        cs = slice(half * 512, (half + 1) * 512)
                        pbs = [nb(), nb(), nb()]
                        for bi in range(3):
                            for c in range(4):
                                k.op("pe", lambda bi=bi, c=c, cs=cs, pbs=pbs: nc.tensor.matmul(
                                    PS[pbs[bi]][:, 0:512], lhsT=otin[bi][:, c, :], rhs=wbr[bi][:, c, cs],
                                    start=(c == 0), stop=(c == 3)), r=[B[f"ot{bi}"], bwbr[bi]], w=[PSB[pbs[bi]]])
                        k.op("dve", lambda cs=cs, half=half, pbs=pbs: nc.vector.tensor_tensor(
                            out=ya[:, cs], in0=PS[pbs[0]][:, 0:512], in1=gts[:, half * 512:(half + 1) * 512],
                            op=ALU.mult), r=[PSB[pbs[0]], B["gts"]], w=[B["ya"]])
                        for bi in (1, 2):
                            k.op("dve", lambda bi=bi, half=half, pbs=pbs: nc.vector.tensor_tensor(
                                out=yt[:], in0=PS[pbs[bi]][:, 0:512],
                                in1=gts[:, bi * D + half * 512:bi * D + (half + 1) * 512], op=ALU.mult),
                                 r=[PSB[pbs[bi]], B["gts"]], w=[B["yt"]])
                            k.op("dve", lambda cs=cs: nc.vector.tensor_tensor(out=ya[:, cs], in0=ya[:, cs],
                                                                              in1=yt[:], op=ALU.add),
                                 r=[B["ya"], B["yt"]], w=[B["ya"]])
                    pt = nb()
                    k.op("act", lambda: nc.scalar.copy(out=yb[:], in_=ya[:]), r=[B["ya"]], w=[B["yb"]], n=1024)
                    for kk in range(8):
                        k.op("pe", lambda kk=kk: nc.tensor.transpose(out=psbf(pt)[:, kk * 128:(kk + 1) * 128],
                                                                     in_=yb[:, kk * 128:(kk + 1) * 128],
                                                                     identity=identb[:]),
                             r=[B["yb"], idB], w=[PSB[pt]], n=128)
                    k.op("act", lambda: nc.scalar.copy(out=yT[:], in_=psbf(pt)[:, 0:1024]), r=[PSB[pt]],
                         w=[B["yT"]], n=1024)
                    for half in range(2):
                        cs = slice(half * 512, (half + 1) * 512)
                        pw = nb()
                        for kk in range(8):
                            k.op("pe", lambda kk=kk, cs=cs, pw=pw: nc.tensor.matmul(
                                PS[pw][:, 0:512], lhsT=yT[:, kk * 128:(kk + 1) * 128], rhs=wo[:, kk, cs],
                                start=(kk == 0), stop=(kk == 7)), r=[B["yT"], bwo], w=[PSB[pw]])
                        k.op("dve", lambda cs=cs, pw=pw: nc.vector.tensor_tensor(
                            out=tmp[:, cs], in0=PS[pw][:, 0:512], in1=gt1[ic][:, cs], op=ALU.mult),
                             r=[PSB[pw], bgt1[ic]], w=[B["tmp"]])
                    k.op("dve", lambda: nc.vector.tensor_tensor(out=xm[:], in0=tmp[:], in1=xt[:], op=ALU.add),
                         r=[B["tmp"], B["xt"]], w=[B["xm"]], n=1024)
                    k.dma("sp", xmid[tok, :], xm[:], r=[B["xm"]])
                    k.op("act", lambda: nc.scalar.activation(out=tmp[:], in_=xm[:], func=AF.Square),
                         r=[B["xm"]], w=[B["tmp"]], n=1024)
                    k.op("dve", lambda: nc.vector.tensor_reduce(out=st[:, 0:1], in_=tmp[:], axis=AX.X, op=ALU.add),
                         r=[B["tmp"]], w=[B["st"]], n=1024)
                    k.op("act", lambda: nc.scalar.activation(out=rs[:, 0:1], in_=st[:, 0:1], func=AF.Ln,
                                                             scale=1.0 / D, bias=EPS), r=[B["st"]], w=[B["rs"]],
                         n=16)
                    k.op("act", lambda: nc.scalar.activation(out=rs[:, 0:1], in_=rs[:, 0:1], func=AF.Exp,
                                                             scale=-0.5), r=[B["rs"]], w=[B["rs"]], n=16)
                    k.op("dve", lambda: nc.vector.scalar_tensor_tensor(out=tmp[:], in0=xm[:],
                                                                       scalar=rs[:, 0:1], in1=gm2[ic][:],
                                                                       op0=ALU.mult, op1=ALU.mult),
                         r=[B["xm"], B["rs"], bgm2[ic]], w=[B["tmp"]], n=1024)
                    k.op("pool", lambda: nc.gpsimd.tensor_tensor(out=hb[:], in0=tmp[:], in1=sh2[ic][:],
                                                                 op=ALU.add),
                         r=[B["tmp"], bsh2[ic]], w=[B["hb"]], n=1024)
                    pt2 = nb()
                    for kk in range(8):
                        k.op("pe", lambda kk=kk: nc.tensor.transpose(out=psbf(pt2)[:, kk * 128:(kk + 1) * 128],
                                                                     in_=hb[:, kk * 128:(kk + 1) * 128],
                                                                     identity=identb[:]),
                             r=[B["hb"], idB], w=[PSB[pt2]], n=128)
                    k.op("act", lambda: nc.scalar.copy(out=hT[:], in_=psbf(pt2)[:, 0:1024]), r=[PSB[pt2]],
                         w=[B["hT"]], n=1024)
                    k.dma("sp", H2T[:, :, tok].rearrange("k p t -> p k t"),
                          hT[:].rearrange("p (k t) -> p k t", k=8), r=[B["hT"]])

                tiles = list(range(first, NT)) if l < depth - 1 else list(range(2, 19))
                for n_, i in enumerate(tiles):
                    tile(i, n_)
                k.barrier()

        def phase5(l, xdst, last):
            with ExitStack() as es:
                E = es.enter_context

                def sb(name, shape, dt=F32):
                    return E(sbt("p5_" + name, list(shape), dt))

                wup = sb("wup", [128, 8, 2 * DFF], BF16)
                wdn = sb("wdn", [128, 22, D], BF16)
                cw = sb("cw", [128, 44, 3])
                cbi = sb("cb", [128, 44])
                gt2 = [sb("gt2_0", [128, D]), sb("gt2_1", [128, D])]
                cgrp = [(0, 6), (6, 12), (12, 17), (17, 22)]
                bwup = [[Buf() for _ in cgrp] for _ in range(2)]
                cg_of = {}
                for gi_, (ca, cb2) in enumerate(cgrp):
                    for c_ in range(ca, cb2):
                        cg_of[c_] = gi_
                bwdn = [Buf() for _ in range(11)]
                bcw, bcb_ = Buf(), Buf()
                bgt2 = [Buf(), Buf()]
                wu = w_up[l].rearrange("(k p) n -> p k n", p=128)
                for gi_, (ca, cb2) in enumerate(cgrp):
                    for part_ in range(2):
                        c0_ = (ca + 22 * part_) * 128
                        c1_ = (cb2 + 22 * part_) * 128
                        k.dma("pool", wup[:, :, c0_:c1_], wu[:, :, c0_:c1_], w=[bwup[part_][gi_]])
                wd_ = w_down[l].rearrange("(k p) n -> p k n", p=128)
                for c in range(0, 22, 2):
                    k.dma("pool", wdn[:, c:c + 2, :], wd_[:, c:c + 2, :], w=[bwdn[c // 2]])
                k.dma("sp", cw[:], convw[l], w=[bcw])
                k.dma("sp", cbi[:], convb[l], w=[bcb_])
                for j in range(2):
                    k.dma("sp", gt2[j][:], modv[l, j, 5 * D:6 * D].partition_broadcast(128), w=[bgt2[j]])
                FB = 256
                NU = 4
                h2 = [sb(f"h2_{i}", [128, 8, FB + 2], BF16) for i in range(2)]
                us = [sb(f"us{i}", [128, FB + 2]) for i in range(NU)]
                ta = [sb(f"ta{i}", [128, FB]) for i in range(2)]
                tv = [sb(f"tv{i}", [128, FB]) for i in range(2)]
                sa = [sb(f"sa{i}", [128, FB]) for i in range(2)]
                gT = [sb(f"gT{i}", [128, 22, FB], BF16) for i in range(2)]
                xm = [sb(f"xm{i}", [128, D]) for i in range(2)]
                xo = [sb(f"xo{i}", [128, D]) for i in range(2)]
                tmp = [sb(f"tmp{i}", [128, D]) for i in range(2)]
                bh2 = [Buf(), Buf()]
                bus = [Buf() for _ in range(NU)]
                bta, btv, bsa = [Buf(), Buf()], [Buf(), Buf()], [Buf(), Buf()]
                bgT = [[Buf() for _ in range(22)] for _ in range(2)]
                btmp = [Buf(), Buf()]
                bxm, bxo = [Buf(), Buf()], [Buf(), Buf()]
                cm = sb("cm", [128, 2])
                bcm = Buf()
                k.dma("sp", cm[:], cmask, w=[bcm])
                nblk = SEQ // FB
                blocks = []
                for j in range(nblk if not last else nblk // 2):
                    left, right = "c", "c"
                    if j == 0:
                        left = "z"
                    if j == nblk - 1:
                        right = "z"
                    blocks.append((CTX + FB * j, FB, left, right))
                if not last:
                    blocks.append((0, 256, "z", "z"))
                h2v = H2T.rearrange("k p t -> p k t")
                cnt = {"u": 0, "x": 0}

                def unit(j, tn, c, part, gs):
                    ch = c + 22 * part
                    ui = cnt["u"]
                    cnt["u"] += 1
                    pb = ui % 4
                    u = us[ui % NU]
                    bu = bus[ui % NU]
                    cs_ = c % 2
                    tdst, btd = ((ta[cs_], bta[cs_]), (tv[cs_], btv[cs_]))[part]
                    for kk in range(8):
                        k.op("pe", lambda kk=kk: nc.tensor.matmul(
                            PS[pb][:, 0:tn + 2], lhsT=wup[:, kk, ch * 128:(ch + 1) * 128],
                            rhs=h2[j][:, kk, 0:tn + 2], start=(kk == 0), stop=(kk == 7)),
                             r=[bwup[part][cg_of[c]], bh2[j]], w=[PSB[pb]], n=tn + 2)
                    k.op("act", lambda: nc.scalar.activation(
                        out=tdst[:, 0:tn], in_=PS[pb][:, 1:tn + 1], func=AF.Identity, scale=cw[:, ch, 1:2],
                        bias=cbi[:, ch:ch + 1]), r=[PSB[pb], bcw, bcb_], w=[btd], n=tn)
                    k.op("dve", lambda: nc.vector.scalar_tensor_tensor(
                        out=tdst[:, 0:tn], in0=PS[pb][:, 0:tn], scalar=cw[:, ch, 0:1], in1=tdst[:, 0:tn],
                        op0=ALU.mult, op1=ALU.add), r=[PSB[pb], bcw, btd], w=[btd], n=tn)
                    k.op("dve", lambda: nc.vector.scalar_tensor_tensor(
                        out=tdst[:, 0:tn], in0=PS[pb][:, 2:tn + 2], scalar=cw[:, ch, 2:3], in1=tdst[:, 0:tn],
                        op0=ALU.mult, op1=ALU.add), r=[PSB[pb], bcw, btd], w=[btd], n=tn)
                    if part == 1:
                        k.op("act", lambda: nc.scalar.activation(out=sa[cs_][:, 0:tn], in_=ta[cs_][:, 0:tn],
                                                                 func=AF.Silu), r=[bta[cs_]], w=[bsa[cs_]], n=tn)
                        k.op("pool", lambda: nc.gpsimd.tensor_tensor(out=gT[gs][:, c, 0:tn], in0=sa[cs_][:, 0:tn],
                                                                     in1=tv[cs_][:, 0:tn], op=ALU.mult),
                             r=[bsa[cs_], btv[cs_]], w=[bgT[gs][c]], n=tn)

                def down(gs, tn, t0, tt, ic):
                    jx = cnt["x"] % 2
                    cnt["x"] += 1
                    tok = slice(t0 + tt * 128, t0 + (tt + 1) * 128)
                    k.dma("sp", xm[jx][:], xmid[tok, :], w=[bxm[jx]])
                    for half in range(2):
                        cs = slice(half * 512, (half + 1) * 512)
                        for c in range(22):
                            k.op("pe", lambda c=c, cs=cs, half=half: nc.tensor.matmul(
                                PS[4 + half][:, 0:512], lhsT=gT[gs][:, c, tt * 128:(tt + 1) * 128],
                                rhs=wdn[:, c, cs], start=(c == 0), stop=(c == 21)),
                                 r=[bgT[gs][c], bwdn[c // 2]], w=[PSB[4 + half]])
                        k.op("dve", lambda half=half, cs=cs: nc.vector.tensor_tensor(
                            out=tmp[jx][:, cs], in0=PS[4 + half][:, 0:512], in1=gt2[ic][:, cs], op=ALU.mult),
                             r=[PSB[4 + half], bgt2[ic]], w=[btmp[jx]])
                    k.op("pool", lambda: nc.gpsimd.tensor_tensor(out=xo[jx][:], in0=tmp[jx][:], in1=xm[jx][:],
                                                                 op=ALU.add),
                         r=[btmp[jx], bxm[jx]], w=[bxo[jx]], n=1024)
                    if last:
                        dst = xdst[t0 - CTX + tt * 128:t0 - CTX + (tt + 1) * 128, :]
                    else:
                        dst = xdst[tok, :]
                    k.dma("sp", dst, xo[jx][:], r=[bxo[jx]])

                def block(n, t0, tn, left, right):
                    j = n % 2
                    gs = n % 2
                    ic = 1 if t0 < CTX else 0
                    lo = t0 - 1 if left == "c" else t0
                    hi = t0 + tn + 1 if right == "c" else t0 + tn
                    k.dma("sp", h2[j][:, :, lo - (t0 - 1):hi - (t0 - 1)], h2v[:, :, lo:hi], w=[bh2[j]])
                    for spec, col in ((left, 0), (right, tn + 1)):
                        if spec == "c":
                            continue
                        if spec == "z":
                            k.op("pool", lambda col=col: nc.gpsimd.memset(h2[j][:, :, col:col + 1], 0.0),
                                 w=[bh2[j]], n=8)
                            continue
                        src_tok, mi = spec
                        k.dma("sp", h2[j][:, :, col:col + 1], h2v[:, :, src_tok:src_tok + 1], w=[bh2[j]],
                              slow=True)
                        k.op("dve", lambda col=col, mi=mi: nc.vector.tensor_scalar(
                            out=h2[j][:, :, col:col + 1], in0=h2[j][:, :, col:col + 1], scalar1=cm[:, mi:mi + 1],
                            scalar2=None, op0=ALU.mult), r=[bh2[j], bcm], w=[bh2[j]], n=8)
                    for c in range(22):
                        for part in range(2):
                            unit(j, tn, c, part, gs)
                    for tt in range(tn // 128):
                        down(gs, tn, t0, tt, ic)

                for n, bl_ in enumerate(blocks):
                    block(n, *bl_)
                k.barrier()

        xcur = xin
        for l in range(depth):
            last = (l == depth - 1)
            phase0(l)
            if check_stop(f"p0_{l}"):
                break
            phase1(l, xcur)
            if check_stop(f"p1_{l}"):
                break
            with ExitStack() as esw:
                wbr_p = [esw.enter_context(sbt(f"pre_wbr{i}", [128, 4, D], BF16)) for i in range(3)]
                wo_p = esw.enter_context(sbt("pre_wo", [128, 8, D], BF16))
                with ExitStack() as esa:
                    ab_p = esa.enter_context(sbt("pre_ab", [128, NT, 1024], BF16))

                    def prefetch(bdep, l=l):
                        abv_ = ABd.rearrange("(t p) n -> p t n", p=128)
                        for t0 in range(0, NT, 2):
                            k.dma("sp", ab_p[:, t0:t0 + 2, :], abv_[:, t0:t0 + 2, :], r=[bdep], w=[Buf()])
                        for i in range(3):
                            k.dma("pool", wbr_p[i][:], w_br[i][l].rearrange("(k p) n -> p k n", p=128),
                                  r=[bdep], w=[Buf()])
                        k.dma("pool", wo_p[:], w_o[l].rearrange("(k p) n -> p k n", p=128), r=[bdep], w=[Buf()])

                    phase2(l, prefetch)
                    if check_stop(f"p2_{l}"):
                        break
                    phase3(l, ab_p)
                    if check_stop(f"p3_{l}"):
                        break
                phase4(l, xcur, (wbr_p, wo_p))
                if check_stop(f"p4_{l}"):
                    break
            phase5(l, y_out if last else x1, last)
            if check_stop(f"p5_{l}"):
                break
            xcur = x1
        k.barrier()
    return nc, k


def _consts():
    GRID_W = 64
    rows = SEQ // GRID_W
    row = np.repeat(np.arange(rows, dtype=np.float32), GRID_W)
    col = np.tile(np.arange(GRID_W, dtype=np.float32), rows)

    def tab(rot):
        n_f = rot // 4
        inv = (np.float32(10000.0) ** (-np.arange(n_f, dtype=np.float32) / np.float32(n_f))).astype(np.float32)
        ang = np.concatenate([row[:, None] * inv, col[:, None] * inv], axis=-1).astype(np.float32)
        return np.cos(ang).astype(np.float32), np.sin(ang).astype(np.float32)

    cm, sm = tab(32)
    cg, sg = tab(64)
    rope = np.zeros((NTOK, 96), np.float32)
    rope[:CTX, 0:16] = 1.0
    rope[:CTX, 32:64] = 1.0
    rope[CTX:, 0:16] = cm
    rope[CTX:, 16:32] = sm
    rope[CTX:, 32:64] = cg
    rope[CTX:, 64:96] = sg

    def dft(n, scale):
        idx = (np.outer(np.arange(n, dtype=np.int64), np.arange(n, dtype=np.int64)) % n).astype(np.float64)
        ang = 2.0 * np.pi * idx / n
        return np.cos(ang) * scale, np.sin(ang) * scale

    c128, s128 = dft(128, 128 ** -0.5)
    ccsc = np.concatenate([c128, s128], axis=1).astype(np.float32)
    cL, sL = dft(SEQ, 1.0 / 64)
    cC, sC = dft(CTX, 1.0 / 16)
    bf = ml_dtypes.bfloat16
    return dict(rope=rope, ccsc=ccsc, dftc=cL.astype(np.float32).astype(bf), dfts=(-sL).astype(np.float32).astype(bf),
                dftc_c=cC.astype(np.float32).astype(bf), dfts_c=(-sC).astype(np.float32).astype(bf))


_CONSTS = None


def make_in_maps(inputs, depth=DEPTH, n_cores=8, parity=0):
    global _CONSTS
    if _CONSTS is None:
        _CONSTS = _consts()
    f = lambda a: np.ascontiguousarray(np.asarray(a, dtype=np.float32))
    x, c, ctx, c_ctx = f(inputs["x"]), f(inputs["c"]), f(inputs["ctx"]), f(inputs["c_ctx"])
    shared = {
        "w_mod": f(inputs["w_mod"])[:depth], "b_mod": f(inputs["b_mod"])[:depth],
        "g_norm1": f(inputs["g_norm1"])[:depth], "g_norm2": f(inputs["g_norm2"])[:depth],
        "w_in": f(inputs["w_in"])[:depth],
        "gvec": np.ascontiguousarray(np.concatenate(
            [f(inputs[n]) for n in ("g_ckv", "g_cq", "g_kn_mla", "g_qn_mla", "g_kn_gqa", "g_qn_gqa")],
            axis=1)[:depth]),
        "w_uq": f(inputs["w_uq"])[:depth], "w_ukv": f(inputs["w_ukv"])[:depth],
        "w_br_mla": f(inputs["w_br_mla"])[:depth], "w_br_gqa": f(inputs["w_br_gqa"])[:depth],
        "w_four": f(inputs["w_four"])[:depth], "w_o": f(inputs["w_o"])[:depth],
        "w_up": f(inputs["w_up"])[:depth],
        "convw": np.ascontiguousarray(
            f(inputs["conv_w"])[:depth].reshape(depth, 3, 44, 128).transpose(0, 3, 2, 1)),
        "convb": np.ascontiguousarray(f(inputs["conv_b"])[:depth].reshape(depth, 44, 128).transpose(0, 2, 1)),
        "w_down": f(inputs["w_down"])[:depth],
    }
    shared.update(_CONSTS)
    in_maps = []
    rev = np.arange(SEQ - 1, -1, -1)
    revc = np.arange(CTX - 1, -1, -1)
    odd_consts = None
    for core in range(n_cores):
        par = (core % 2) if n_cores == 8 else parity
        b = (core // 2 if n_cores == 8 else core) % x.shape[0]
        m = dict(shared)
        cT = np.stack([c[b].reshape(8, 128).T, c_ctx.reshape(8, 128).T], axis=-1)
        m["cT"] = np.ascontiguousarray(cT.astype(np.float32))
        m["cmask"] = np.zeros((128, 2), np.float32)
        if par == 0:
            m["xin"] = np.ascontiguousarray(np.concatenate([ctx[b], x[b]], axis=0))
        else:
            if odd_consts is None:
                rp = _CONSTS["rope"]
                odd_consts = {
                    "rope": np.ascontiguousarray(np.concatenate([rp[:CTX][revc], rp[CTX:][rev]], axis=0)),
                    "dftc": np.ascontiguousarray(_CONSTS["dftc"][rev][:, rev]),
                    "dfts": np.ascontiguousarray(_CONSTS["dfts"][rev][:, rev]),
                    "dftc_c": np.ascontiguousarray(_CONSTS["dftc_c"][revc][:, revc]),
                    "dfts_c": np.ascontiguousarray(_CONSTS["dfts_c"][revc][:, revc]),
                    "convw": np.ascontiguousarray(shared["convw"][..., ::-1]),
                }
            m.update(odd_consts)
            m["xin"] = np.ascontiguousarray(np.concatenate([ctx[b][revc], x[b][rev]], axis=0))
        in_maps.append(m)
    return in_maps


def kernel(**inputs):
    nc, _ = build()
    in_maps = make_in_maps(inputs)
    res = run_bass_kernel_spmd(nc, in_maps, core_ids=list(range(8)))
    nb = np.asarray(inputs["x"]).shape[0]
    out = np.stack([np.concatenate([np.asarray(res.results[2 * b]["y"], dtype=np.float32),
                                    np.asarray(res.results[2 * b + 1]["y"], dtype=np.float32)[::-1]], axis=0)
                    for b in range(nb)], axis=0)
    return out
```
